# Optimizing a Trainium2 kernel written in Bass

```python
import jax, jax.numpy as jnp
from jax import lax
import numpy as np

D_MODEL = 1024
BATCH = 8
SEQ = 4096
DEPTH = 4

N_EVEN = (DEPTH + 1) // 2
N_ODD = DEPTH // 2

NSA_HEADS = 8
HEAD_DIM = 64
KV_GROUPS = 2
HEADS_PER_GROUP = NSA_HEADS // KV_GROUPS
N_BRANCH = 3
CMP_BLOCK = 32
CMP_STRIDE = 16
CMP_HIDDEN = 128
SEL_BLOCK = 64
SEL_TOPK = 8
WINDOW = 512
Q_BLOCK = 128
FORCE_BONUS = 1e4
NEG = -1e30

POOL_WIDTH = D_MODEL // 2
POOL_WINDOWS = (2, 4, 8, 16)
POOL_GROUPS = len(POOL_WINDOWS)
POOL_GROUP_DIM = POOL_WIDTH // POOL_GROUPS

NSA_WIDTH = NSA_HEADS * HEAD_DIM
KV_WIDTH = KV_GROUPS * HEAD_DIM
GATE_WIDTH = NSA_HEADS * N_BRANCH
IN_WIDTH = NSA_WIDTH + N_BRANCH * 2 * KV_WIDTH + GATE_WIDTH + POOL_WIDTH
MIX_WIDTH = NSA_WIDTH + POOL_WIDTH

CONV_WIDTH = 31

FFN_HIDDEN = -(-8 * D_MODEL // (3 * 256)) * 256

kernel_name = "hybrid_nsa_pool_conformer_swiglu"


def rmsnorm(x, g, eps=1e-6):
    xf = x.astype(jnp.float32)
    y = xf * lax.rsqrt(jnp.mean(xf * xf, axis=-1, keepdims=True) + eps)
    return (y * g.astype(jnp.float32)).astype(x.dtype)


def layernorm(x, g, b, eps=1e-5):
    xf = x.astype(jnp.float32)
    mu = jnp.mean(xf, axis=-1, keepdims=True)
    xc = xf - mu
    var = jnp.mean(xc * xc, axis=-1, keepdims=True)
    y = xc * lax.rsqrt(var + eps) * g.astype(jnp.float32) + b.astype(jnp.float32)
    return y.astype(x.dtype)


def alibi_slopes():
    h = jnp.arange(1, NSA_HEADS + 1, dtype=jnp.float32)
    return jnp.exp2(-8.0 * h / NSA_HEADS).reshape(KV_GROUPS, HEADS_PER_GROUP)


def compress_blocks(kv, pos, w1, w2):
    b, g, s, dh = kv.shape
    r = CMP_BLOCK // CMP_STRIDE
    n_cmp = s // CMP_STRIDE - r + 1
    sub = kv.reshape(b, g, s // CMP_STRIDE, CMP_STRIDE, dh)
    blocks = jnp.concatenate([sub[:, :, j:j + n_cmp] for j in range(r)], axis=3)
    blocks = (blocks + pos).reshape(b, g, n_cmp, CMP_BLOCK * dh)
    return jax.nn.gelu(blocks @ w1) @ w2


def compressed_branch(q, kc, vc, slopes):
    s = q.shape[3]
    n_cmp = kc.shape[2]
    t = jnp.arange(s, dtype=jnp.int32)[:, None]
    end = jnp.arange(n_cmp, dtype=jnp.int32)[None, :] * CMP_STRIDE + (CMP_BLOCK - 1)
    dist = (t - end).astype(jnp.float32)
    valid = dist >= 0
    logits = jnp.einsum('bgrtd,bgnd->bgrtn', q, kc).astype(jnp.float32)
    logits = logits - slopes[:, :, None, None] * dist
    logits = jnp.where(valid, logits, NEG)
    p = jax.nn.softmax(logits, axis=-1) * valid
    o = jnp.einsum('bgrtn,bgnd->bgrtd', p.astype(vc.dtype), vc)
    return o, p


def selection_indices(p_cmp, s):
    n_cmp = p_cmp.shape[-1]
    n_sel = s // SEL_BLOCK
    c0 = np.arange(n_cmp)[:, None] * CMP_STRIDE
    s0 = np.arange(n_sel)[None, :] * SEL_BLOCK
    overlap = np.clip(np.minimum(c0 + CMP_BLOCK, s0 + SEL_BLOCK) - np.maximum(c0, s0), 0, None) / CMP_BLOCK
    imp = jnp.einsum('bgtn,nj->bgtj', p_cmp.sum(axis=2), jnp.asarray(overlap, jnp.float32))
    t_blk = (jnp.arange(s, dtype=jnp.int32) // SEL_BLOCK)[:, None]
    j = jnp.arange(n_sel, dtype=jnp.int32)[None, :]
    forced = (j == 0) | (j == t_blk) | (j == t_blk - 1)
    imp = jnp.where(forced, imp + FORCE_BONUS, imp)
    imp = jnp.where(j > t_blk, NEG, imp)
    _, idx = lax.top_k(imp, min(SEL_TOPK, n_sel))
    return idx


def selected_branch(q, ks, vs, idx, slopes):
    b, g, r, s, dh = q.shape
    k = idx.shape[-1]
    n_sel = s // SEL_BLOCK
    nq = s // Q_BLOCK
    kb = ks.reshape(b, g, n_sel, SEL_BLOCK, dh)
    vb = vs.reshape(b, g, n_sel, SEL_BLOCK, dh)
    qc = jnp.moveaxis(q.reshape(b, g, r, nq, Q_BLOCK, dh), 3, 0)
    ic = jnp.moveaxis(idx.reshape(b, g, nq, Q_BLOCK, k), 2, 0)
    tc = jnp.arange(s, dtype=jnp.int32).reshape(nq, Q_BLOCK)
    bi = jnp.arange(b)[:, None, None, None]
    gi = jnp.arange(g)[None, :, None, None]
    offs = jnp.arange(SEL_BLOCK, dtype=jnp.int32)

    def one_block(args):
        qq, ii, tt = args
        k_sel = kb[bi, gi, ii].reshape(b, g, Q_BLOCK, k * SEL_BLOCK, dh)
        v_sel = vb[bi, gi, ii].reshape(b, g, Q_BLOCK, k * SEL_BLOCK, dh)
        pos = (ii[..., None] * SEL_BLOCK + offs).reshape(b, g, Q_BLOCK, k * SEL_BLOCK)
        dist = (tt[None, None, :, None] - pos).astype(jnp.float32)[:, :, None]
        logits = jnp.einsum('bgrqd,bgqkd->bgrqk', qq, k_sel).astype(jnp.float32)
        logits = logits - slopes[None, :, :, None, None] * dist
        logits = jnp.where(dist >= 0, logits, NEG)
        p = jax.nn.softmax(logits, axis=-1)
        return jnp.einsum('bgrqk,bgqkd->bgrqd', p.astype(v_sel.dtype), v_sel)

    o = lax.map(one_block, (qc, ic, tc))
    return jnp.moveaxis(o, 0, 3).reshape(b, g, r, s, dh)


def window_branch(q, kw, vw, slopes):
    b, g, r, s, dh = q.shape
    nq = s // Q_BLOCK
    nw = WINDOW // Q_BLOCK
    pad = ((0, 0), (0, 0), (WINDOW, 0), (0, 0))
    kp = jnp.pad(kw, pad).reshape(b, g, nq + nw, Q_BLOCK, dh)
    vp = jnp.pad(vw, pad).reshape(b, g, nq + nw, Q_BLOCK, dh)
    k_band = jnp.concatenate([kp[:, :, j:j + nq] for j in range(nw + 1)], axis=3)
    v_band = jnp.concatenate([vp[:, :, j:j + nq] for j in range(nw + 1)], axis=3)
    qb = q.reshape(b, g, r, nq, Q_BLOCK, dh)
    t = jnp.arange(s, dtype=jnp.int32).reshape(nq, Q_BLOCK)
    kpos = (jnp.arange(nq, dtype=jnp.int32)[:, None] - nw) * Q_BLOCK + jnp.arange((nw + 1) * Q_BLOCK, dtype=jnp.int32)[None, :]
    dist = t[:, :, None] - kpos[:, None, :]
    valid = (dist >= 0) & (dist < WINDOW) & (kpos[:, None, :] >= 0)
    logits = jnp.einsum('bgrnqd,bgnkd->bgrnqk', qb, k_band).astype(jnp.float32)
    logits = logits - slopes[:, :, None, None, None] * dist.astype(jnp.float32)
    logits = jnp.where(valid, logits, NEG)
    p = jax.nn.softmax(logits, axis=-1)
    o = jnp.einsum('bgrnqk,bgnkd->bgrnqd', p.astype(v_band.dtype), v_band)
    return o.reshape(b, g, r, s, dh)


def multiscale_pool(u, w_pool, scale):
    b, s, c = u.shape
    uf = u.astype(jnp.float32)
    cs = jnp.pad(jnp.cumsum(uf, axis=1), ((0, 0), (1, 0), (0, 0)))
    t = jnp.arange(s, dtype=jnp.float32)
    outs = []
    for gi, w in enumerate(POOL_WINDOWS):
        sl = slice(gi * POOL_GROUP_DIM, (gi + 1) * POOL_GROUP_DIM)
        csg = cs[:, :, sl]
        start = jnp.pad(csg, ((0, 0), (w - 1, 0), (0, 0)))[:, :s]
        mean = (csg[:, 1:] - start) / jnp.minimum(t + 1.0, float(w))[None, :, None]
        outs.append(mean - uf[:, :, sl])
    d = jnp.stack(outs, axis=2).astype(u.dtype)
    y = jnp.einsum('bsgc,gce->bsge', d, w_pool).reshape(b, s, c)
    return y * scale


def nsa_pool_mixer(h, w_in, pos_k, pos_v, k_w1, k_w2, v_w1, v_w2, w_pool, pool_scale, w_out):
    b, s, _ = h.shape
    proj = h @ w_in
    sizes = [NSA_WIDTH] + [KV_WIDTH] * (2 * N_BRANCH) + [GATE_WIDTH]
    cuts = [int(c) for c in np.cumsum(sizes)]
    q, kc, vc, ks, vs, kw, vw, gate, u = jnp.split(proj, cuts, axis=-1)

    def kv_heads(z):
        return z.reshape(b, s, KV_GROUPS, HEAD_DIM).transpose(0, 2, 1, 3)

    q = q.reshape(b, s, KV_GROUPS, HEADS_PER_GROUP, HEAD_DIM).transpose(0, 2, 3, 1, 4) * (HEAD_DIM ** -0.5)
    slopes = alibi_slopes()

    kcmp = compress_blocks(kv_heads(kc), pos_k, k_w1, k_w2)
    vcmp = compress_blocks(kv_heads(vc), pos_v, v_w1, v_w2)
    o_cmp, p_cmp = compressed_branch(q, kcmp, vcmp, slopes)
    idx = selection_indices(p_cmp, s)
    o_sel = selected_branch(q, kv_heads(ks), kv_heads(vs), idx, slopes)
    o_win = window_branch(q, kv_heads(kw), kv_heads(vw), slopes)

    g = jax.nn.sigmoid(gate.reshape(b, s, KV_GROUPS, HEADS_PER_GROUP, N_BRANCH)).transpose(0, 2, 3, 1, 4)
    o = g[..., 0:1] * o_cmp + g[..., 1:2] * o_sel + g[..., 2:3] * o_win
    o = o.transpose(0, 3, 1, 2, 4).reshape(b, s, NSA_WIDTH)
    y_pool = multiscale_pool(u, w_pool, pool_scale)
    return jnp.concatenate([o, y_pool], axis=-1) @ w_out


def conformer_conv(h, w_pw1, b_pw1, w_dw, b_dw, ln_g, ln_b, w_pw2, b_pw2):
    a, gt = jnp.split(h @ w_pw1 + b_pw1, 2, axis=-1)
    z = a * jax.nn.sigmoid(gt)
    z = lax.conv_general_dilated(
        z, w_dw[:, None, :], window_strides=(1,), padding=[(CONV_WIDTH - 1, 0)],
        dimension_numbers=('NWC', 'WIO', 'NWC'), feature_group_count=D_MODEL) + b_dw
    z = jax.nn.silu(layernorm(z, ln_g, ln_b))
    return z @ w_pw2 + b_pw2


def swiglu(h, w_gate, w_up, w_down):
    return (jax.nn.silu(h @ w_gate) * (h @ w_up)) @ w_down


def setup_inputs(seed: int = 0) -> dict:
    key = jax.random.key(seed)
    keys = iter(jax.random.split(key, 32))

    def nrm(shape, scale):
        return jax.random.normal(next(keys), shape, jnp.float32) * scale

    def gain(shape):
        return 1.0 + nrm(shape, 0.05)

    dh = HEAD_DIM
    return {
        "x": nrm((BATCH, SEQ, D_MODEL), 1.0),
        "mix_norm": gain((DEPTH, D_MODEL)),
        "ffn_norm": gain((DEPTH, D_MODEL)),
        "nsa_w_in": nrm((N_EVEN, D_MODEL, IN_WIDTH), D_MODEL ** -0.5),
        "cmp_pos_k": nrm((N_EVEN, CMP_BLOCK, dh), 0.1),
        "cmp_pos_v": nrm((N_EVEN, CMP_BLOCK, dh), 0.1),
        "cmp_k_w1": nrm((N_EVEN, CMP_BLOCK * dh, CMP_HIDDEN), (CMP_BLOCK * dh) ** -0.5),
        "cmp_k_w2": nrm((N_EVEN, CMP_HIDDEN, dh), CMP_HIDDEN ** -0.5),
        "cmp_v_w1": nrm((N_EVEN, CMP_BLOCK * dh, CMP_HIDDEN), (CMP_BLOCK * dh) ** -0.5),
        "cmp_v_w2": nrm((N_EVEN, CMP_HIDDEN, dh), CMP_HIDDEN ** -0.5),
        "pool_w": nrm((N_EVEN, POOL_GROUPS, POOL_GROUP_DIM, POOL_GROUP_DIM), POOL_GROUP_DIM ** -0.5),
        "pool_scale": gain((N_EVEN, POOL_WIDTH)),
        "mix_w_out": nrm((N_EVEN, MIX_WIDTH, D_MODEL), MIX_WIDTH ** -0.5),
        "conv_w_pw1": nrm((N_ODD, D_MODEL, 2 * D_MODEL), D_MODEL ** -0.5),
        "conv_b_pw1": nrm((N_ODD, 2 * D_MODEL), 0.02),
        "conv_w_dw": nrm((N_ODD, CONV_WIDTH, D_MODEL), CONV_WIDTH ** -0.5),
        "conv_b_dw": nrm((N_ODD, D_MODEL), 0.02),
        "conv_ln_g": gain((N_ODD, D_MODEL)),
        "conv_ln_b": nrm((N_ODD, D_MODEL), 0.02),
        "conv_w_pw2": nrm((N_ODD, D_MODEL, D_MODEL), D_MODEL ** -0.5),
        "conv_b_pw2": nrm((N_ODD, D_MODEL), 0.02),
        "ffn_w_gate": nrm((DEPTH, D_MODEL, FFN_HIDDEN), D_MODEL ** -0.5),
        "ffn_w_up": nrm((DEPTH, D_MODEL, FFN_HIDDEN), D_MODEL ** -0.5),
        "ffn_w_down": nrm((DEPTH, FFN_HIDDEN, D_MODEL), FFN_HIDDEN ** -0.5),
        "final_norm": gain((D_MODEL,)),
    }


def reference(x, mix_norm, ffn_norm, nsa_w_in, cmp_pos_k, cmp_pos_v, cmp_k_w1, cmp_k_w2,
              cmp_v_w1, cmp_v_w2, pool_w, pool_scale, mix_w_out, conv_w_pw1, conv_b_pw1,
              conv_w_dw, conv_b_dw, conv_ln_g, conv_ln_b, conv_w_pw2, conv_b_pw2,
              ffn_w_gate, ffn_w_up, ffn_w_down, final_norm):
    for layer in range(DEPTH):
        h = rmsnorm(x, mix_norm[layer])
        i = layer // 2
        if layer % 2 == 0:
            x = x + nsa_pool_mixer(h, nsa_w_in[i], cmp_pos_k[i], cmp_pos_v[i], cmp_k_w1[i], cmp_k_w2[i],
                                   cmp_v_w1[i], cmp_v_w2[i], pool_w[i], pool_scale[i], mix_w_out[i])
        else:
            x = x + conformer_conv(h, conv_w_pw1[i], conv_b_pw1[i], conv_w_dw[i], conv_b_dw[i],
                                   conv_ln_g[i], conv_ln_b[i], conv_w_pw2[i], conv_b_pw2[i])
        h = rmsnorm(x, ffn_norm[layer])
        x = x + swiglu(h, ffn_w_gate[layer], ffn_w_up[layer], ffn_w_down[layer])
    return rmsnorm(x, final_norm)
```

```python
from contextlib import ExitStack, contextmanager
import numpy as np
import concourse.bass as bass
import concourse.mybir as mybir
from concourse.bass_utils import run_bass_kernel_spmd

F32 = mybir.dt.float32
BF16 = mybir.dt.bfloat16
AF = mybir.ActivationFunctionType
ALU = mybir.AluOpType

S = 4096
D = 1024
T = 512
NT = S // T
HID = 2816
HC = HID // 128
TG = 2048
NEGB = -30000.0
ENGS = ("pe", "act", "dve", "pool", "sp")
SLOPES = [2.0 ** (-(h + 1)) for h in range(8)]


class Buf:
    __slots__ = ("name", "writes", "reads", "dsem", "psum")

    def __init__(self, name, psum=False):
        self.name = name
        self.writes = {}
        self.reads = {}
        self.dsem = None
        self.psum = psum


class TB:
    __slots__ = ("t", "b")

    def __init__(self, t, b):
        self.t = t
        self.b = b


class _Rec:
    def __init__(self):
        self.call = None

    def __getattr__(self, name):
        def f(*a, **k):
            self.call = (name, a, k)
            return self
        return f


class FW:
    def __init__(self, nc, stack):
        self.nc = nc
        self.stack = stack
        self.ops = {e: [] for e in ENGS}
        self.sems = {}
        self.count = {}
        self.seen = {e: {} for e in ENGS}
        self.same_engine_sync = True
        self.mute = False
        self.ekey = {}
        self.epoch = 0
        for e in ENGS:
            self.ekey[e] = self._newsem("E_" + e)
        self.free_dsems = {'sw': [], 'hw': []}

    def _newsem(self, key):
        h = self.stack.enter_context(self.nc.semaphore(key))
        self.sems[key] = h
        self.count[key] = 0
        return key

    def _deps(self, eng, reads, writes):
        deps = {}
        own = self.ekey[eng]
        for b in reads:
            for k, v in b.writes.items():
                if deps.get(k, 0) < v:
                    deps[k] = v
            if b.psum:
                for k, v in b.reads.items():
                    if k != own and deps.get(k, 0) < v:
                        deps[k] = v
        for b in writes:
            for d in (b.writes, b.reads):
                for k, v in d.items():
                    if deps.get(k, 0) < v:
                        deps[k] = v
        out = []
        seen = self.seen[eng]
        for k, v in deps.items():
            if k == own and (eng == "pe" or not self.same_engine_sync):
                continue
            if seen.get(k, 0) >= v:
                continue
            seen[k] = v
            out.append((k, v))
        return out

    def _record(self, key, val, reads, writes):
        for b in reads:
            b.reads[key] = val
        for b in writes:
            b.reads = {}
            b.writes[key] = val

    def op(self, eng, fn, reads=(), writes=()):
        if self.mute:
            return
        waits = self._deps(eng, reads, writes)
        key = self.ekey[eng]
        self.count[key] += 1
        val = self.count[key]
        sems = self.sems
        rec = _Rec()
        fn(rec)
        cname, cargs, ckw = rec.call

        def emit(e, waits=waits, key=key, cname=cname, cargs=cargs, ckw=ckw):
            for k, v in waits:
                e.wait_ge(sems[k], v)
            getattr(e, cname)(*cargs, **ckw).then_inc(sems[key], 1)
        self.ops[eng].append(emit)
        self._record(key, val, reads, writes)

    def dma(self, eng, out_ap, in_ap, reads=(), writes=(), sem_buf=None):
        if self.mute:
            return
        waits = self._deps(eng, reads, writes)
        sb = sem_buf
        qk = "sw" if eng == "pool" else "hw"
        if sb.dsem is None:
            sb.dsem = {}
        if qk not in sb.dsem:
            if self.free_dsems[qk]:
                sb.dsem[qk] = self.free_dsems[qk].pop()
            else:
                sb.dsem[qk] = self._newsem("D%s%d" % (qk, len(self.sems)))
        key = sb.dsem[qk]
        self.count[key] += 16
        val = self.count[key]
        sems = self.sems

        def emit(e, waits=waits, key=key):
            for k, v in waits:
                e.wait_ge(sems[k], v)
            e.dma_start(out=out_ap, in_=in_ap).then_inc(sems[key], 16)
        self.ops[eng].append(emit)
        self._record(key, val, reads, writes)

    def barrier(self):
        allb = Buf("ALL")
        allb.writes = {k: v for k, v in self.count.items() if v > 0}
        sems = self.sems
        for eng in ENGS:
            waits = self._deps(eng, [allb], ())

            def emit(e, waits=waits):
                for k, v in waits:
                    e.wait_ge(sems[k], v)
            self.ops[eng].append(emit)

    def new_epoch(self):
        self.epoch += 1
        for e in ENGS:
            self.ekey[e] = self._newsem("E_%s_%d" % (e, self.epoch))

    def finish(self):
        nc = self.nc
        ops = self.ops
        with nc.Block() as block:
            @block.tensor
            def _(e):
                for f in ops["pe"]:
                    f(e)

            @block.scalar
            def _(e):
                for f in ops["act"]:
                    f(e)

            @block.vector
            def _(e):
                for f in ops["dve"]:
                    f(e)

            @block.gpsimd
            def _(e):
                for f in ops["pool"]:
                    f(e)

            @block.sync
            def _(e):
                for f in ops["sp"]:
                    f(e)


class Phase:
    def __init__(self, A, name):
        self.A = A
        self.name = name
        self.stack = ExitStack()
        self.bufs = []

    def sb(self, name, shape, dtype):
        t = self.stack.enter_context(self.A.nc.sbuf_tensor(self.name + "_" + name, shape, dtype))
        tb = TB(t, Buf(name))
        self.bufs.append(tb)
        return tb

    def close(self):
        self.A.fw.barrier()
        self.A.fw.new_epoch()
        for tb in self.bufs:
            if tb.b.dsem is not None:
                for qk, key in tb.b.dsem.items():
                    self.A.fw.free_dsems[qk].append(key)
                tb.b.dsem = None
        self.stack.close()


def _small_layout():
    off = {}
    c = 0

    def add(name, n):
        nonlocal c
        off[name] = c
        c += n
    add("normg", 9 * 8)
    for i in range(2):
        add("cb1_%d" % i, 16)
        add("cbdw_%d" % i, 8)
        add("clng_%d" % i, 8)
        add("clnb_%d" % i, 8)
        add("cb2_%d" % i, 8)
        add("cdw_%d" % i, 31 * 8)
        add("pscale_%d" % i, 4)
    add("bias_sw", 8 * 32)
    add("bias_cmp", 8 * 2 * 8)
    add("fbw", 126)
    add("rc15", 4 * 16)
    add("arow", 512)
    add("eps6", 1)
    add("eps5", 1)
    add("eps30", 1)
    off["_n"] = c
    return off


SM = _small_layout()


def _bf_consts():
    off = {}
    c = 0

    def add(name, n):
        nonlocal c
        off[name] = c
        c += n
    add("caus", 896)
    add("far", 896)
    add("g1", 2048)
    add("ident", 128)
    add("ones", 128)
    add("ova", 2 * 65)
    add("eg", 24 * 64 + 64)
    off["_n"] = c
    return off


BC = _bf_consts()


def make_consts():
    sm = np.zeros((128, SM["_n"]), np.float32)
    p = np.arange(128)[:, None].astype(np.float64)
    for h in range(8):
        for di, delta in enumerate(range(-28, 4)):
            sm[:, SM["bias_sw"] + h * 32 + di] = (SLOPES[h] * (128 * delta + p[:, 0] - 256))
        for cc in range(2):
            for i in range(8):
                sm[:, SM["bias_cmp"] + (h * 2 + cc) * 8 + i] = SLOPES[h] * (16 * (128 * cc + p[:, 0]) + 15 - 512 * i - 256)
    hb = (np.arange(128) >= 64).astype(np.int64)[:, None]
    x = (np.arange(126) - 62)[None, :]
    fb = np.zeros((128, 126), np.float32)
    fb[(x == hb) | (x == hb - 1)] = 1e4
    fb[np.broadcast_to(x > hb, fb.shape)] = -1e30
    sm[:, SM["fbw"]:SM["fbw"] + 126] = fb
    for m, w in enumerate((2, 4, 8, 16)):
        t = np.arange(16)
        sm[:, SM["rc15"] + m * 16: SM["rc15"] + (m + 1) * 16] = (1.0 / np.minimum(t + 1.0, float(w)))[None, :]
    sm[:, SM["arow"]:SM["arow"] + 512] = (np.arange(512) - 256)[None, :]
    sm[:, SM["eps6"]] = 1024.0 * 1e-6
    sm[:, SM["eps5"]] = 1e-5
    sm[:, SM["eps30"]] = 1e-30

    bc = np.zeros((128, BC["_n"]), np.float32)
    xx = (np.arange(896) - 384)[None, :]
    pp = np.arange(128)[:, None]
    bc[:, BC["caus"]:BC["caus"] + 896] = np.where(xx >= pp, 0.0, NEGB)
    bc[:, BC["far"]:BC["far"] + 896] = np.where(xx < pp, 0.0, NEGB)
    x2 = np.arange(2048)[None, :]
    g = np.where(x2 >= 16 * pp + 15, 0.0, NEGB)
    bc[:, BC["g1"]:BC["g1"] + 2048] = g
    bc[:, BC["ident"]:BC["ident"] + 128] = np.eye(128)
    bc[:, BC["ones"]:BC["ones"] + 128] = 1.0
    ova = np.zeros((256, 65), np.float32)
    for n1 in range(1, 256):
        n = n1 - 1
        c0 = n * 16
        for j in range(64):
            s0 = j * 64
            ov = min(c0 + 32, s0 + 64) - max(c0, s0)
            if ov > 0:
                ova[n1, j] = ov / 32.0
        ova[n1, 64] = 1.0
    bc[:, BC["ova"]:BC["ova"] + 130] = ova.reshape(2, 128, 65).transpose(1, 0, 2).reshape(128, 130)
    eg = np.zeros((128, 24, 64), np.float32)
    for r in range(24):
        eg[r, r, :] = 1.0
    bc[:, BC["eg"]:BC["eg"] + 24 * 64] = eg.reshape(128, -1)
    ex = np.zeros((64, S), np.float32)
    for j in range(64):
        ex[j, j * 64:(j + 1) * 64] = 1.0
    e0 = np.zeros((64, S), np.float32)
    e0[0, :] = 1.0
    return sm, bc, ex, e0


class Prog:
    def __init__(self, layers=(0, 1, 2, 3), final=True, skip_ffn=False):
        self.layers = tuple(layers)
        self.final = final
        self.skip_ffn = skip_ffn
        self.nc = bass.Bass("TRN2", target_bir_lowering=False)
        self.stack = ExitStack()
        self.dins = {}

    def dram_in(self, name, shape):
        if name not in self.dins:
            self.dins[name] = self.nc.dram_tensor(name, list(shape), F32, kind="ExternalInput").ap()
        return self.dins[name]

    @property
    def d_wgu(self): return self.dram_in("wgu", (4, HC, 128, 2 * 8 * 128))
    @property
    def d_wd(self): return self.dram_in("wd", (4, 8, 128, HC * 128))
    @property
    def d_win(self): return self.dram_in("win", (2, 128, 8 * 1816))
    @property
    def d_wo(self): return self.dram_in("wo", (2, 128, 8 * 1024))
    @property
    def d_w1c(self): return self.dram_in("w1c", (2, 128, 2 * 2 * 32 * 128))
    @property
    def d_post(self): return self.dram_in("post", (2, 128, 2 * 32 * 2))
    @property
    def d_w2c(self): return self.dram_in("w2c", (2, 128, 2 * 64))
    @property
    def d_pw(self): return self.dram_in("poolw", (2, 128, 4 * 128))
    @property
    def d_cw1(self): return self.dram_in("cw1", (2, 128, 8 * 2048))
    @property
    def d_cw2(self): return self.dram_in("cw2", (2, 128, 8 * 1024))
    @property
    def d_ex(self): return self.dram_in("exrows", (64, S))
    @property
    def d_e0(self): return self.dram_in("e0rows", (64, S))

    @contextmanager
    def phase(self, name):
        ph = Phase(self, name)
        try:
            yield ph
        finally:
            ph.close()

    def build(self):
        nc = self.nc
        st = self.stack
        with st:
            self.fw = FW(nc, st)
            fw = self.fw
            self.x_in = self.dram_in("xT", (D, S))
            self.y_out = nc.dram_tensor("yT", [D, S], F32, kind="ExternalOutput").ap()
            self.xs = nc.dram_tensor("xs", [D, S], F32).ap()
            self.d_sm = self.dram_in("smalls", (128, SM["_n"]))
            self.d_bc = self.dram_in("bconst", (128, BC["_n"]))
            self.xs_b = [Buf("xs%d" % i) for i in range(NT)]
            self.xin_b = Buf("x_in")
            self.y_b = [Buf("y%d" % i) for i in range(NT)]
            sm_t = st.enter_context(nc.sbuf_tensor("smalls_sb", [128, SM["_n"]], F32))
            self.sm = TB(sm_t, Buf("smalls"))
            bc_t = st.enter_context(nc.sbuf_tensor("bconst_sb", [128, BC["_n"]], BF16))
            self.bc = TB(bc_t, Buf("bconst"))
            idf_t = st.enter_context(nc.sbuf_tensor("identf", [128, 128], F32))
            self.identf = TB(idf_t, Buf("identf"))
            self.ps = []
            for i in range(8):
                t = st.enter_context(nc.psum_tensor("ps%d" % i, [128, 512], F32))
                self.ps.append(TB(t, Buf("ps%d" % i, psum=True)))
            fw.dma("sp", sm_t[:], self.d_sm[:], writes=[self.sm.b], sem_buf=self.sm.b)
            half = BC["_n"] // 2
            fw.dma("pool", bc_t[:, 0:half], self.d_bc[:, 0:half], writes=[self.bc.b], sem_buf=self.bc.b)
            fw.dma("pool", bc_t[:, half:], self.d_bc[:, half:], writes=[self.bc.b], sem_buf=self.bc.b)
            ng = sm_t[:, SM["normg"]:SM["normg"] + 72]
            fw.op("dve", lambda e: e.tensor_scalar(ng, ng, 32.0, None, ALU.mult), writes=[self.sm.b])
            fw.op("dve", lambda e: e.tensor_copy(idf_t[:], bc_t[:, BC["ident"]:BC["ident"] + 128]),
                  reads=[self.bc.b], writes=[self.identf.b])
            fw.barrier()

            first = True
            for l in self.layers:
                src = (self.x_in, [self.xin_b] * NT) if first else (self.xs, self.xs_b)
                first = False
                if l % 2 == 0:
                    self.even_layer(l, src)
                else:
                    self.conv_layer(l, src)
                if not self.skip_ffn:
                    self.ffn_layer(l)
            if self.final:
                src = (self.x_in, [self.xin_b] * NT) if first else (self.xs, self.xs_b)
                self.final_norm(src)
            fw.barrier()
            fw.finish()
        return nc

    def tile_ap(self, dram, i):
        return dram.rearrange("(c p) t -> p c t", p=128)[:, :, i * T:(i + 1) * T]

    def ones_bf(self):
        return self.bc.t[:, BC["ones"]:BC["ones"] + 128]

    def ident_bf(self):
        return self.bc.t[:, BC["ident"]:BC["ident"] + 128]

    def smc(self, name, j=0, n=1):
        o = SM[name] + j
        return self.sm.t[:, o:o + n]

    def rmsnorm(self, x, h, sq, rstd, ps, gidx, sqc=8):
        fw = self.fw
        for q in range(8 // sqc):
            fw.op("act", lambda e: e.activation(sq.t[:, 0:sqc, :], x.t[:, q * sqc:(q + 1) * sqc, :], AF.Square), reads=[x.b], writes=[sq.b])
            for kk in range(sqc):
                k = q * sqc + kk
                fw.op("pe", lambda e: e.matmul(ps.t[:], self.ones_bf(), sq.t[:, kk, :], start=(k == 0), stop=(k == 7)),
                      reads=[sq.b, self.bc.b], writes=[ps.b])
        fw.op("act", lambda e: e.activation(rstd.t[:], ps.t[:], AF.Sqrt, bias=self.smc("eps6"), scale=1.0),
              reads=[ps.b, self.sm.b], writes=[rstd.b])
        fw.op("dve", lambda e: e.reciprocal(rstd.t[:], rstd.t[:]), reads=[rstd.b], writes=[rstd.b])
        hap, hb = h
        for k in range(8):
            fw.op("dve", lambda e, k=k: e.scalar_tensor_tensor(hap(k), x.t[:, k, :], self.smc("normg", gidx * 8 + k),
                                                             rstd.t[:], ALU.mult, ALU.mult),
                  reads=[x.b, rstd.b, self.sm.b], writes=[hb])

    def ffn_layer(self, l):
        fw = self.fw
        ntg = TG // T
        ngrp = S // TG
        with self.phase("ffn%d" % l) as ph:
            hT = ph.sb("hT", [128, 8, TG], BF16)
            hT_b = [Buf("hT%d" % i) for i in range(ntg)]
            act = ph.sb("act", [128, HC, TG], BF16)
            act_b = [Buf("act%d" % i) for i in range(ntg)]
            xin = [ph.sb("xin%d" % i, [128, 8, T], F32) for i in range(2)]
            sq = ph.sb("sq", [128, 4, T], BF16)
            rstd = ph.sb("rstd", [128, T], F32)
            wgu = [ph.sb("wgu%d" % i, [128, 2, 8, 128], BF16) for i in range(3)]
            wd = [ph.sb("wd%d" % i, [128, HC, 128], BF16) for i in range(2)]
            sg = [ph.sb("sg%d" % i, [128, T], F32) for i in range(2)]
            xc = [ph.sb("xc%d" % i, [128, T], F32) for i in range(2)]
            ps = self.ps
            ncnt = [0]

            def load_wgu(hc):
                w = wgu[hc % 3]
                fw.dma("pool", w.t[:].rearrange("p a k c -> p (a k c)"), self.d_wgu[l, hc], writes=[w.b], sem_buf=w.b)

            def load_wd(m):
                w = wd[m % 2]
                fw.dma("pool", w.t[:].rearrange("p k c -> p (k c)"), self.d_wd[l, m], writes=[w.b], sem_buf=w.b)

            def norm_tile(gi, tl):
                ti = gi * ntg + tl
                x = xin[ncnt[0] % 2]
                ncnt[0] += 1
                fw.dma("sp", x.t[:], self.tile_ap(self.xs, ti), reads=[self.xs_b[ti]], writes=[x.b], sem_buf=x.b)
                self.rmsnorm(x, (lambda k, tl=tl: hT.t[:, k, tl * T:(tl + 1) * T], hT_b[tl]), sq, rstd, ps[6], 4 + l, sqc=4)

            load_wgu(0)
            load_wgu(1)
            for tl in range(ntg):
                norm_tile(0, tl)
            for gi in range(ngrp):
                if gi > 0:
                    load_wgu(0)
                    load_wgu(1)
                it = 0
                for hc in range(HC):
                    if hc + 2 < HC:
                        load_wgu(hc + 2)
                    elif hc + 2 == HC:
                        load_wd(0)
                    w = wgu[hc % 3]
                    for tl in range(ntg):
                        pg = ps[it % 2]
                        pu = ps[2 + it % 2]
                        s_ = sg[it % 2]
                        it += 1
                        for a_, pp in ((0, pg), (1, pu)):
                            for k in range(8):
                                fw.op("pe", lambda e: e.matmul(
                                    pp.t[:], w.t[:, a_, k, :], hT.t[:, k, tl * T:(tl + 1) * T], start=(k == 0), stop=(k == 7)),
                                    reads=[w.b, hT_b[tl]], writes=[pp.b])
                        fw.op("act", lambda e: e.activation(s_.t[:], pg.t[:], AF.Silu), reads=[pg.b], writes=[s_.b])
                        fw.op("dve", lambda e: e.tensor_tensor(
                            act.t[:, hc, tl * T:(tl + 1) * T], pu.t[:], s_.t[:], ALU.mult),
                            reads=[pu.b, s_.b], writes=[act_b[tl]])
                it = 0
                for m in range(8):
                    if m + 1 < 8:
                        load_wd(m + 1)
                    w = wd[m % 2]
                    for tl in range(ntg):
                        ti = gi * ntg + tl
                        po = ps[4 + it % 2]
                        x = xc[it % 2]
                        it += 1
                        cap = self.xs[m * 128:(m + 1) * 128, ti * T:(ti + 1) * T]
                        fw.dma("sp", x.t[:], cap, reads=[self.xs_b[ti]], writes=[x.b], sem_buf=x.b)
                        for k in range(HC):
                            fw.op("pe", lambda e: e.matmul(
                                po.t[:], w.t[:, k, :], act.t[:, k, tl * T:(tl + 1) * T], start=(k == 0), stop=(k == HC - 1)),
                                reads=[w.b, act_b[tl]], writes=[po.b])
                        fw.op("dve", lambda e: e.tensor_tensor(x.t[:], po.t[:], x.t[:], ALU.add),
                              reads=[po.b, x.b], writes=[x.b])
                        fw.dma("sp", cap, x.t[:], reads=[x.b], writes=[self.xs_b[ti]], sem_buf=x.b)
                    if gi + 1 < ngrp and m % 2 == 1 and m // 2 < ntg:
                        norm_tile(gi + 1, m // 2)

    def final_norm(self, src):
        fw = self.fw
        sdram, sb = src
        with self.phase("fin") as ph:
            xin = [ph.sb("xin%d" % i, [128, 8, T], F32) for i in range(2)]
            yo = [ph.sb("yo%d" % i, [128, 8, T], F32) for i in range(2)]
            sq = ph.sb("sq", [128, 8, T], BF16)
            rstd = ph.sb("rstd", [128, T], F32)
            for ti in range(NT):
                x = xin[ti % 2]
                y = yo[ti % 2]
                fw.dma("sp", x.t[:], self.tile_ap(sdram, ti), reads=[sb[ti]], writes=[x.b], sem_buf=x.b)
                if self.final == "copy":
                    fw.dma("sp", self.tile_ap(self.y_out, ti), x.t[:], reads=[x.b], writes=[self.y_b[ti]], sem_buf=x.b)
                    continue
                self.rmsnorm(x, (lambda k, y=y: y.t[:, k, :], y.b), sq, rstd, self.ps[6], 8)
                fw.dma("sp", self.tile_ap(self.y_out, ti), y.t[:], reads=[y.b], writes=[self.y_b[ti]], sem_buf=y.b)

    def conv_layer(self, l, src):
        fw = self.fw
        i2 = l // 2
        sdram, sbufs = src
        ps = self.ps
        with self.phase("conv%d" % l) as ph:
            w1 = ph.sb("w1", [128, 8, 2048], BF16)
            w2 = ph.sb("w2", [128, 8, 1024], BF16)
            dg = ph.sb("dg", [128, 31 * 8, 128], BF16)
            dg_b = [Buf("dg0"), Buf("dg1")]
            xin = [ph.sb("xin0", [128, 8, T], F32)] * 2
            h = ph.sb("h", [128, 8, T], BF16)
            sq = h
            rstd = ph.sb("rstd", [128, T], F32)
            z = [ph.sb("z%d" % i, [128, 8, 30 + T], BF16) for i in range(2)]
            sg = [ph.sb("sg%d" % i, [128, T], F32) for i in range(2)]
            y = ph.sb("y", [128, 8, T], F32)
            ybf = h
            mean = ph.sb("mean", [128, T], F32)
            msq = ph.sb("msq", [128, T], F32)
            lrs = ph.sb("lrs", [128, T], F32)
            yn = sg
            s_bf = ph.sb("s_bf", [128, 8, T], BF16)
            ysq = s_bf
            for q in range(4):
                fw.dma("pool", w1.t[:, 2 * q:2 * q + 2, :].rearrange("p k c -> p (k c)"),
                       self.d_cw1[i2, :, 2 * q * 2048:(2 * q + 2) * 2048], writes=[w1.b], sem_buf=w1.b)
            for q in range(2):
                fw.dma("pool", w2.t[:, 4 * q:4 * q + 4, :].rearrange("p k c -> p (k c)"),
                       self.d_cw2[i2, :, 4 * q * 1024:(4 * q + 4) * 1024], writes=[w2.b], sem_buf=w2.b)
            for j in range(31 * 8):
                if j % 2 == 0:
                    fw.op("dve", lambda e, j=j: e.tensor_scalar(dg.t[:, j, :], self.identf.t[:], self.smc("cdw_%d" % i2, j), None, ALU.mult),
                          reads=[self.identf.b, self.sm.b], writes=[dg_b[0]])
                else:
                    fw.op("act", lambda e, j=j: e.activation(dg.t[:, j, :], self.identf.t[:], AF.Identity, scale=self.smc("cdw_%d" % i2, j)),
                          reads=[self.identf.b, self.sm.b], writes=[dg_b[1]])
            fw.op("pool", lambda e: e.memset(z[0].t[:, :, 0:30], 0.0), writes=[z[0].b])
            for ti in range(NT):
                x = xin[ti % 2]
                zc = z[ti % 2]
                zn = z[(ti + 1) % 2]
                if ti == 0:
                    fw.dma("sp", x.t[:], self.tile_ap(sdram, ti), reads=[sbufs[ti]], writes=[x.b], sem_buf=x.b)
                self.rmsnorm(x, (lambda k: h.t[:, k, :], h.b), sq, rstd, ps[6], l)
                if ti + 1 < NT:
                    fw.dma("sp", x.t[:], self.tile_ap(sdram, ti + 1), reads=[sbufs[ti + 1]], writes=[x.b], sem_buf=x.b)
                for m in range(8):
                    pa = ps[m % 2]
                    pg = ps[2 + m % 2]
                    s = sg[m % 2]
                    for a, pp in ((0, pa), (1, pg)):
                        for k in range(8):
                            fw.op("pe", lambda e, a=a, k=k, pp=pp, m=m: e.matmul(
                                pp.t[:], w1.t[:, k, a * 1024 + m * 128: a * 1024 + (m + 1) * 128], h.t[:, k, :],
                                start=(k == 0), stop=(k == 7)), reads=[w1.b, h.b], writes=[pp.b])
                    fw.op("act", lambda e, pg=pg, s=s, m=m: e.activation(s.t[:], pg.t[:], AF.Sigmoid, bias=self.smc("cb1_%d" % i2, 8 + m)),
                          reads=[pg.b, self.sm.b], writes=[s.b])
                    fw.op("dve", lambda e, pa=pa, s=s, m=m, zc=zc: e.scalar_tensor_tensor(
                        zc.t[:, m, 30:30 + T], pa.t[:], self.smc("cb1_%d" % i2, m), s.t[:], ALU.add, ALU.mult),
                        reads=[pa.b, s.b, self.sm.b], writes=[zc.b])
                if ti + 1 < NT:
                    fw.op("pool", lambda e, zc=zc, zn=zn: e.tensor_copy(zn.t[:, :, 0:30], zc.t[:, :, T:T + 30]),
                          reads=[zc.b], writes=[zn.b])
                for m in range(8):
                    pc = ps[4 + m % 2]
                    for k in range(31):
                        fw.op("pe", lambda e, k=k, m=m, pc=pc, zc=zc: e.matmul(
                            pc.t[:], dg.t[:, k * 8 + m, :], zc.t[:, m, k:k + T], start=(k == 0), stop=(k == 30)),
                            reads=[dg_b[0], dg_b[1], zc.b], writes=[pc.b])
                    fw.op("act", lambda e, m=m, pc=pc: e.activation(y.t[:, m, :], pc.t[:], AF.Identity, bias=self.smc("cbdw_%d" % i2, m)),
                          reads=[pc.b, self.sm.b], writes=[y.b])
                fw.op("dve", lambda e: e.tensor_copy(ybf.t[:], y.t[:]), reads=[y.b], writes=[ybf.b])
                fw.op("act", lambda e: e.activation(ysq.t[:], y.t[:], AF.Square), reads=[y.b], writes=[ysq.b])
                p1, p2 = ps[6], ps[7]
                for k in range(8):
                    fw.op("pe", lambda e, k=k: e.matmul(p1.t[:], self.ones_bf(), ybf.t[:, k, :], start=(k == 0), stop=(k == 7)),
                          reads=[ybf.b, self.bc.b], writes=[p1.b])
                for k in range(8):
                    fw.op("pe", lambda e, k=k: e.matmul(p2.t[:], self.ones_bf(), ysq.t[:, k, :], start=(k == 0), stop=(k == 7)),
                          reads=[ysq.b, self.bc.b], writes=[p2.b])
                fw.op("dve", lambda e: e.tensor_scalar(mean.t[:], p1.t[:], 1.0 / 1024, None, ALU.mult), reads=[p1.b], writes=[mean.b])
                fw.op("dve", lambda e: e.tensor_tensor(msq.t[:], mean.t[:], mean.t[:], ALU.mult), reads=[mean.b], writes=[msq.b])
                fw.op("dve", lambda e: e.scalar_tensor_tensor(lrs.t[:], p2.t[:], 1.0 / 1024, msq.t[:], ALU.mult, ALU.subtract),
                      reads=[p2.b, msq.b], writes=[lrs.b])
                fw.op("act", lambda e: e.activation(lrs.t[:], lrs.t[:], AF.Sqrt, bias=self.smc("eps5"), scale=1.0),
                      reads=[lrs.b, self.sm.b], writes=[lrs.b])
                fw.op("dve", lambda e: e.reciprocal(lrs.t[:], lrs.t[:]), reads=[lrs.b], writes=[lrs.b])
                for m in range(8):
                    yy = yn[m % 2]
                    fw.op("dve", lambda e, m=m, yy=yy: e.tensor_tensor(yy.t[:], y.t[:, m, :], mean.t[:], ALU.subtract),
                          reads=[y.b, mean.b], writes=[yy.b])
                    fw.op("dve", lambda e, yy=yy: e.tensor_tensor(yy.t[:], yy.t[:], lrs.t[:], ALU.mult),
                          reads=[yy.b, lrs.b], writes=[yy.b])
                    fw.op("act", lambda e, m=m, yy=yy: e.activation(s_bf.t[:, m, :], yy.t[:], AF.Silu,
                                                                    bias=self.smc("clnb_%d" % i2, m), scale=self.smc("clng_%d" % i2, m)),
                          reads=[yy.b, self.sm.b], writes=[s_bf.b])
                for m in range(8):
                    po = ps[m % 2]
                    for k in range(8):
                        fw.op("pe", lambda e, k=k, m=m, po=po: e.matmul(po.t[:], w2.t[:, k, m * 128:(m + 1) * 128], s_bf.t[:, k, :],
                                                                        start=(k == 0), stop=(k == 7)),
                              reads=[w2.b, s_bf.b], writes=[po.b])
                    xq = sg[m % 2]
                    src_c = sdram[m * 128:(m + 1) * 128, ti * T:(ti + 1) * T]
                    dst_c = self.xs[m * 128:(m + 1) * 128, ti * T:(ti + 1) * T]
                    fw.dma("sp", xq.t[:], src_c, reads=[sbufs[ti]], writes=[xq.b], sem_buf=xq.b)
                    fw.op("dve", lambda e, m=m, po=po, xq=xq: e.scalar_tensor_tensor(
                        xq.t[:], po.t[:], self.smc("cb2_%d" % i2, m), xq.t[:], ALU.add, ALU.add),
                        reads=[po.b, xq.b, self.sm.b], writes=[xq.b])
                    fw.dma("sp", dst_c, xq.t[:], reads=[xq.b], writes=[self.xs_b[ti]], sem_buf=xq.b)

    def even_layer(self, l, src):
        fw = self.fw
        import os
        STAGE = int(os.environ.get('EVEN_STAGE', '9'))
        SUB = os.environ.get('EVEN_SUB', 'NQMKGVP')
        BR = os.environ.get('EVEN_BR', 'csw')
        HEADS = [int(c) for c in os.environ.get('EVEN_HEADS', '01234567')]
        i2 = l // 2
        sdram, sbufs = src
        ps = self.ps
        bc = self.bc
        with self.phase("nsa%d" % l) as ph:
            win = ph.sb("win", [128, 8, 1816], BF16)
            wo = ph.sb("wo", [128, 8, 1024], BF16)
            w1c = ph.sb("w1c", [128, 2, 2, 32, 128], BF16)
            post = ph.sb("post", [128, 2, 32, 2], BF16)
            w2c = ph.sb("w2c", [128, 2, 64], BF16)
            pw = ph.sb("pw", [128, 4, 128], BF16)
            KS = [ph.sb("ks%d" % g, [128, S], BF16) for g in range(2)]
            KW = [ph.sb("kw%d" % g, [128, 1024], BF16) for g in range(2)]
            VS = ph.sb("vs", [128, 2, 32, 128], BF16)
            VW = ph.sb("vw", [128, 2, 8, 128], BF16)
            KCT = ph.sb("kct", [128, 16 + T], BF16)
            VCT = ph.sb("vct", [128, 16 + T], BF16)
            KC = [ph.sb("kc%d" % g, [128, 256], BF16) for g in range(2)]
            VC = ph.sb("vc", [128, 2, 2, 128], BF16)
            Qs = [ph.sb("q%d" % h, [128, T], BF16) for h in range(8)]
            x = ph.sb("x", [128, 8, T], F32)
            h_ = ph.sb("h", [128, 8, T], BF16)
            sgT = ph.sb("sgT", [128, T], BF16)
            pa = [ph.sb("pa%d" % i, [128, 16 + T], F32) for i in range(3)]
            rstd = TB(pa[2].t[:, 0:T], pa[2].b)
            halo = ph.sb("halo", [128, 4, 16], F32)
            dT = ph.sb("dT", [128, 4, T], BF16)
            yp = ph.sb("yp", [128, 4, T], BF16)
            oT = ph.sb("oT", [128, 4, T], BF16)
            E = [ph.sb("E%d" % i, [128, T], BF16) for i in range(3)]
            Ec = [ph.sb("Ec%d" % i, [128, T], BF16) for i in range(2)]
            zc = ph.sb("zc", [128, T], F32)
            gz = ph.sb("gz", [128, T], F32)
            acc2 = [ph.sb("acc%d" % i, [128, T], F32) for i in range(2)]
            zc_b = [Buf("zc0"), Buf("zc1")]
            gz_b = [Buf("gz0"), Buf("gz1")]
            acc_b = [Buf("acc%d" % i) for i in range(4)]
            impa = ph.sb("impa", [128, 4, 64], F32)
            impf = ph.sb("impf", [128, 64], F32)
            top8 = ph.sb("top8", [128, 8], F32)
            selb = [ph.sb("selb0", [128, 128], F32)] * 2
            rz4 = ph.sb("rz4", [128, 4], F32)
            posb = ph.sb("posb", [128, 2], F32)
            gu = ph.sb("gu", [128, 32], F32)
            gv = ph.sb("gv", [128, 32], F32)
            gs = ph.sb("gs", [128, 32], F32)
            G = ph.sb("G", [128, 128], BF16)
            for q in range(4):
                fw.dma("pool", win.t[:, 2 * q:2 * q + 2, :].rearrange("p k c -> p (k c)"),
                       self.d_win[i2, :, 2 * q * 1816:(2 * q + 2) * 1816], writes=[win.b], sem_buf=win.b)
            for q in range(2):
                fw.dma("pool", wo.t[:, 4 * q:4 * q + 4, :].rearrange("p k c -> p (k c)"),
                       self.d_wo[i2, :, 4 * q * 1024:(4 * q + 4) * 1024], writes=[wo.b], sem_buf=wo.b)
            for g_ in range(2):
                for s_ in range(2):
                    for q in range(2):
                        o_ = ((g_ * 2 + s_) * 32 + 16 * q) * 128
                        fw.dma("pool", w1c.t[:, g_, s_, 16 * q:16 * q + 16, :].rearrange("p l c -> p (l c)"),
                               self.d_w1c[i2, :, o_:o_ + 2048], writes=[w1c.b], sem_buf=w1c.b)
            fw.dma("pool", post.t[:].rearrange("p s l r -> p (s l r)"), self.d_post[i2], writes=[post.b], sem_buf=post.b)
            fw.dma("pool", w2c.t[:].rearrange("p s d -> p (s d)"), self.d_w2c[i2], writes=[w2c.b], sem_buf=w2c.b)
            fw.dma("pool", pw.t[:].rearrange("p m e -> p (m e)"), self.d_pw[i2], writes=[pw.b], sem_buf=pw.b)
            for g in range(2):
                fw.dma("pool", KS[g].t[64:128, :], self.d_ex[:, :], writes=[KS[g].b], sem_buf=KS[g].b)
                fw.dma("pool", KW[g].t[64:128, :], self.d_e0[:, 0:1024], writes=[KW[g].b], sem_buf=KW[g].b)
                fw.dma("pool", KC[g].t[64:128, :], self.d_e0[:, 0:256], writes=[KC[g].b], sem_buf=KC[g].b)
                fw.op("dve", lambda e, g=g: e.memset(KC[g].t[0:64, :], 0.0), writes=[KC[g].b])
            fw.op("dve", lambda e: e.memset(VS.t[:, :, :, 64:128], 1.0), writes=[VS.b])
            fw.op("dve", lambda e: e.memset(VW.t[:, :, :, 64:128], 1.0), writes=[VW.b])
            fw.op("dve", lambda e: e.memset(VC.t[:, :, :, 0:64], 0.0), writes=[VC.b])
            fw.op("dve", lambda e: e.memset(VC.t[:, :, :, 64:128], 1.0), writes=[VC.b])
            fw.op("dve", lambda e: e.memset(KCT.t[:, 0:16], 0.0), writes=[KCT.b])
            fw.op("dve", lambda e: e.memset(VCT.t[:, 0:16], 0.0), writes=[VCT.b])
            fw.op("dve", lambda e: e.memset(halo.t[:], 0.0), writes=[halo.b])
            fw.op("dve", lambda e: e.memset(oT.t[:], 0.0), writes=[oT.b])
            fw.op("dve", lambda e: e.memset(G.t[:], 0.0), writes=[G.b])
            fw.op("dve", lambda e: e.memset(selb[0].t[:], 0.0), writes=[selb[0].b])
            for s_ in range(2):
                pp = ps[6 + s_]
                for ll in range(32):
                    fw.op("pe", lambda e, s_=s_, ll=ll, pp=pp: e.matmul(pp.t[:, 0:2], w1c.t[0:64, 0, s_, ll, :], post.t[0:64, s_, ll, :],
                                                                      start=(ll == 0), stop=(ll == 31)),
                          reads=[w1c.b, post.b], writes=[pp.b])
                fw.op("dve", lambda e, s_=s_, pp=pp: e.tensor_copy(posb.t[:, s_:s_ + 1], pp.t[:, 0:1]), reads=[pp.b], writes=[posb.b])

            fw.barrier()
            gen_i = [0]

            def gbank():
                gen_i[0] += 1
                return ps[6 + gen_i[0] % 2]

            ecnt = [0]
            scnt = [0]
            ocnt = [0]

            fw.op("dve", lambda e: e.memset(yp.t[:], 0.0), writes=[yp.b])
            for i in range(NT):
                c0 = i * T
                fw.mute = False
                if i == 0:
                    fw.dma("sp", x.t[:], self.tile_ap(sdram, i), reads=[sbufs[i]], writes=[x.b], sem_buf=x.b)
                fw.mute = STAGE < 1 or 'N' not in SUB
                self.rmsnorm(x, (lambda k: h_.t[:, k, :], h_.b), h_, rstd, ps[5], l)
                fw.mute = False
                if i + 1 < NT:
                    fw.dma("sp", x.t[:], self.tile_ap(sdram, i + 1), reads=[sbufs[i + 1]], writes=[x.b], sem_buf=x.b)

                def proj(col0, ncol):
                    pp = gbank()
                    for k in range(8):
                        fw.op("pe", lambda e, k=k, pp=pp: e.matmul(pp.t[0:ncol, :], win.t[:, k, col0:col0 + ncol], h_.t[:, k, :],
                                                                  start=(k == 0), stop=(k == 7)),
                              reads=[win.b, h_.b], writes=[pp.b])
                    return pp
                fw.mute = STAGE < 1 or 'Q' not in SUB
                for m in range(4):
                    pp = proj(m * 128, 128)
                    ha, hb2 = 2 * m, 2 * m + 1
                    fw.op("act", lambda e, pp=pp, ha=ha: e.activation(Qs[ha].t[0:64, :], pp.t[0:64, :], AF.Identity, scale=0.125),
                          reads=[pp.b], writes=[Qs[ha].b])
                    fw.op("dve", lambda e, pp=pp, hb2=hb2: e.tensor_scalar(Qs[hb2].t[0:64, :], pp.t[64:128, :], 0.125, None, ALU.mult),
                          reads=[pp.b], writes=[Qs[hb2].b])
                fw.mute = STAGE < 1 or 'M' not in SUB
                for hh in range(8):
                    fw.op("dve", lambda e, hh=hh: e.tensor_scalar(Qs[hh].t[64:128, :], self.smc("arow", 0, 512)[64:128, :], -SLOPES[hh], None, ALU.mult),
                          reads=[self.sm.b], writes=[Qs[hh].b])
                fw.mute = STAGE < 1 or 'K' not in SUB
                pp = proj(512, 128)
                fw.op("act", lambda e, pp=pp: e.activation(KCT.t[:, 16:16 + T], pp.t[:], AF.Identity), reads=[pp.b], writes=[KCT.b])
                pp = proj(640, 128)
                fw.op("dve", lambda e, pp=pp: e.tensor_copy(VCT.t[:, 16:16 + T], pp.t[:]), reads=[pp.b], writes=[VCT.b])
                pp = proj(768, 128)
                fw.op("act", lambda e, pp=pp: e.activation(KS[0].t[0:64, c0:c0 + T], pp.t[0:64, :], AF.Identity), reads=[pp.b], writes=[KS[0].b])
                fw.op("dve", lambda e, pp=pp: e.tensor_copy(KS[1].t[0:64, c0:c0 + T], pp.t[64:128, :]), reads=[pp.b], writes=[KS[1].b])
                pp = proj(896, 128)
                r0 = (i % 2) * T
                fw.op("act", lambda e, pp=pp: e.activation(KW[0].t[0:64, r0:r0 + T], pp.t[0:64, :], AF.Identity), reads=[pp.b], writes=[KW[0].b])
                fw.op("dve", lambda e, pp=pp: e.tensor_copy(KW[1].t[0:64, r0:r0 + T], pp.t[64:128, :]), reads=[pp.b], writes=[KW[1].b])
                fw.mute = STAGE < 1 or 'G' not in SUB
                pp = proj(1024, 128)
                fw.op("act", lambda e, pp=pp: e.activation(sgT.t[:], pp.t[:], AF.Sigmoid), reads=[pp.b], writes=[sgT.b])
                fw.mute = STAGE < 1 or 'V' not in SUB
                for s_ in range(4):
                    cs = 4 * i + s_
                    for half in range(2):
                        pp = gbank()
                        for k in range(8):
                            fw.op("pe", lambda e, k=k, pp=pp, s_=s_, half=half: e.matmul(
                                pp.t[:, 0:128], h_.t[:, k, s_ * 128:(s_ + 1) * 128],
                                win.t[:, k, 1560 + half * 128:1560 + (half + 1) * 128], start=(k == 0), stop=(k == 7)),
                                reads=[win.b, h_.b], writes=[pp.b])
                        if half == 0:
                            fw.op("act", lambda e, pp=pp, cs=cs: e.activation(VS.t[:, :, cs, 0:64], pp.t[:, 0:128].rearrange("p (g d) -> p g d", g=2), AF.Identity),
                                  reads=[pp.b], writes=[VS.b])
                        else:
                            fw.op("dve", lambda e, pp=pp, cs=cs: e.tensor_copy(VW.t[:, :, cs % 8, 0:64], pp.t[:, 0:128].rearrange("p (g d) -> p g d", g=2)),
                                  reads=[pp.b], writes=[VW.b])
                fw.mute = STAGE < 1 or 'P' not in SUB
                for m in range(4):
                    w_ = 2 ** (m + 1)
                    pp = proj(1048 + m * 128, 128)
                    A_, B_, C_ = pa
                    fw.op("act", lambda e, pp=pp: e.activation(A_.t[:, 16:16 + T], pp.t[:], AF.Identity), reads=[pp.b], writes=[A_.b])
                    fw.op("dve", lambda e, m=m: e.tensor_copy(A_.t[:, 0:16], halo.t[:, m, :]), reads=[halo.b], writes=[A_.b])
                    fw.op("dve", lambda e, m=m: e.tensor_copy(halo.t[:, m, :], A_.t[:, T:T + 16]), reads=[A_.b], writes=[halo.b])
                    cur = A_
                    for st_ in range(m + 1):
                        sh = 2 ** st_
                        lo = 2 ** (st_ + 1) - 1
                        dst = B_ if st_ % 2 == 0 else C_
                        fw.op("dve", lambda e, cur=cur, dst=dst, sh=sh, lo=lo: e.tensor_tensor(
                            dst.t[:, lo:16 + T], cur.t[:, lo:16 + T], cur.t[:, lo - sh:16 + T - sh], ALU.add),
                            reads=[cur.b], writes=[dst.b])
                        cur = dst
                    fw.op("dve", lambda e, cur=cur, m=m, w_=w_: e.scalar_tensor_tensor(dT.t[:, m, :], cur.t[:, 16:16 + T], 1.0 / w_, A_.t[:, 16:16 + T],
                                                                                     ALU.mult, ALU.subtract),
                          reads=[cur.b, A_.b], writes=[dT.b])
                    if i == 0:
                        fw.op("dve", lambda e, cur=cur, m=m: e.tensor_tensor(gu.t[:, 0:16], cur.t[:, 16:32], self.smc("rc15", m * 16, 16), ALU.mult),
                              reads=[cur.b, self.sm.b], writes=[gu.b])
                        fw.op("dve", lambda e, m=m: e.tensor_tensor(dT.t[:, m, 0:16], gu.t[:, 0:16], A_.t[:, 16:32], ALU.subtract),
                              reads=[gu.b, A_.b], writes=[dT.b])
                    pq = gbank()
                    fw.op("pe", lambda e, pq=pq, m=m: e.matmul(pq.t[:], pw.t[:, m, :], dT.t[:, m, :], start=True, stop=True),
                          reads=[pw.b, dT.b], writes=[pq.b])
                    fw.op("act", lambda e, pq=pq, m=m: e.activation(yp.t[:, m, :], pq.t[:], AF.Identity, scale=self.smc("pscale_%d" % i2, m)),
                          reads=[pq.b, self.sm.b], writes=[yp.b])
                fw.mute = STAGE < 2
                ccn = (32 * i) // 128
                pn = (32 * i) % 128
                for s_ in range(2):
                    XT = KCT if s_ == 0 else VCT
                    for g in range(2):
                        pp = gbank()
                        for ll in range(32):
                            fw.op("pe", lambda e, ll=ll, pp=pp, g=g, s_=s_, XT=XT: e.matmul(
                                pp.t[:, 0:32], w1c.t[:, g, s_, ll, :], XT.t[:, ll:ll + 497:16],
                                start=(ll == 0), stop=(ll == 31)), reads=[w1c.b, XT.b], writes=[pp.b])
                        fw.op("act", lambda e, pp=pp, s_=s_: e.activation(gu.t[:], pp.t[:, 0:32], AF.Identity, bias=posb.t[:, s_:s_ + 1]),
                              reads=[pp.b, posb.b], writes=[gu.b])
                        fw.op("dve", lambda e: e.tensor_tensor(gv.t[:], gu.t[:], gu.t[:], ALU.mult), reads=[gu.b], writes=[gv.b])
                        fw.op("dve", lambda e: e.tensor_scalar(gv.t[:], gv.t[:], 0.044715, 1.0, ALU.mult, ALU.add), reads=[gv.b], writes=[gv.b])
                        fw.op("dve", lambda e: e.tensor_tensor(gv.t[:], gv.t[:], gu.t[:], ALU.mult), reads=[gv.b, gu.b], writes=[gv.b])
                        fw.op("act", lambda e: e.activation(gs.t[:], gv.t[:], AF.Sigmoid, scale=1.5957691216057308), reads=[gv.b], writes=[gs.b])
                        fw.op("dve", lambda e: e.tensor_tensor(G.t[:, 0:32], gu.t[:], gs.t[:], ALU.mult), reads=[gu.b, gs.b], writes=[G.b])
                        pq = gbank()
                        if s_ == 0:
                            fw.op("pe", lambda e, pq=pq: e.matmul(pq.t[:, 0:32], w2c.t[:].rearrange("p s d -> p (s d)"), G.t[:, 0:32], start=True, stop=True),
                                  reads=[w2c.b, G.b], writes=[pq.b])
                            fw.op("dve", lambda e, pq=pq, g=g: e.tensor_copy(KC[g].t[0:64, 32 * i:32 * i + 32], pq.t[0:64, 0:32]),
                                  reads=[pq.b], writes=[KC[g].b])
                        else:
                            fw.op("pe", lambda e, pq=pq: e.matmul(pq.t[:, 0:64], G.t[:], w2c.t[:, 1, :], start=True, stop=True),
                                  reads=[w2c.b, G.b], writes=[pq.b])
                            fw.op("dve", lambda e, pq=pq, g=g: e.tensor_copy(VC.t[pn:pn + 32, g, ccn, 0:64], pq.t[0:32, 0:64]),
                                  reads=[pq.b], writes=[VC.b])
                            if i == 0:
                                fw.op("dve", lambda e, g=g: e.memset(VC.t[0:1, g, 0, :], 0.0), writes=[VC.b])
                fw.op("pool", lambda e: e.tensor_copy(KCT.t[:, 0:16], KCT.t[:, T:T + 16]), reads=[KCT.b], writes=[KCT.b])
                fw.op("pool", lambda e: e.tensor_copy(VCT.t[:, 0:16], VCT.t[:, T:T + 16]), reads=[VCT.b], writes=[VCT.b])

                fw.mute = STAGE < 3
                def run_branch(hh, chunks, ebufs, keep=False):
                    po = ps[3 + ocnt[0] % 2]
                    ocnt[0] += 1
                    n = len(chunks)
                    pend = None
                    used = []
                    for idx, (kap, kb, vap, vb, bcol, mask) in enumerate(chunks):
                        sb_ = ps[scnt[0] % 3]
                        scnt[0] += 1
                        eb = ebufs[ecnt[0] % len(ebufs)]
                        ecnt[0] += 1
                        fw.op("pe", lambda e, sb_=sb_, kap=kap, mask=mask: e.matmul(sb_.t[:], kap, Qs[hh].t[:], start=True, stop=(mask is None)),
                              reads=[kb, Qs[hh].b], writes=[sb_.b])
                        if mask is not None:
                            fw.op("pe", lambda e, sb_=sb_, mask=mask: e.matmul(sb_.t[:], self.ident_bf(), mask, start=False, stop=True),
                                  reads=[bc.b], writes=[sb_.b])
                        fw.op("act", lambda e, sb_=sb_, eb=eb, bcol=bcol: e.activation(eb.t[:], sb_.t[:], AF.Exp, bias=self.sm.t[:, bcol:bcol + 1]),
                              reads=[sb_.b, self.sm.b], writes=[eb.b])
                        if pend is not None:
                            pend()
                        used.append(eb)

                        def pv(idx=idx, vap=vap, vb=vb, eb=eb):
                            fw.op("pe", lambda e: e.matmul(po.t[:], vap, eb.t[:], start=(idx == 0), stop=(idx == n - 1)),
                                  reads=[vb, eb.b], writes=[po.b])
                        pend = pv
                    pend()
                    return po, used

                def combine(hh, b, po):
                    pg = gbank()
                    hp = (hh % 2) * 64
                    hs = slice(hp, hp + 64)
                    zcb, gzb = zc_b[hh % 2], gz_b[hh % 2]
                    ac = acc2[(hh % 4) // 2]
                    acb = acc_b[hh % 4]
                    on = ("csw"[b] in BR)
                    if on:
                        fw.op("pe", lambda e: e.matmul(pg.t[:], bc.t[:, BC["eg"] + (hh * 3 + b) * 64: BC["eg"] + (hh * 3 + b) * 64 + 128], sgT.t[:],
                                                       start=True, stop=True), reads=[bc.b, sgT.b], writes=[pg.b])
                        fw.op("act", lambda e: e.activation(zc.t[hs, :], po.t[64:128, :], AF.Identity, bias=self.smc("eps30")[hs, :]),
                              reads=[po.b, self.sm.b], writes=[zcb])
                        fw.op("dve", lambda e: e.reciprocal(zc.t[hs, :], zc.t[hs, :]), reads=[zcb], writes=[zcb])
                        fw.op("dve", lambda e: e.tensor_tensor(gz.t[hs, :], pg.t[0:64, :], zc.t[hs, :], ALU.mult), reads=[pg.b, zcb], writes=[gzb])
                    if b == 0:
                        if on:
                            fw.op("dve", lambda e: e.tensor_tensor(ac.t[hs, :], po.t[0:64, :], gz.t[hs, :], ALU.mult), reads=[po.b, gzb], writes=[acb])
                        else:
                            fw.op("dve", lambda e: e.memset(ac.t[hs, :], 0.0), writes=[acb])
                    elif b == 1:
                        if on:
                            fw.op("dve", lambda e: e.tensor_tensor(gz.t[hs, :], po.t[0:64, :], gz.t[hs, :], ALU.mult), reads=[po.b, gzb], writes=[gzb])
                            fw.op("pool", lambda e: e.tensor_tensor(ac.t[hs, :], ac.t[hs, :], gz.t[hs, :], ALU.add), reads=[acb, gzb], writes=[acb])
                    else:
                        if on:
                            fw.op("dve", lambda e: e.tensor_tensor(gz.t[hs, :], po.t[0:64, :], gz.t[hs, :], ALU.mult), reads=[po.b, gzb], writes=[gzb])
                            fw.op("dve", lambda e: e.tensor_tensor(oT.t[hs, hh // 2, :], ac.t[hs, :], gz.t[hs, :], ALU.add),
                                  reads=[acb, gzb], writes=[oT.b])
                        else:
                            fw.op("dve", lambda e: e.tensor_copy(oT.t[hs, hh // 2, :], ac.t[hs, :]), reads=[acb], writes=[oT.b])

                for g in range(2):
                    fw.mute = STAGE < 3
                    ccs = [0] if i < 4 else [0, 1]
                    for hi_, hh in enumerate(range(4 * g, 4 * g + 4)):
                        chunks = []
                        for cc in ccs:
                            x0 = 512 * i - 2048 * cc
                            mask = bc.t[:, BC["g1"] + x0: BC["g1"] + x0 + 512] if x0 <= 1536 else None
                            chunks.append((KC[g].t[:, cc * 128:(cc + 1) * 128], KC[g].b, VC.t[:, g, cc, :], VC.b,
                                           SM["bias_cmp"] + (hh * 2 + cc) * 8 + i, mask))
                        po, used = run_branch(hh, chunks, Ec)
                        pu = ps[5]
                        for s_ in range(4):
                            for ci, cc in enumerate(ccs):
                                fw.op("pe", lambda e, s_=s_, ci=ci, cc=cc: e.matmul(
                                    pu.t[:, s_ * 65:(s_ + 1) * 65], used[ci].t[:, s_ * 128:(s_ + 1) * 128],
                                    bc.t[:, BC["ova"] + cc * 65: BC["ova"] + (cc + 1) * 65], start=(ci == 0), stop=(ci == len(ccs) - 1)),
                                    reads=[used[ci].b, bc.b], writes=[pu.b])
                        combine(hh, 0, po)
                        zsl = pu.t[:, 0:260].rearrange("p (s c) -> p s c", c=65)[:, :, 64]
                        fw.op("dve", lambda e, zsl=zsl: e.tensor_scalar(rz4.t[:], zsl, 1e-30, None, ALU.add), reads=[pu.b], writes=[rz4.b])
                        fw.op("dve", lambda e: e.reciprocal(rz4.t[:], rz4.t[:]), reads=[rz4.b], writes=[rz4.b])
                        for s_ in range(4):
                            if hi_ == 0:
                                fw.op("dve", lambda e, s_=s_: e.tensor_scalar(impa.t[:, s_, :], pu.t[:, s_ * 65:s_ * 65 + 64], rz4.t[:, s_:s_ + 1], None, ALU.mult),
                                      reads=[pu.b, rz4.b], writes=[impa.b])
                            else:
                                fw.op("dve", lambda e, s_=s_: e.scalar_tensor_tensor(impa.t[:, s_, :], pu.t[:, s_ * 65:s_ * 65 + 64], rz4.t[:, s_:s_ + 1],
                                                                                   impa.t[:, s_, :], ALU.mult, ALU.add),
                                      reads=[pu.b, rz4.b, impa.b], writes=[impa.b])
                    fw.mute = STAGE < 4
                    pT = ps[5]
                    for s_ in range(4):
                        sub = 4 * i + s_
                        fo = SM["fbw"] + 62 - 2 * sub
                        sl_ = selb[s_ % 2]
                        fw.op("dve", lambda e, s_=s_, fo=fo: e.tensor_tensor(impf.t[:], impa.t[:, s_, :], self.sm.t[:, fo:fo + 64], ALU.add),
                              reads=[impa.b, self.sm.b], writes=[impf.b])
                        if sub >= 1:
                            fw.op("dve", lambda e: e.tensor_scalar(impf.t[:, 0:1], impf.t[:, 0:1], 1e4, None, ALU.add), reads=[impf.b], writes=[impf.b])
                        fw.op("dve", lambda e: e.max(top8.t[:], impf.t[:]), reads=[impf.b], writes=[top8.b])
                        fw.op("dve", lambda e, sl_=sl_: e.tensor_scalar(sl_.t[:, 0:64], impf.t[:], top8.t[:, 7:8], NEGB, ALU.is_lt, ALU.mult),
                              reads=[impf.b, top8.b], writes=[sl_.b])
                        fw.op("pe", lambda e, s_=s_, sl_=sl_: e.transpose(pT.t[:, s_ * 128:(s_ + 1) * 128], sl_.t[:], self.identf.t[:]),
                              reads=[sl_.b, self.identf.b], writes=[pT.b])
                    for hh in range(4 * g, 4 * g + 4):
                        fw.op("dve", lambda e, hh=hh: e.scalar_tensor_tensor(Qs[hh].t[64:128, :], self.smc("arow", 0, 512)[0:64, :], -SLOPES[hh], pT.t[0:64, :],
                                                                           ALU.mult, ALU.add),
                              reads=[self.sm.b, pT.b], writes=[Qs[hh].b])
                    fw.mute = STAGE < 5
                    for hh in range(4 * g, 4 * g + 4):
                        if hh not in HEADS:
                            continue
                        chunks = []
                        for c in range(0, 4 * i + 4):
                            dl = c - 4 * i
                            mask = None
                            if dl >= 0:
                                st0 = BC["caus"] + 384 - 128 * dl
                                mask = bc.t[:, st0:st0 + 512]
                            chunks.append((KS[g].t[:, c * 128:(c + 1) * 128], KS[g].b, VS.t[:, g, c, :], VS.b,
                                           SM["bias_sw"] + hh * 32 + dl + 28, mask))
                        po, _ = run_branch(hh, chunks, E)
                        combine(hh, 1, po)
                        chunks = []
                        for c in range(max(0, 4 * i - 4), 4 * i + 4):
                            dl = c - 4 * i
                            if dl >= 0:
                                st0 = BC["caus"] + 384 - 128 * dl
                            else:
                                st0 = BC["far"] + 384 - 128 * (dl + 4)
                            mask = bc.t[:, st0:st0 + 512]
                            sl8 = c % 8
                            chunks.append((KW[g].t[:, sl8 * 128:(sl8 + 1) * 128], KW[g].b, VW.t[:, g, sl8, :], VW.b,
                                           SM["bias_sw"] + hh * 32 + dl + 28, mask))
                        po, _ = run_branch(hh, chunks, E)
                        combine(hh, 2, po)
                fw.mute = False
                for mo in range(8):
                    pp = gbank()
                    for k in range(8):
                        rhs = oT.t[:, k, :] if k < 4 else yp.t[:, k - 4, :]
                        rb = oT.b if k < 4 else yp.b
                        fw.op("pe", lambda e, k=k, pp=pp, rhs=rhs, mo=mo: e.matmul(pp.t[:], wo.t[:, k, mo * 128:(mo + 1) * 128], rhs,
                                                                                  start=(k == 0), stop=(k == 7)),
                              reads=[wo.b, rb], writes=[pp.b])
                    xq = pa[mo % 2]
                    src_c = sdram[mo * 128:(mo + 1) * 128, i * T:(i + 1) * T]
                    dst_c = self.xs[mo * 128:(mo + 1) * 128, i * T:(i + 1) * T]
                    fw.dma("sp", xq.t[:, 0:T], src_c, reads=[sbufs[i]], writes=[xq.b], sem_buf=xq.b)
                    fw.op("dve", lambda e, pp=pp, mo=mo: e.tensor_tensor(xq.t[:, 0:T], pp.t[:], xq.t[:, 0:T], ALU.add),
                          reads=[pp.b, xq.b], writes=[xq.b])
                    fw.dma("sp", dst_c, xq.t[:, 0:T], reads=[xq.b], writes=[self.xs_b[i]], sem_buf=xq.b)


def _prep_weights(inp):
    f = lambda a: np.ascontiguousarray(a, dtype=np.float32)
    out = {}
    sm, bc, ex, e0 = make_consts()

    def pc(v):
        return np.asarray(v, np.float32).reshape(-1, 128).T

    for l in range(4):
        sm[:, SM["normg"] + l * 8: SM["normg"] + (l + 1) * 8] = pc(inp["mix_norm"][l])
        sm[:, SM["normg"] + (4 + l) * 8: SM["normg"] + (5 + l) * 8] = pc(inp["ffn_norm"][l])
    sm[:, SM["normg"] + 64: SM["normg"] + 72] = pc(inp["final_norm"])
    for i in range(2):
        sm[:, SM["cb1_%d" % i]: SM["cb1_%d" % i] + 16] = pc(inp["conv_b_pw1"][i])
        sm[:, SM["cbdw_%d" % i]: SM["cbdw_%d" % i] + 8] = pc(inp["conv_b_dw"][i])
        sm[:, SM["clng_%d" % i]: SM["clng_%d" % i] + 8] = pc(inp["conv_ln_g"][i])
        sm[:, SM["clnb_%d" % i]: SM["clnb_%d" % i] + 8] = pc(inp["conv_ln_b"][i])
        sm[:, SM["cb2_%d" % i]: SM["cb2_%d" % i] + 8] = pc(inp["conv_b_pw2"][i])
        wdw = np.asarray(inp["conv_w_dw"][i], np.float32)
        sm[:, SM["cdw_%d" % i]: SM["cdw_%d" % i] + 248] = wdw.reshape(31, 8, 128).transpose(2, 0, 1).reshape(128, 248)
        sm[:, SM["pscale_%d" % i]: SM["pscale_%d" % i] + 4] = pc(inp["pool_scale"][i])
    out["smalls"] = f(sm)
    out["bconst"] = f(bc)
    out["exrows"] = f(ex)
    out["e0rows"] = f(e0)
    wg = np.asarray(inp["ffn_w_gate"], np.float32).reshape(4, 8, 128, HC, 128)
    wu = np.asarray(inp["ffn_w_up"], np.float32).reshape(4, 8, 128, HC, 128)
    wgu = np.stack([wg, wu], axis=0)
    out["wgu"] = f(wgu.transpose(1, 4, 3, 0, 2, 5).reshape(4, HC, 128, 2 * 8 * 128))
    wd = np.asarray(inp["ffn_w_down"], np.float32).reshape(4, HC, 128, 8, 128)
    out["wd"] = f(wd.transpose(0, 3, 2, 1, 4).reshape(4, 8, 128, HC * 128))
    perm = np.concatenate([np.arange(0, 512), np.arange(512, 640), np.arange(640, 768), np.arange(768, 896),
                           np.arange(1024, 1152), np.arange(1280, 1304), np.arange(1304, 1816),
                           np.arange(896, 1024), np.arange(1152, 1280)])
    win = np.asarray(inp["nsa_w_in"], np.float32)[:, :, perm].reshape(2, 8, 128, 1816)
    out["win"] = f(win.transpose(0, 2, 1, 3).reshape(2, 128, 8 * 1816))
    wo = np.asarray(inp["mix_w_out"], np.float32).reshape(2, 8, 128, 1024)
    out["wo"] = f(wo.transpose(0, 2, 1, 3).reshape(2, 128, 8 * 1024))
    w1c = np.stack([np.asarray(inp["cmp_k_w1"], np.float32), np.asarray(inp["cmp_v_w1"], np.float32)], axis=1)
    w1c = w1c.reshape(2, 2, 32, 64, 128).transpose(0, 3, 1, 2, 4)
    w1z = np.zeros((2, 128, 2, 2, 32, 128), np.float32)
    w1z[:, 0:64, 0] = w1c
    w1z[:, 64:128, 1] = w1c
    out["w1c"] = f(w1z.reshape(2, 128, 2 * 2 * 32 * 128))
    pos = np.stack([np.asarray(inp["cmp_pos_k"], np.float32), np.asarray(inp["cmp_pos_v"], np.float32)], axis=1)
    pos = pos.transpose(0, 3, 1, 2)
    out["post"] = f(np.repeat(np.concatenate([pos, pos], axis=1).reshape(2, 128, 64), 2, axis=2))
    w2c = np.stack([np.asarray(inp["cmp_k_w2"], np.float32), np.asarray(inp["cmp_v_w2"], np.float32)], axis=1)
    out["w2c"] = f(w2c.transpose(0, 2, 1, 3).reshape(2, 128, 128))
    pw = np.asarray(inp["pool_w"], np.float32)
    out["poolw"] = f(pw.transpose(0, 2, 1, 3).reshape(2, 128, 512))
    cw1 = np.asarray(inp["conv_w_pw1"], np.float32).reshape(2, 8, 128, 2048)
    out["cw1"] = f(cw1.transpose(0, 2, 1, 3).reshape(2, 128, 8 * 2048))
    cw2 = np.asarray(inp["conv_w_pw2"], np.float32).reshape(2, 8, 128, 1024)
    out["cw2"] = f(cw2.transpose(0, 2, 1, 3).reshape(2, 128, 8 * 1024))
    return out


_CACHE = {}


def run(inputs, layers=(0, 1, 2, 3), final=True, n_cores=8):
    key = (tuple(layers), final)
    if key not in _CACHE:
        pr = Prog(layers, final)
        _CACHE[key] = (pr.build(), set(pr.dins.keys()))
    nc, names = _CACHE[key]
    w = _prep_weights(inputs)
    x = np.asarray(inputs["x"], np.float32)
    in_maps = []
    for b in range(n_cores):
        m = {k: v for k, v in w.items() if k in names}
        m["xT"] = np.ascontiguousarray(x[b].T)
        in_maps.append(m)
    res = run_bass_kernel_spmd(nc, in_maps, core_ids=list(range(n_cores)))
    out = np.stack([np.ascontiguousarray(res.results[b]["yT"].T) for b in range(n_cores)], axis=0)
    return out.astype(np.float32)


def kernel(**inputs):
    return run(inputs)
```

```python
from contextlib import ExitStack, contextmanager
import numpy as np
import concourse.bass as bass
import concourse.mybir as mybir
from concourse.bass_utils import run_bass_kernel_spmd

F32 = mybir.dt.float32
BF16 = mybir.dt.bfloat16
AF = mybir.ActivationFunctionType
ALU = mybir.AluOpType

S = 4096
D = 1024
T = 512
NT = S // T
HID = 2816
HC = HID // 128
TG = 2048
NEGB = -30000.0
ENGS = ("pe", "act", "dve", "pool", "sp")
SLOPES = [2.0 ** (-(h + 1)) for h in range(8)]


class Buf:
    __slots__ = ("name", "writes", "reads", "dsem", "psum")

    def __init__(self, name, psum=False):
        self.name = name
        self.writes = {}
        self.reads = {}
        self.dsem = None
        self.psum = psum


class TB:
    __slots__ = ("t", "b")

    def __init__(self, t, b):
        self.t = t
        self.b = b


class _Rec:
    def __init__(self):
        self.call = None

    def __getattr__(self, name):
        def f(*a, **k):
            self.call = (name, a, k)
            return self
        return f


class FW:
    def __init__(self, nc, stack):
        self.nc = nc
        self.stack = stack
        self.ops = {e: [] for e in ENGS}
        self.sems = {}
        self.count = {}
        self.seen = {e: {} for e in ENGS}
        self.same_engine_sync = True
        self.mute = False
        self.ekey = {}
        self.epoch = 0
        for e in ENGS:
            self.ekey[e] = self._newsem("E_" + e)
        self.free_dsems = {'sw': [], 'hw': []}

    def _newsem(self, key):
        h = self.stack.enter_context(self.nc.semaphore(key))
        self.sems[key] = h
        self.count[key] = 0
        return key

    def _deps(self, eng, reads, writes):
        deps = {}
        own = self.ekey[eng]
        for b in reads:
            for k, v in b.writes.items():
                if deps.get(k, 0) < v:
                    deps[k] = v
            if b.psum:
                for k, v in b.reads.items():
                    if k != own and deps.get(k, 0) < v:
                        deps[k] = v
        for b in writes:
            for d in (b.writes, b.reads):
                for k, v in d.items():
                    if deps.get(k, 0) < v:
                        deps[k] = v
        out = []
        seen = self.seen[eng]
        for k, v in deps.items():
            if k == own and (eng == "pe" or not self.same_engine_sync):
                continue
            if seen.get(k, 0) >= v:
                continue
            seen[k] = v
            out.append((k, v))
        return out

    def _record(self, key, val, reads, writes):
        for b in reads:
            b.reads[key] = val
        for b in writes:
            b.reads = {}
            b.writes[key] = val

    def op(self, eng, fn, reads=(), writes=()):
        if self.mute:
            return
        waits = self._deps(eng, reads, writes)
        key = self.ekey[eng]
        self.count[key] += 1
        val = self.count[key]
        sems = self.sems
        rec = _Rec()
        fn(rec)
        cname, cargs, ckw = rec.call

        def emit(e, waits=waits, key=key, cname=cname, cargs=cargs, ckw=ckw):
            for k, v in waits:
                e.wait_ge(sems[k], v)
            getattr(e, cname)(*cargs, **ckw).then_inc(sems[key], 1)
        self.ops[eng].append(emit)
        self._record(key, val, reads, writes)

    def dma(self, eng, out_ap, in_ap, reads=(), writes=(), sem_buf=None):
        if self.mute:
            return
        waits = self._deps(eng, reads, writes)
        sb = sem_buf
        qk = "sw" if eng == "pool" else "hw"
        if sb.dsem is None:
            sb.dsem = {}
        if qk not in sb.dsem:
            if self.free_dsems[qk]:
                sb.dsem[qk] = self.free_dsems[qk].pop()
            else:
                sb.dsem[qk] = self._newsem("D%s%d" % (qk, len(self.sems)))
        key = sb.dsem[qk]
        self.count[key] += 16
        val = self.count[key]
        sems = self.sems

        def emit(e, waits=waits, key=key):
            for k, v in waits:
                e.wait_ge(sems[k], v)
            e.dma_start(out=out_ap, in_=in_ap).then_inc(sems[key], 16)
        self.ops[eng].append(emit)
        self._record(key, val, reads, writes)

    def barrier(self):
        allb = Buf("ALL")
        allb.writes = {k: v for k, v in self.count.items() if v > 0}
        sems = self.sems
        for eng in ENGS:
            waits = self._deps(eng, [allb], ())

            def emit(e, waits=waits):
                for k, v in waits:
                    e.wait_ge(sems[k], v)
            self.ops[eng].append(emit)

    def new_epoch(self):
        self.epoch += 1
        for e in ENGS:
            self.ekey[e] = self._newsem("E_%s_%d" % (e, self.epoch))

    def finish(self):
        nc = self.nc
        ops = self.ops
        with nc.Block() as block:
            @block.tensor
            def _(e):
                for f in ops["pe"]:
                    f(e)

            @block.scalar
            def _(e):
                for f in ops["act"]:
                    f(e)

            @block.vector
            def _(e):
                for f in ops["dve"]:
                    f(e)

            @block.gpsimd
            def _(e):
                for f in ops["pool"]:
                    f(e)

            @block.sync
            def _(e):
                for f in ops["sp"]:
                    f(e)


class Phase:
    def __init__(self, A, name):
        self.A = A
        self.name = name
        self.stack = ExitStack()
        self.bufs = []

    def sb(self, name, shape, dtype):
        t = self.stack.enter_context(self.A.nc.sbuf_tensor(self.name + "_" + name, shape, dtype))
        tb = TB(t, Buf(name))
        self.bufs.append(tb)
        return tb

    def close(self):
        self.A.fw.barrier()
        self.A.fw.new_epoch()
        for tb in self.bufs:
            if tb.b.dsem is not None:
                for qk, key in tb.b.dsem.items():
                    self.A.fw.free_dsems[qk].append(key)
                tb.b.dsem = None
        self.stack.close()


def _small_layout():
    off = {}
    c = 0

    def add(name, n):
        nonlocal c
        off[name] = c
        c += n
    add("normg", 9 * 8)
    for i in range(2):
        add("cb1_%d" % i, 16)
        add("cbdw_%d" % i, 8)
        add("clng_%d" % i, 8)
        add("clnb_%d" % i, 8)
        add("cb2_%d" % i, 8)
        add("cdw_%d" % i, 31 * 8)
        add("pscale_%d" % i, 4)
    add("bias_sw", 8 * 32)
    add("bias_cmp", 8 * 2 * 8)
    add("fbw", 126)
    add("rc15", 4 * 16)
    add("arow", 512)
    add("eps6", 1)
    add("eps5", 1)
    add("eps30", 1)
    off["_n"] = c
    return off


SM = _small_layout()


def _bf_consts():
    off = {}
    c = 0

    def add(name, n):
        nonlocal c
        off[name] = c
        c += n
    add("caus", 896)
    add("far", 896)
    add("g1", 2048)
    add("ident", 128)
    add("ones", 128)
    add("ova", 2 * 65)
    add("eg", 24 * 64 + 64)
    off["_n"] = c
    return off


BC = _bf_consts()


def make_consts():
    sm = np.zeros((128, SM["_n"]), np.float32)
    p = np.arange(128)[:, None].astype(np.float64)
    for h in range(8):
        for di, delta in enumerate(range(-28, 4)):
            sm[:, SM["bias_sw"] + h * 32 + di] = (SLOPES[h] * (128 * delta + p[:, 0] - 256))
        for cc in range(2):
            for i in range(8):
                sm[:, SM["bias_cmp"] + (h * 2 + cc) * 8 + i] = SLOPES[h] * (16 * (128 * cc + p[:, 0]) + 15 - 512 * i - 256)
    hb = (np.arange(128) >= 64).astype(np.int64)[:, None]
    x = (np.arange(126) - 62)[None, :]
    fb = np.zeros((128, 126), np.float32)
    fb[(x == hb) | (x == hb - 1)] = 1e4
    fb[np.broadcast_to(x > hb, fb.shape)] = -1e30
    sm[:, SM["fbw"]:SM["fbw"] + 126] = fb
    for m, w in enumerate((2, 4, 8, 16)):
        t = np.arange(16)
        sm[:, SM["rc15"] + m * 16: SM["rc15"] + (m + 1) * 16] = (1.0 / np.minimum(t + 1.0, float(w)))[None, :]
    sm[:, SM["arow"]:SM["arow"] + 512] = (np.arange(512) - 256)[None, :]
    sm[:, SM["eps6"]] = 1024.0 * 1e-6
    sm[:, SM["eps5"]] = 1e-5
    sm[:, SM["eps30"]] = 1e-30

    bc = np.zeros((128, BC["_n"]), np.float32)
    xx = (np.arange(896) - 384)[None, :]
    pp = np.arange(128)[:, None]
    bc[:, BC["caus"]:BC["caus"] + 896] = np.where(xx >= pp, 0.0, NEGB)
    bc[:, BC["far"]:BC["far"] + 896] = np.where(xx < pp, 0.0, NEGB)
    x2 = np.arange(2048)[None, :]
    g = np.where(x2 >= 16 * pp + 15, 0.0, NEGB)
    bc[:, BC["g1"]:BC["g1"] + 2048] = g
    bc[:, BC["ident"]:BC["ident"] + 128] = np.eye(128)
    bc[:, BC["ones"]:BC["ones"] + 128] = 1.0
    ova = np.zeros((256, 65), np.float32)
    for n1 in range(1, 256):
        n = n1 - 1
        c0 = n * 16
        for j in range(64):
            s0 = j * 64
            ov = min(c0 + 32, s0 + 64) - max(c0, s0)
            if ov > 0:
                ova[n1, j] = ov / 32.0
        ova[n1, 64] = 1.0
    bc[:, BC["ova"]:BC["ova"] + 130] = ova.reshape(2, 128, 65).transpose(1, 0, 2).reshape(128, 130)
    eg = np.zeros((128, 24, 64), np.float32)
    for r in range(24):
        eg[r, r, :] = 1.0
    bc[:, BC["eg"]:BC["eg"] + 24 * 64] = eg.reshape(128, -1)
    ex = np.zeros((64, S), np.float32)
    for j in range(64):
        ex[j, j * 64:(j + 1) * 64] = 1.0
    e0 = np.zeros((64, S), np.float32)
    e0[0, :] = 1.0
    return sm, bc, ex, e0


class Prog:
    def __init__(self, layers=(0, 1, 2, 3), final=True, skip_ffn=False):
        self.layers = tuple(layers)
        self.final = final
        self.skip_ffn = skip_ffn
        self.nc = bass.Bass("TRN2", target_bir_lowering=False)
        self.stack = ExitStack()
        self.dins = {}

    def dram_in(self, name, shape):
        if name not in self.dins:
            self.dins[name] = self.nc.dram_tensor(name, list(shape), F32, kind="ExternalInput").ap()
        return self.dins[name]

    @property
    def d_wgu(self): return self.dram_in("wgu", (4, HC, 128, 2 * 8 * 128))
    @property
    def d_wd(self): return self.dram_in("wd", (4, 8, 128, HC * 128))
    @property
    def d_win(self): return self.dram_in("win", (2, 128, 8 * 1816))
    @property
    def d_wo(self): return self.dram_in("wo", (2, 128, 8 * 1024))
    @property
    def d_w1c(self): return self.dram_in("w1c", (2, 128, 2 * 2 * 32 * 128))
    @property
    def d_post(self): return self.dram_in("post", (2, 128, 2 * 32 * 2))
    @property
    def d_w2c(self): return self.dram_in("w2c", (2, 128, 2 * 64))
    @property
    def d_pw(self): return self.dram_in("poolw", (2, 128, 4 * 128))
    @property
    def d_cw1(self): return self.dram_in("cw1", (2, 128, 8 * 2048))
    @property
    def d_cw2(self): return self.dram_in("cw2", (2, 128, 8 * 1024))
    @property
    def d_ex(self): return self.dram_in("exrows", (64, S))
    @property
    def d_e0(self): return self.dram_in("e0rows", (64, S))

    @contextmanager
    def phase(self, name):
        ph = Phase(self, name)
        try:
            yield ph
        finally:
            ph.close()

    def build(self):
        nc = self.nc
        st = self.stack
        with st:
            self.fw = FW(nc, st)
            fw = self.fw
            self.x_in = self.dram_in("xT", (D, S))
            self.y_out = nc.dram_tensor("yT", [D, S], F32, kind="ExternalOutput").ap()
            self.xs = nc.dram_tensor("xs", [D, S], F32).ap()
            self.d_sm = self.dram_in("smalls", (128, SM["_n"]))
            self.d_bc = self.dram_in("bconst", (128, BC["_n"]))
            self.xs_b = [Buf("xs%d" % i) for i in range(NT)]
            self.xin_b = Buf("x_in")
            self.y_b = [Buf("y%d" % i) for i in range(NT)]
            sm_t = st.enter_context(nc.sbuf_tensor("smalls_sb", [128, SM["_n"]], F32))
            self.sm = TB(sm_t, Buf("smalls"))
            bc_t = st.enter_context(nc.sbuf_tensor("bconst_sb", [128, BC["_n"]], BF16))
            self.bc = TB(bc_t, Buf("bconst"))
            idf_t = st.enter_context(nc.sbuf_tensor("identf", [128, 128], F32))
            self.identf = TB(idf_t, Buf("identf"))
            self.ps = []
            for i in range(8):
                t = st.enter_context(nc.psum_tensor("ps%d" % i, [128, 512], F32))
                self.ps.append(TB(t, Buf("ps%d" % i, psum=True)))
            fw.dma("sp", sm_t[:], self.d_sm[:], writes=[self.sm.b], sem_buf=self.sm.b)
            half = BC["_n"] // 2
            fw.dma("pool", bc_t[:, 0:half], self.d_bc[:, 0:half], writes=[self.bc.b], sem_buf=self.bc.b)
            fw.dma("pool", bc_t[:, half:], self.d_bc[:, half:], writes=[self.bc.b], sem_buf=self.bc.b)
            ng = sm_t[:, SM["normg"]:SM["normg"] + 72]
            fw.op("dve", lambda e: e.tensor_scalar(ng, ng, 32.0, None, ALU.mult), writes=[self.sm.b])
            fw.op("dve", lambda e: e.tensor_copy(idf_t[:], bc_t[:, BC["ident"]:BC["ident"] + 128]),
                  reads=[self.bc.b], writes=[self.identf.b])
            fw.barrier()

            first = True
            for l in self.layers:
                src = (self.x_in, [self.xin_b] * NT) if first else (self.xs, self.xs_b)
                first = False
                if l % 2 == 0:
                    self.even_layer(l, src)
                else:
                    self.conv_layer(l, src)
                if not self.skip_ffn:
                    self.ffn_layer(l)
            if self.final:
                src = (self.x_in, [self.xin_b] * NT) if first else (self.xs, self.xs_b)
                self.final_norm(src)
            fw.barrier()
            fw.finish()
        return nc

    def tile_ap(self, dram, i):
        return dram.rearrange("(c p) t -> p c t", p=128)[:, :, i * T:(i + 1) * T]

    def ones_bf(self):
        return self.bc.t[:, BC["ones"]:BC["ones"] + 128]

    def ident_bf(self):
        return self.bc.t[:, BC["ident"]:BC["ident"] + 128]

    def smc(self, name, j=0, n=1):
        o = SM[name] + j
        return self.sm.t[:, o:o + n]

    def rmsnorm(self, x, h, sq, rstd, ps, gidx, sqc=8):
        fw = self.fw
        for q in range(8 // sqc):
            fw.op("act", lambda e: e.activation(sq.t[:, 0:sqc, :], x.t[:, q * sqc:(q + 1) * sqc, :], AF.Square), reads=[x.b], writes=[sq.b])
            for kk in range(sqc):
                k = q * sqc + kk
                fw.op("pe", lambda e: e.matmul(ps.t[:], self.ones_bf(), sq.t[:, kk, :], start=(k == 0), stop=(k == 7)),
                      reads=[sq.b, self.bc.b], writes=[ps.b])
        fw.op("act", lambda e: e.activation(rstd.t[:], ps.t[:], AF.Sqrt, bias=self.smc("eps6"), scale=1.0),
              reads=[ps.b, self.sm.b], writes=[rstd.b])
        fw.op("dve", lambda e: e.reciprocal(rstd.t[:], rstd.t[:]), reads=[rstd.b], writes=[rstd.b])
        hap, hb = h
        for k in range(8):
            fw.op("dve", lambda e, k=k: e.scalar_tensor_tensor(hap(k), x.t[:, k, :], self.smc("normg", gidx * 8 + k),
                                                             rstd.t[:], ALU.mult, ALU.mult),
                  reads=[x.b, rstd.b, self.sm.b], writes=[hb])

    def ffn_layer(self, l):
        fw = self.fw
        ntg = TG // T
        ngrp = S // TG
        with self.phase("ffn%d" % l) as ph:
            hT = ph.sb("hT", [128, 8, TG], BF16)
            hT_b = [Buf("hT%d" % i) for i in range(ntg)]
            act = ph.sb("act", [128, HC, TG], BF16)
            act_b = [Buf("act%d" % i) for i in range(ntg)]
            xin = [ph.sb("xin%d" % i, [128, 8, T], F32) for i in range(2)]
            sq = ph.sb("sq", [128, 4, T], BF16)
            rstd = ph.sb("rstd", [128, T], F32)
            wgu = [ph.sb("wgu%d" % i, [128, 2, 8, 128], BF16) for i in range(3)]
            wd = [ph.sb("wd%d" % i, [128, HC, 128], BF16) for i in range(2)]
            sg = [ph.sb("sg%d" % i, [128, T], F32) for i in range(2)]
            xc = [ph.sb("xc%d" % i, [128, T], F32) for i in range(2)]
            ps = self.ps
            ncnt = [0]

            def load_wgu(hc):
                w = wgu[hc % 3]
                fw.dma("pool", w.t[:].rearrange("p a k c -> p (a k c)"), self.d_wgu[l, hc], writes=[w.b], sem_buf=w.b)

            def load_wd(m):
                w = wd[m % 2]
                fw.dma("pool", w.t[:].rearrange("p k c -> p (k c)"), self.d_wd[l, m], writes=[w.b], sem_buf=w.b)

            def norm_tile(gi, tl):
                ti = gi * ntg + tl
                x = xin[ncnt[0] % 2]
                ncnt[0] += 1
                fw.dma("sp", x.t[:], self.tile_ap(self.xs, ti), reads=[self.xs_b[ti]], writes=[x.b], sem_buf=x.b)
                self.rmsnorm(x, (lambda k, tl=tl: hT.t[:, k, tl * T:(tl + 1) * T], hT_b[tl]), sq, rstd, ps[6], 4 + l, sqc=4)

            load_wgu(0)
            load_wgu(1)
            for tl in range(ntg):
                norm_tile(0, tl)
            for gi in range(ngrp):
                if gi > 0:
                    load_wgu(0)
                    load_wgu(1)
                it = 0
                for hc in range(HC):
                    if hc + 2 < HC:
                        load_wgu(hc + 2)
                    elif hc + 2 == HC:
                        load_wd(0)
                    w = wgu[hc % 3]
                    for tl in range(ntg):
                        pg = ps[it % 2]
                        pu = ps[2 + it % 2]
                        s_ = sg[it % 2]
                        it += 1
                        for a_, pp in ((0, pg), (1, pu)):
                            for k in range(8):
                                fw.op("pe", lambda e: e.matmul(
                                    pp.t[:], w.t[:, a_, k, :], hT.t[:, k, tl * T:(tl + 1) * T], start=(k == 0), stop=(k == 7)),
                                    reads=[w.b, hT_b[tl]], writes=[pp.b])
                        fw.op("act", lambda e: e.activation(s_.t[:], pg.t[:], AF.Silu), reads=[pg.b], writes=[s_.b])
                        fw.op("dve", lambda e: e.tensor_tensor(
                            act.t[:, hc, tl * T:(tl + 1) * T], pu.t[:], s_.t[:], ALU.mult),
                            reads=[pu.b, s_.b], writes=[act_b[tl]])
                it = 0
                for m in range(8):
                    if m + 1 < 8:
                        load_wd(m + 1)
                    w = wd[m % 2]
                    for tl in range(ntg):
                        ti = gi * ntg + tl
                        po = ps[4 + it % 2]
                        x = xc[it % 2]
                        it += 1
                        cap = self.xs[m * 128:(m + 1) * 128, ti * T:(ti + 1) * T]
                        fw.dma("sp", x.t[:], cap, reads=[self.xs_b[ti]], writes=[x.b], sem_buf=x.b)
                        for k in range(HC):
                            fw.op("pe", lambda e: e.matmul(
                                po.t[:], w.t[:, k, :], act.t[:, k, tl * T:(tl + 1) * T], start=(k == 0), stop=(k == HC - 1)),
                                reads=[w.b, act_b[tl]], writes=[po.b])
                        fw.op("dve", lambda e: e.tensor_tensor(x.t[:], po.t[:], x.t[:], ALU.add),
                              reads=[po.b, x.b], writes=[x.b])
                        fw.dma("sp", cap, x.t[:], reads=[x.b], writes=[self.xs_b[ti]], sem_buf=x.b)
                    if gi + 1 < ngrp and m % 2 == 1 and m // 2 < ntg:
                        norm_tile(gi + 1, m // 2)

    def final_norm(self, src):
        fw = self.fw
        sdram, sb = src
        with self.phase("fin") as ph:
            xin = [ph.sb("xin%d" % i, [128, 8, T], F32) for i in range(2)]
            yo = [ph.sb("yo%d" % i, [128, 8, T], F32) for i in range(2)]
            sq = ph.sb("sq", [128, 8, T], BF16)
            rstd = ph.sb("rstd", [128, T], F32)
            for ti in range(NT):
                x = xin[ti % 2]
                y = yo[ti % 2]
                fw.dma("sp", x.t[:], self.tile_ap(sdram, ti), reads=[sb[ti]], writes=[x.b], sem_buf=x.b)
                if self.final == "copy":
                    fw.dma("sp", self.tile_ap(self.y_out, ti), x.t[:], reads=[x.b], writes=[self.y_b[ti]], sem_buf=x.b)
                    continue
                self.rmsnorm(x, (lambda k, y=y: y.t[:, k, :], y.b), sq, rstd, self.ps[6], 8)
                fw.dma("sp", self.tile_ap(self.y_out, ti), y.t[:], reads=[y.b], writes=[self.y_b[ti]], sem_buf=y.b)

    def conv_layer(self, l, src):
        fw = self.fw
        i2 = l // 2
        sdram, sbufs = src
        ps = self.ps
        with self.phase("conv%d" % l) as ph:
            w1 = ph.sb("w1", [128, 8, 2048], BF16)
            w2 = ph.sb("w2", [128, 8, 1024], BF16)
            dg = ph.sb("dg", [128, 31 * 8, 128], BF16)
            dg_b = [Buf("dg0"), Buf("dg1")]
            xin = [ph.sb("xin0", [128, 8, T], F32)] * 2
            h = ph.sb("h", [128, 8, T], BF16)
            sq = h
            rstd = ph.sb("rstd", [128, T], F32)
            z = [ph.sb("z%d" % i, [128, 8, 30 + T], BF16) for i in range(2)]
            sg = [ph.sb("sg%d" % i, [128, T], F32) for i in range(2)]
            y = ph.sb("y", [128, 8, T], F32)
            ybf = h
            mean = ph.sb("mean", [128, T], F32)
            msq = ph.sb("msq", [128, T], F32)
            lrs = ph.sb("lrs", [128, T], F32)
            yn = sg
            s_bf = ph.sb("s_bf", [128, 8, T], BF16)
            ysq = s_bf
            for q in range(4):
                fw.dma("pool", w1.t[:, 2 * q:2 * q + 2, :].rearrange("p k c -> p (k c)"),
                       self.d_cw1[i2, :, 2 * q * 2048:(2 * q + 2) * 2048], writes=[w1.b], sem_buf=w1.b)
            for q in range(2):
                fw.dma("pool", w2.t[:, 4 * q:4 * q + 4, :].rearrange("p k c -> p (k c)"),
                       self.d_cw2[i2, :, 4 * q * 1024:(4 * q + 4) * 1024], writes=[w2.b], sem_buf=w2.b)
            for j in range(31 * 8):
                if j % 2 == 0:
                    fw.op("dve", lambda e, j=j: e.tensor_scalar(dg.t[:, j, :], self.identf.t[:], self.smc("cdw_%d" % i2, j), None, ALU.mult),
                          reads=[self.identf.b, self.sm.b], writes=[dg_b[0]])
                else:
                    fw.op("act", lambda e, j=j: e.activation(dg.t[:, j, :], self.identf.t[:], AF.Identity, scale=self.smc("cdw_%d" % i2, j)),
                          reads=[self.identf.b, self.sm.b], writes=[dg_b[1]])
            fw.op("pool", lambda e: e.memset(z[0].t[:, :, 0:30], 0.0), writes=[z[0].b])
            for ti in range(NT):
                x = xin[ti % 2]
                zc = z[ti % 2]
                zn = z[(ti + 1) % 2]
                if ti == 0:
                    fw.dma("sp", x.t[:], self.tile_ap(sdram, ti), reads=[sbufs[ti]], writes=[x.b], sem_buf=x.b)
                self.rmsnorm(x, (lambda k: h.t[:, k, :], h.b), sq, rstd, ps[6], l)
                if ti + 1 < NT:
                    fw.dma("sp", x.t[:], self.tile_ap(sdram, ti + 1), reads=[sbufs[ti + 1]], writes=[x.b], sem_buf=x.b)
                for m in range(8):
                    pa = ps[m % 2]
                    pg = ps[2 + m % 2]
                    s = sg[m % 2]
                    for a, pp in ((0, pa), (1, pg)):
                        for k in range(8):
                            fw.op("pe", lambda e, a=a, k=k, pp=pp, m=m: e.matmul(
                                pp.t[:], w1.t[:, k, a * 1024 + m * 128: a * 1024 + (m + 1) * 128], h.t[:, k, :],
                                start=(k == 0), stop=(k == 7)), reads=[w1.b, h.b], writes=[pp.b])
                    fw.op("act", lambda e, pg=pg, s=s, m=m: e.activation(s.t[:], pg.t[:], AF.Sigmoid, bias=self.smc("cb1_%d" % i2, 8 + m)),
                          reads=[pg.b, self.sm.b], writes=[s.b])
                    fw.op("dve", lambda e, pa=pa, s=s, m=m, zc=zc: e.scalar_tensor_tensor(
                        zc.t[:, m, 30:30 + T], pa.t[:], self.smc("cb1_%d" % i2, m), s.t[:], ALU.add, ALU.mult),
                        reads=[pa.b, s.b, self.sm.b], writes=[zc.b])
                if ti + 1 < NT:
                    fw.op("pool", lambda e, zc=zc, zn=zn: e.tensor_copy(zn.t[:, :, 0:30], zc.t[:, :, T:T + 30]),
                          reads=[zc.b], writes=[zn.b])
                for m in range(8):
                    pc = ps[4 + m % 2]
                    for k in range(31):
                        fw.op("pe", lambda e, k=k, m=m, pc=pc, zc=zc: e.matmul(
                            pc.t[:], dg.t[:, k * 8 + m, :], zc.t[:, m, k:k + T], start=(k == 0), stop=(k == 30)),
                            reads=[dg_b[0], dg_b[1], zc.b], writes=[pc.b])
                    fw.op("act", lambda e, m=m, pc=pc: e.activation(y.t[:, m, :], pc.t[:], AF.Identity, bias=self.smc("cbdw_%d" % i2, m)),
                          reads=[pc.b, self.sm.b], writes=[y.b])
                fw.op("dve", lambda e: e.tensor_copy(ybf.t[:], y.t[:]), reads=[y.b], writes=[ybf.b])
                fw.op("act", lambda e: e.activation(ysq.t[:], y.t[:], AF.Square), reads=[y.b], writes=[ysq.b])
                p1, p2 = ps[6], ps[7]
                for k in range(8):
                    fw.op("pe", lambda e, k=k: e.matmul(p1.t[:], self.ones_bf(), ybf.t[:, k, :], start=(k == 0), stop=(k == 7)),
                          reads=[ybf.b, self.bc.b], writes=[p1.b])
                for k in range(8):
                    fw.op("pe", lambda e, k=k: e.matmul(p2.t[:], self.ones_bf(), ysq.t[:, k, :], start=(k == 0), stop=(k == 7)),
                          reads=[ysq.b, self.bc.b], writes=[p2.b])
                fw.op("dve", lambda e: e.tensor_scalar(mean.t[:], p1.t[:], 1.0 / 1024, None, ALU.mult), reads=[p1.b], writes=[mean.b])
                fw.op("dve", lambda e: e.tensor_tensor(msq.t[:], mean.t[:], mean.t[:], ALU.mult), reads=[mean.b], writes=[msq.b])
                fw.op("dve", lambda e: e.scalar_tensor_tensor(lrs.t[:], p2.t[:], 1.0 / 1024, msq.t[:], ALU.mult, ALU.subtract),
                      reads=[p2.b, msq.b], writes=[lrs.b])
                fw.op("act", lambda e: e.activation(lrs.t[:], lrs.t[:], AF.Sqrt, bias=self.smc("eps5"), scale=1.0),
                      reads=[lrs.b, self.sm.b], writes=[lrs.b])
                fw.op("dve", lambda e: e.reciprocal(lrs.t[:], lrs.t[:]), reads=[lrs.b], writes=[lrs.b])
                for m in range(8):
                    yy = yn[m % 2]
                    fw.op("dve", lambda e, m=m, yy=yy: e.tensor_tensor(yy.t[:], y.t[:, m, :], mean.t[:], ALU.subtract),
                          reads=[y.b, mean.b], writes=[yy.b])
                    fw.op("dve", lambda e, yy=yy: e.tensor_tensor(yy.t[:], yy.t[:], lrs.t[:], ALU.mult),
                          reads=[yy.b, lrs.b], writes=[yy.b])
                    fw.op("act", lambda e, m=m, yy=yy: e.activation(s_bf.t[:, m, :], yy.t[:], AF.Silu,
                                                                    bias=self.smc("clnb_%d" % i2, m), scale=self.smc("clng_%d" % i2, m)),
                          reads=[yy.b, self.sm.b], writes=[s_bf.b])
                for m in range(8):
                    po = ps[m % 2]
                    for k in range(8):
                        fw.op("pe", lambda e, k=k, m=m, po=po: e.matmul(po.t[:], w2.t[:, k, m * 128:(m + 1) * 128], s_bf.t[:, k, :],
                                                                        start=(k == 0), stop=(k == 7)),
                              reads=[w2.b, s_bf.b], writes=[po.b])
                    xq = sg[m % 2]
                    src_c = sdram[m * 128:(m + 1) * 128, ti * T:(ti + 1) * T]
                    dst_c = self.xs[m * 128:(m + 1) * 128, ti * T:(ti + 1) * T]
                    fw.dma("sp", xq.t[:], src_c, reads=[sbufs[ti]], writes=[xq.b], sem_buf=xq.b)
                    fw.op("dve", lambda e, m=m, po=po, xq=xq: e.scalar_tensor_tensor(
                        xq.t[:], po.t[:], self.smc("cb2_%d" % i2, m), xq.t[:], ALU.add, ALU.add),
                        reads=[po.b, xq.b, self.sm.b], writes=[xq.b])
                    fw.dma("sp", dst_c, xq.t[:], reads=[xq.b], writes=[self.xs_b[ti]], sem_buf=xq.b)

    def even_layer(self, l, src):
        fw = self.fw
        import os
        STAGE = int(os.environ.get('EVEN_STAGE', '9'))
        SUB = os.environ.get('EVEN_SUB', 'NQMKGVP')
        BR = os.environ.get('EVEN_BR', 'csw')
        HEADS = [int(c) for c in os.environ.get('EVEN_HEADS', '01234567')]
        i2 = l // 2
        sdram, sbufs = src
        ps = self.ps
        bc = self.bc
        with self.phase("nsa%d" % l) as ph:
            win = ph.sb("win", [128, 8, 1816], BF16)
            wo = ph.sb("wo", [128, 8, 1024], BF16)
            w1c = ph.sb("w1c", [128, 2, 2, 32, 128], BF16)
            post = ph.sb("post", [128, 2, 32, 2], BF16)
            w2c = ph.sb("w2c", [128, 2, 64], BF16)
            pw = ph.sb("pw", [128, 4, 128], BF16)
            KS = [ph.sb("ks%d" % g, [128, S], BF16) for g in range(2)]
            KW = [ph.sb("kw%d" % g, [128, 1024], BF16) for g in range(2)]
            VS = ph.sb("vs", [128, 2, 32, 128], BF16)
            VW = ph.sb("vw", [128, 2, 8, 128], BF16)
            KCT = ph.sb("kct", [128, 16 + T], BF16)
            VCT = ph.sb("vct", [128, 16 + T], BF16)
            KC = [ph.sb("kc%d" % g, [128, 256], BF16) for g in range(2)]
            VC = ph.sb("vc", [128, 2, 2, 128], BF16)
            Qs = [ph.sb("q%d" % h, [128, T], BF16) for h in range(8)]
            x = ph.sb("x", [128, 8, T], F32)
            h_ = ph.sb("h", [128, 8, T], BF16)
            sgT = ph.sb("sgT", [128, T], BF16)
            pa = [ph.sb("pa%d" % i, [128, 16 + T], F32) for i in range(3)]
            rstd = TB(pa[2].t[:, 0:T], pa[2].b)
            halo = ph.sb("halo", [128, 4, 16], F32)
            dT = ph.sb("dT", [128, 4, T], BF16)
            yp = ph.sb("yp", [128, 4, T], BF16)
            oT = ph.sb("oT", [128, 4, T], BF16)
            E = [ph.sb("E%d" % i, [128, T], BF16) for i in range(3)]
            zc = ph.sb("zc", [128, T], F32)
            gz = ph.sb("gz", [128, T], F32)
            acc2 = [ph.sb("acc%d" % i, [128, T], F32) for i in range(2)]
            zc_b = [Buf("zc0"), Buf("zc1")]
            gz_b = [Buf("gz0"), Buf("gz1")]
            acc_b = [Buf("acc%d" % i) for i in range(4)]
            impa = ph.sb("impa", [128, 4, 64], F32)
            impf = ph.sb("impf", [128, 64], F32)
            top8 = ph.sb("top8", [128, 8], F32)
            selb = [ph.sb("selb0", [128, 128], F32)] * 2
            rz4 = ph.sb("rz4", [128, 4], F32)
            posb = ph.sb("posb", [128, 2], F32)
            gus = [ph.sb("gu%d" % i, [128, 32], F32) for i in range(4)]
            gvs = [ph.sb("gv%d" % i, [128, 32], F32) for i in range(4)]
            gss = [ph.sb("gs%d" % i, [128, 32], F32) for i in range(4)]
            Gs = [ph.sb("G%d" % i, [128, 128], BF16) for i in range(4)]
            gu = gus[0]
            for q in range(4):
                fw.dma("pool", win.t[:, 2 * q:2 * q + 2, :].rearrange("p k c -> p (k c)"),
                       self.d_win[i2, :, 2 * q * 1816:(2 * q + 2) * 1816], writes=[win.b], sem_buf=win.b)
            for q in range(2):
                fw.dma("pool", wo.t[:, 4 * q:4 * q + 4, :].rearrange("p k c -> p (k c)"),
                       self.d_wo[i2, :, 4 * q * 1024:(4 * q + 4) * 1024], writes=[wo.b], sem_buf=wo.b)
            for g_ in range(2):
                for s_ in range(2):
                    for q in range(2):
                        o_ = ((g_ * 2 + s_) * 32 + 16 * q) * 128
                        fw.dma("pool", w1c.t[:, g_, s_, 16 * q:16 * q + 16, :].rearrange("p l c -> p (l c)"),
                               self.d_w1c[i2, :, o_:o_ + 2048], writes=[w1c.b], sem_buf=w1c.b)
            fw.dma("pool", post.t[:].rearrange("p s l r -> p (s l r)"), self.d_post[i2], writes=[post.b], sem_buf=post.b)
            fw.dma("pool", w2c.t[:].rearrange("p s d -> p (s d)"), self.d_w2c[i2], writes=[w2c.b], sem_buf=w2c.b)
            fw.dma("pool", pw.t[:].rearrange("p m e -> p (m e)"), self.d_pw[i2], writes=[pw.b], sem_buf=pw.b)
            for g in range(2):
                fw.dma("pool", KS[g].t[64:128, :], self.d_ex[:, :], writes=[KS[g].b], sem_buf=KS[g].b)
                fw.dma("pool", KW[g].t[64:128, :], self.d_e0[:, 0:1024], writes=[KW[g].b], sem_buf=KW[g].b)
                fw.dma("pool", KC[g].t[64:128, :], self.d_e0[:, 0:256], writes=[KC[g].b], sem_buf=KC[g].b)
                fw.op("dve", lambda e, g=g: e.memset(KC[g].t[0:64, :], 0.0), writes=[KC[g].b])
            fw.op("dve", lambda e: e.memset(VS.t[:, :, :, 64:128], 1.0), writes=[VS.b])
            fw.op("dve", lambda e: e.memset(VW.t[:, :, :, 64:128], 1.0), writes=[VW.b])
            fw.op("dve", lambda e: e.memset(VC.t[:, :, :, 0:64], 0.0), writes=[VC.b])
            fw.op("dve", lambda e: e.memset(VC.t[:, :, :, 64:128], 1.0), writes=[VC.b])
            fw.op("dve", lambda e: e.memset(KCT.t[:, 0:16], 0.0), writes=[KCT.b])
            fw.op("dve", lambda e: e.memset(VCT.t[:, 0:16], 0.0), writes=[VCT.b])
            fw.op("dve", lambda e: e.memset(halo.t[:], 0.0), writes=[halo.b])
            fw.op("dve", lambda e: e.memset(oT.t[:], 0.0), writes=[oT.b])
            for G_ in Gs:
                fw.op("dve", lambda e: e.memset(G_.t[:], 0.0), writes=[G_.b])
            fw.op("dve", lambda e: e.memset(selb[0].t[:], 0.0), writes=[selb[0].b])
            for s_ in range(2):
                pp = ps[6 + s_]
                for ll in range(32):
                    fw.op("pe", lambda e, s_=s_, ll=ll, pp=pp: e.matmul(pp.t[:, 0:2], w1c.t[0:64, 0, s_, ll, :], post.t[0:64, s_, ll, :],
                                                                      start=(ll == 0), stop=(ll == 31)),
                          reads=[w1c.b, post.b], writes=[pp.b])
                fw.op("dve", lambda e, s_=s_, pp=pp: e.tensor_copy(posb.t[:, s_:s_ + 1], pp.t[:, 0:1]), reads=[pp.b], writes=[posb.b])

            fw.barrier()
            gen_i = [0]

            def gbank():
                gen_i[0] += 1
                return ps[6 + gen_i[0] % 2]

            ecnt = [0]
            import collections
            pend = collections.deque()
            scnt = [0]
            ocnt = [0]

            fw.op("dve", lambda e: e.memset(yp.t[:], 0.0), writes=[yp.b])
            for i in range(NT):
                c0 = i * T
                fw.mute = False
                if i == 0:
                    fw.dma("sp", x.t[:], self.tile_ap(sdram, i), reads=[sbufs[i]], writes=[x.b], sem_buf=x.b)
                fw.mute = STAGE < 1 or 'N' not in SUB
                self.rmsnorm(x, (lambda k: h_.t[:, k, :], h_.b), h_, rstd, ps[5], l)
                fw.mute = False
                if i + 1 < NT:
                    fw.dma("sp", x.t[:], self.tile_ap(sdram, i + 1), reads=[sbufs[i + 1]], writes=[x.b], sem_buf=x.b)

                def proj(col0, ncol):
                    pp = gbank()
                    for k in range(8):
                        fw.op("pe", lambda e, k=k, pp=pp: e.matmul(pp.t[0:ncol, :], win.t[:, k, col0:col0 + ncol], h_.t[:, k, :],
                                                                  start=(k == 0), stop=(k == 7)),
                              reads=[win.b, h_.b], writes=[pp.b])
                    return pp
                fw.mute = STAGE < 1 or 'Q' not in SUB
                for m in range(4):
                    pp = proj(m * 128, 128)
                    ha, hb2 = 2 * m, 2 * m + 1
                    fw.op("act", lambda e, pp=pp, ha=ha: e.activation(Qs[ha].t[0:64, :], pp.t[0:64, :], AF.Identity, scale=0.125),
                          reads=[pp.b], writes=[Qs[ha].b])
                    fw.op("dve", lambda e, pp=pp, hb2=hb2: e.tensor_scalar(Qs[hb2].t[0:64, :], pp.t[64:128, :], 0.125, None, ALU.mult),
                          reads=[pp.b], writes=[Qs[hb2].b])
                fw.mute = STAGE < 1 or 'M' not in SUB
                for hh in range(8):
                    fw.op("dve", lambda e, hh=hh: e.tensor_scalar(Qs[hh].t[64:128, :], self.smc("arow", 0, 512)[64:128, :], -SLOPES[hh], None, ALU.mult),
                          reads=[self.sm.b], writes=[Qs[hh].b])
                fw.mute = STAGE < 1 or 'K' not in SUB
                pp = proj(512, 128)
                fw.op("act", lambda e, pp=pp: e.activation(KCT.t[:, 16:16 + T], pp.t[:], AF.Identity), reads=[pp.b], writes=[KCT.b])
                pp = proj(640, 128)
                fw.op("dve", lambda e, pp=pp: e.tensor_copy(VCT.t[:, 16:16 + T], pp.t[:]), reads=[pp.b], writes=[VCT.b])
                pp = proj(768, 128)
                fw.op("act", lambda e, pp=pp: e.activation(KS[0].t[0:64, c0:c0 + T], pp.t[0:64, :], AF.Identity), reads=[pp.b], writes=[KS[0].b])
                fw.op("dve", lambda e, pp=pp: e.tensor_copy(KS[1].t[0:64, c0:c0 + T], pp.t[64:128, :]), reads=[pp.b], writes=[KS[1].b])
                pp = proj(896, 128)
                r0 = (i % 2) * T
                fw.op("act", lambda e, pp=pp: e.activation(KW[0].t[0:64, r0:r0 + T], pp.t[0:64, :], AF.Identity), reads=[pp.b], writes=[KW[0].b])
                fw.op("dve", lambda e, pp=pp: e.tensor_copy(KW[1].t[0:64, r0:r0 + T], pp.t[64:128, :]), reads=[pp.b], writes=[KW[1].b])
                fw.mute = STAGE < 1 or 'G' not in SUB
                pp = proj(1024, 128)
                fw.op("act", lambda e, pp=pp: e.activation(sgT.t[:], pp.t[:], AF.Sigmoid), reads=[pp.b], writes=[sgT.b])
                fw.mute = STAGE < 1 or 'V' not in SUB
                for s_ in range(4):
                    cs = 4 * i + s_
                    for half in range(2):
                        pp = gbank()
                        for k in range(8):
                            fw.op("pe", lambda e, k=k, pp=pp, s_=s_, half=half: e.matmul(
                                pp.t[:, 0:128], h_.t[:, k, s_ * 128:(s_ + 1) * 128],
                                win.t[:, k, 1560 + half * 128:1560 + (half + 1) * 128], start=(k == 0), stop=(k == 7)),
                                reads=[win.b, h_.b], writes=[pp.b])
                        if half == 0:
                            fw.op("act", lambda e, pp=pp, cs=cs: e.activation(VS.t[:, :, cs, 0:64], pp.t[:, 0:128].rearrange("p (g d) -> p g d", g=2), AF.Identity),
                                  reads=[pp.b], writes=[VS.b])
                        else:
                            fw.op("dve", lambda e, pp=pp, cs=cs: e.tensor_copy(VW.t[:, :, cs % 8, 0:64], pp.t[:, 0:128].rearrange("p (g d) -> p g d", g=2)),
                                  reads=[pp.b], writes=[VW.b])
                fw.mute = STAGE < 1 or 'P' not in SUB
                for m in range(4):
                    w_ = 2 ** (m + 1)
                    pp = proj(1048 + m * 128, 128)
                    A_, B_, C_ = pa
                    fw.op("act", lambda e, pp=pp: e.activation(A_.t[:, 16:16 + T], pp.t[:], AF.Identity), reads=[pp.b], writes=[A_.b])
                    fw.op("dve", lambda e, m=m: e.tensor_copy(A_.t[:, 0:16], halo.t[:, m, :]), reads=[halo.b], writes=[A_.b])
                    fw.op("dve", lambda e, m=m: e.tensor_copy(halo.t[:, m, :], A_.t[:, T:T + 16]), reads=[A_.b], writes=[halo.b])
                    cur = A_
                    for st_ in range(m + 1):
                        sh = 2 ** st_
                        lo = 2 ** (st_ + 1) - 1
                        dst = B_ if st_ % 2 == 0 else C_
                        fw.op("dve", lambda e, cur=cur, dst=dst, sh=sh, lo=lo: e.tensor_tensor(
                            dst.t[:, lo:16 + T], cur.t[:, lo:16 + T], cur.t[:, lo - sh:16 + T - sh], ALU.add),
                            reads=[cur.b], writes=[dst.b])
                        cur = dst
                    fw.op("dve", lambda e, cur=cur, m=m, w_=w_: e.scalar_tensor_tensor(dT.t[:, m, :], cur.t[:, 16:16 + T], 1.0 / w_, A_.t[:, 16:16 + T],
                                                                                     ALU.mult, ALU.subtract),
                          reads=[cur.b, A_.b], writes=[dT.b])
                    if i == 0:
                        fw.op("dve", lambda e, cur=cur, m=m: e.tensor_tensor(gu.t[:, 0:16], cur.t[:, 16:32], self.smc("rc15", m * 16, 16), ALU.mult),
                              reads=[cur.b, self.sm.b], writes=[gu.b])
                        fw.op("dve", lambda e, m=m: e.tensor_tensor(dT.t[:, m, 0:16], gu.t[:, 0:16], A_.t[:, 16:32], ALU.subtract),
                              reads=[gu.b, A_.b], writes=[dT.b])
                    pq = gbank()
                    fw.op("pe", lambda e, pq=pq, m=m: e.matmul(pq.t[:], pw.t[:, m, :], dT.t[:, m, :], start=True, stop=True),
                          reads=[pw.b, dT.b], writes=[pq.b])
                    fw.op("act", lambda e, pq=pq, m=m: e.activation(yp.t[:, m, :], pq.t[:], AF.Identity, scale=self.smc("pscale_%d" % i2, m)),
                          reads=[pq.b, self.sm.b], writes=[yp.b])
                fw.mute = STAGE < 2
                ccn = (32 * i) // 128
                pn = (32 * i) % 128
                for s_ in range(2):
                    XT = KCT if s_ == 0 else VCT
                    for g in range(2):
                        gu, gv, gs, G = gus[s_ * 2 + g], gvs[s_ * 2 + g], gss[s_ * 2 + g], Gs[s_ * 2 + g]
                        pp = gbank()
                        for ll in range(32):
                            fw.op("pe", lambda e, ll=ll, pp=pp, g=g, s_=s_, XT=XT: e.matmul(
                                pp.t[:, 0:32], w1c.t[:, g, s_, ll, :], XT.t[:, ll:ll + 497:16],
                                start=(ll == 0), stop=(ll == 31)), reads=[w1c.b, XT.b], writes=[pp.b])
                        fw.op("act", lambda e, pp=pp, s_=s_: e.activation(gu.t[:], pp.t[:, 0:32], AF.Identity, bias=posb.t[:, s_:s_ + 1]),
                              reads=[pp.b, posb.b], writes=[gu.b])
                        fw.op("dve", lambda e: e.tensor_tensor(gv.t[:], gu.t[:], gu.t[:], ALU.mult), reads=[gu.b], writes=[gv.b])
                        fw.op("dve", lambda e: e.tensor_scalar(gv.t[:], gv.t[:], 0.044715, 1.0, ALU.mult, ALU.add), reads=[gv.b], writes=[gv.b])
                        fw.op("dve", lambda e: e.tensor_tensor(gv.t[:], gv.t[:], gu.t[:], ALU.mult), reads=[gv.b, gu.b], writes=[gv.b])
                        fw.op("act", lambda e: e.activation(gs.t[:], gv.t[:], AF.Sigmoid, scale=1.5957691216057308), reads=[gv.b], writes=[gs.b])
                        fw.op("dve", lambda e: e.tensor_tensor(G.t[:, 0:32], gu.t[:], gs.t[:], ALU.mult), reads=[gu.b, gs.b], writes=[G.b])
                        pq = gbank()
                        if s_ == 0:
                            fw.op("pe", lambda e, pq=pq: e.matmul(pq.t[:, 0:32], w2c.t[:].rearrange("p s d -> p (s d)"), G.t[:, 0:32], start=True, stop=True),
                                  reads=[w2c.b, G.b], writes=[pq.b])
                            fw.op("dve", lambda e, pq=pq, g=g: e.tensor_copy(KC[g].t[0:64, 32 * i:32 * i + 32], pq.t[0:64, 0:32]),
                                  reads=[pq.b], writes=[KC[g].b])
                        else:
                            fw.op("pe", lambda e, pq=pq: e.matmul(pq.t[:, 0:64], G.t[:], w2c.t[:, 1, :], start=True, stop=True),
                                  reads=[w2c.b, G.b], writes=[pq.b])
                            fw.op("dve", lambda e, pq=pq, g=g: e.tensor_copy(VC.t[pn:pn + 32, g, ccn, 0:64], pq.t[0:32, 0:64]),
                                  reads=[pq.b], writes=[VC.b])
                            if i == 0:
                                fw.op("dve", lambda e, g=g: e.memset(VC.t[0:1, g, 0, :], 0.0), writes=[VC.b])
                fw.op("pool", lambda e: e.tensor_copy(KCT.t[:, 0:16], KCT.t[:, T:T + 16]), reads=[KCT.b], writes=[KCT.b])
                fw.op("pool", lambda e: e.tensor_copy(VCT.t[:, 0:16], VCT.t[:, T:T + 16]), reads=[VCT.b], writes=[VCT.b])

                fw.mute = STAGE < 3
                def pump(lag):
                    while pend and (pend[0][0] == "fin" or sum(1 for k_, _ in pend if k_ == "pv") > lag):
                        pend.popleft()[1]()

                def run_branch(hh, chunks, fin):
                    po = ps[3 + ocnt[0] % 2]
                    ocnt[0] += 1
                    n = len(chunks)
                    used = []
                    for idx, (kap, kb, vap, vb, bcol, mask) in enumerate(chunks):
                        sb_ = ps[scnt[0] % 3]
                        scnt[0] += 1
                        eb = E[ecnt[0] % len(E)]
                        ecnt[0] += 1
                        fw.op("pe", lambda e: e.matmul(sb_.t[:], kap, Qs[hh].t[:], start=True, stop=(mask is None)),
                              reads=[kb, Qs[hh].b], writes=[sb_.b])
                        if mask is not None:
                            fw.op("pe", lambda e: e.matmul(sb_.t[:], self.ident_bf(), mask, start=False, stop=True),
                                  reads=[bc.b], writes=[sb_.b])
                        fw.op("act", lambda e: e.activation(eb.t[:], sb_.t[:], AF.Exp, bias=self.sm.t[:, bcol:bcol + 1]),
                              reads=[sb_.b, self.sm.b], writes=[eb.b])
                        used.append(eb)

                        def pv(idx=idx, vap=vap, vb=vb, eb=eb):
                            fw.op("pe", lambda e: e.matmul(po.t[:], vap, eb.t[:], start=(idx == 0), stop=(idx == n - 1)),
                                  reads=[vb, eb.b], writes=[po.b])
                        pend.append(("pv", pv))
                        if idx == n - 1:
                            pend.append(("fin", lambda: fin(po, used)))
                        pump(2)

                def combine(hh, b, po):
                    pg = gbank()
                    hp = (hh % 2) * 64
                    hs = slice(hp, hp + 64)
                    zcb, gzb = zc_b[hh % 2], gz_b[hh % 2]
                    ac = acc2[(hh % 4) // 2]
                    acb = acc_b[hh % 4]
                    on = ("csw"[b] in BR)
                    if on:
                        fw.op("pe", lambda e: e.matmul(pg.t[:], bc.t[:, BC["eg"] + (hh * 3 + b) * 64: BC["eg"] + (hh * 3 + b) * 64 + 128], sgT.t[:],
                                                       start=True, stop=True), reads=[bc.b, sgT.b], writes=[pg.b])
                        fw.op("dve", lambda e: e.tensor_scalar(zc.t[hs, :], po.t[64:128, :], 1e-30, None, ALU.add), reads=[po.b], writes=[zcb])
                        fw.op("dve", lambda e: e.reciprocal(zc.t[hs, :], zc.t[hs, :]), reads=[zcb], writes=[zcb])
                        fw.op("dve", lambda e: e.tensor_tensor(gz.t[hs, :], pg.t[0:64, :], zc.t[hs, :], ALU.mult), reads=[pg.b, zcb], writes=[gzb])
                    if b == 0:
                        if on:
                            fw.op("dve", lambda e: e.tensor_tensor(ac.t[hs, :], po.t[0:64, :], gz.t[hs, :], ALU.mult), reads=[po.b, gzb], writes=[acb])
                        else:
                            fw.op("dve", lambda e: e.memset(ac.t[hs, :], 0.0), writes=[acb])
                    elif b == 1:
                        if on:
                            fw.op("dve", lambda e: e.tensor_tensor(gz.t[hs, :], po.t[0:64, :], gz.t[hs, :], ALU.mult), reads=[po.b, gzb], writes=[gzb])
                            fw.op("pool", lambda e: e.tensor_tensor(ac.t[hs, :], ac.t[hs, :], gz.t[hs, :], ALU.add), reads=[acb, gzb], writes=[acb])
                    else:
                        if on:
                            fw.op("dve", lambda e: e.tensor_tensor(gz.t[hs, :], po.t[0:64, :], gz.t[hs, :], ALU.mult), reads=[po.b, gzb], writes=[gzb])
                            fw.op("dve", lambda e: e.tensor_tensor(oT.t[hs, hh // 2, :], ac.t[hs, :], gz.t[hs, :], ALU.add),
                                  reads=[acb, gzb], writes=[oT.b])
                        else:
                            fw.op("dve", lambda e: e.tensor_copy(oT.t[hs, hh // 2, :], ac.t[hs, :]), reads=[acb], writes=[oT.b])

                for g in range(2):
                    fw.mute = STAGE < 3
                    ccs = [0] if i < 4 else [0, 1]
                    for hi_, hh in enumerate(range(4 * g, 4 * g + 4)):
                        chunks = []
                        for cc in ccs:
                            x0 = 512 * i - 2048 * cc
                            mask = bc.t[:, BC["g1"] + x0: BC["g1"] + x0 + 512] if x0 <= 1536 else None
                            chunks.append((KC[g].t[:, cc * 128:(cc + 1) * 128], KC[g].b, VC.t[:, g, cc, :], VC.b,
                                           SM["bias_cmp"] + (hh * 2 + cc) * 8 + i, mask))
                        def cmp_fin(po, used, hh=hh, hi_=hi_, ccs=ccs):
                            pu = ps[5]
                            for s_ in range(4):
                                for ci, cc in enumerate(ccs):
                                    fw.op("pe", lambda e: e.matmul(
                                        pu.t[:, s_ * 65:(s_ + 1) * 65], used[ci].t[:, s_ * 128:(s_ + 1) * 128],
                                        bc.t[:, BC["ova"] + cc * 65: BC["ova"] + (cc + 1) * 65], start=(ci == 0), stop=(ci == len(ccs) - 1)),
                                        reads=[used[ci].b, bc.b], writes=[pu.b])
                            combine(hh, 0, po)
                            zsl = pu.t[:, 0:260].rearrange("p (s c) -> p s c", c=65)[:, :, 64]
                            fw.op("dve", lambda e: e.tensor_scalar(rz4.t[:], zsl, 1e-30, None, ALU.add), reads=[pu.b], writes=[rz4.b])
                            fw.op("dve", lambda e: e.reciprocal(rz4.t[:], rz4.t[:]), reads=[rz4.b], writes=[rz4.b])
                            for s_ in range(4):
                                if hi_ == 0:
                                    fw.op("dve", lambda e: e.tensor_scalar(impa.t[:, s_, :], pu.t[:, s_ * 65:s_ * 65 + 64], rz4.t[:, s_:s_ + 1], None, ALU.mult),
                                          reads=[pu.b, rz4.b], writes=[impa.b])
                                else:
                                    fw.op("dve", lambda e: e.scalar_tensor_tensor(impa.t[:, s_, :], pu.t[:, s_ * 65:s_ * 65 + 64], rz4.t[:, s_:s_ + 1],
                                                                                 impa.t[:, s_, :], ALU.mult, ALU.add),
                                          reads=[pu.b, rz4.b, impa.b], writes=[impa.b])
                        run_branch(hh, chunks, cmp_fin)
                        pump(0)
                    pump(0)
                    fw.mute = STAGE < 4
                    pT = ps[5]
                    for s_ in range(4):
                        sub = 4 * i + s_
                        fo = SM["fbw"] + 62 - 2 * sub
                        sl_ = selb[s_ % 2]
                        fw.op("dve", lambda e, s_=s_, fo=fo: e.tensor_tensor(impf.t[:], impa.t[:, s_, :], self.sm.t[:, fo:fo + 64], ALU.add),
                              reads=[impa.b, self.sm.b], writes=[impf.b])
                        if sub >= 1:
                            fw.op("dve", lambda e: e.tensor_scalar(impf.t[:, 0:1], impf.t[:, 0:1], 1e4, None, ALU.add), reads=[impf.b], writes=[impf.b])
                        fw.op("dve", lambda e: e.max(top8.t[:], impf.t[:]), reads=[impf.b], writes=[top8.b])
                        fw.op("dve", lambda e, sl_=sl_: e.tensor_scalar(sl_.t[:, 0:64], impf.t[:], top8.t[:, 7:8], NEGB, ALU.is_lt, ALU.mult),
                              reads=[impf.b, top8.b], writes=[sl_.b])
                        fw.op("pe", lambda e, s_=s_, sl_=sl_: e.transpose(pT.t[:, s_ * 128:(s_ + 1) * 128], sl_.t[:], self.identf.t[:]),
                              reads=[sl_.b, self.identf.b], writes=[pT.b])
                    for hh in range(4 * g, 4 * g + 4):
                        fw.op("dve", lambda e, hh=hh: e.scalar_tensor_tensor(Qs[hh].t[64:128, :], self.smc("arow", 0, 512)[0:64, :], -SLOPES[hh], pT.t[0:64, :],
                                                                           ALU.mult, ALU.add),
                              reads=[self.sm.b, pT.b], writes=[Qs[hh].b])
                    fw.mute = STAGE < 5
                    for hh in range(4 * g, 4 * g + 4):
                        if hh not in HEADS:
                            continue
                        chunks = []
                        for c in range(0, 4 * i + 4):
                            dl = c - 4 * i
                            mask = None
                            if dl >= 0:
                                st0 = BC["caus"] + 384 - 128 * dl
                                mask = bc.t[:, st0:st0 + 512]
                            chunks.append((KS[g].t[:, c * 128:(c + 1) * 128], KS[g].b, VS.t[:, g, c, :], VS.b,
                                           SM["bias_sw"] + hh * 32 + dl + 28, mask))
                        run_branch(hh, chunks, lambda po, used, hh=hh: combine(hh, 1, po))
                        chunks = []
                        for c in range(max(0, 4 * i - 4), 4 * i + 4):
                            dl = c - 4 * i
                            if dl >= 0:
                                st0 = BC["caus"] + 384 - 128 * dl
                            else:
                                st0 = BC["far"] + 384 - 128 * (dl + 4)
                            mask = bc.t[:, st0:st0 + 512]
                            sl8 = c % 8
                            chunks.append((KW[g].t[:, sl8 * 128:(sl8 + 1) * 128], KW[g].b, VW.t[:, g, sl8, :], VW.b,
                                           SM["bias_sw"] + hh * 32 + dl + 28, mask))
                        run_branch(hh, chunks, lambda po, used, hh=hh: combine(hh, 2, po))
                pump(0)
                fw.mute = False
                for mo in range(8):
                    pp = gbank()
                    for k in range(8):
                        rhs = oT.t[:, k, :] if k < 4 else yp.t[:, k - 4, :]
                        rb = oT.b if k < 4 else yp.b
                        fw.op("pe", lambda e, k=k, pp=pp, rhs=rhs, mo=mo: e.matmul(pp.t[:], wo.t[:, k, mo * 128:(mo + 1) * 128], rhs,
                                                                                  start=(k == 0), stop=(k == 7)),
                              reads=[wo.b, rb], writes=[pp.b])
                    xq = pa[mo % 2]
                    src_c = sdram[mo * 128:(mo + 1) * 128, i * T:(i + 1) * T]
                    dst_c = self.xs[mo * 128:(mo + 1) * 128, i * T:(i + 1) * T]
                    fw.dma("sp", xq.t[:, 0:T], src_c, reads=[sbufs[i]], writes=[xq.b], sem_buf=xq.b)
                    fw.op("dve", lambda e, pp=pp, mo=mo: e.tensor_tensor(xq.t[:, 0:T], pp.t[:], xq.t[:, 0:T], ALU.add),
                          reads=[pp.b, xq.b], writes=[xq.b])
                    fw.dma("sp", dst_c, xq.t[:, 0:T], reads=[xq.b], writes=[self.xs_b[i]], sem_buf=xq.b)


def _prep_weights(inp):
    f = lambda a: np.ascontiguousarray(a, dtype=np.float32)
    out = {}
    sm, bc, ex, e0 = make_consts()

    def pc(v):
        return np.asarray(v, np.float32).reshape(-1, 128).T

    for l in range(4):
        sm[:, SM["normg"] + l * 8: SM["normg"] + (l + 1) * 8] = pc(inp["mix_norm"][l])
        sm[:, SM["normg"] + (4 + l) * 8: SM["normg"] + (5 + l) * 8] = pc(inp["ffn_norm"][l])
    sm[:, SM["normg"] + 64: SM["normg"] + 72] = pc(inp["final_norm"])
    for i in range(2):
        sm[:, SM["cb1_%d" % i]: SM["cb1_%d" % i] + 16] = pc(inp["conv_b_pw1"][i])
        sm[:, SM["cbdw_%d" % i]: SM["cbdw_%d" % i] + 8] = pc(inp["conv_b_dw"][i])
        sm[:, SM["clng_%d" % i]: SM["clng_%d" % i] + 8] = pc(inp["conv_ln_g"][i])
        sm[:, SM["clnb_%d" % i]: SM["clnb_%d" % i] + 8] = pc(inp["conv_ln_b"][i])
        sm[:, SM["cb2_%d" % i]: SM["cb2_%d" % i] + 8] = pc(inp["conv_b_pw2"][i])
        wdw = np.asarray(inp["conv_w_dw"][i], np.float32)
        sm[:, SM["cdw_%d" % i]: SM["cdw_%d" % i] + 248] = wdw.reshape(31, 8, 128).transpose(2, 0, 1).reshape(128, 248)
        sm[:, SM["pscale_%d" % i]: SM["pscale_%d" % i] + 4] = pc(inp["pool_scale"][i])
    out["smalls"] = f(sm)
    out["bconst"] = f(bc)
    out["exrows"] = f(ex)
    out["e0rows"] = f(e0)
    wg = np.asarray(inp["ffn_w_gate"], np.float32).reshape(4, 8, 128, HC, 128)
    wu = np.asarray(inp["ffn_w_up"], np.float32).reshape(4, 8, 128, HC, 128)
    wgu = np.stack([wg, wu], axis=0)
    out["wgu"] = f(wgu.transpose(1, 4, 3, 0, 2, 5).reshape(4, HC, 128, 2 * 8 * 128))
    wd = np.asarray(inp["ffn_w_down"], np.float32).reshape(4, HC, 128, 8, 128)
    out["wd"] = f(wd.transpose(0, 3, 2, 1, 4).reshape(4, 8, 128, HC * 128))
    perm = np.concatenate([np.arange(0, 512), np.arange(512, 640), np.arange(640, 768), np.arange(768, 896),
                           np.arange(1024, 1152), np.arange(1280, 1304), np.arange(1304, 1816),
                           np.arange(896, 1024), np.arange(1152, 1280)])
    win = np.asarray(inp["nsa_w_in"], np.float32)[:, :, perm].reshape(2, 8, 128, 1816)
    out["win"] = f(win.transpose(0, 2, 1, 3).reshape(2, 128, 8 * 1816))
    wo = np.asarray(inp["mix_w_out"], np.float32).reshape(2, 8, 128, 1024)
    out["wo"] = f(wo.transpose(0, 2, 1, 3).reshape(2, 128, 8 * 1024))
    w1c = np.stack([np.asarray(inp["cmp_k_w1"], np.float32), np.asarray(inp["cmp_v_w1"], np.float32)], axis=1)
    w1c = w1c.reshape(2, 2, 32, 64, 128).transpose(0, 3, 1, 2, 4)
    w1z = np.zeros((2, 128, 2, 2, 32, 128), np.float32)
    w1z[:, 0:64, 0] = w1c
    w1z[:, 64:128, 1] = w1c
    out["w1c"] = f(w1z.reshape(2, 128, 2 * 2 * 32 * 128))
    pos = np.stack([np.asarray(inp["cmp_pos_k"], np.float32), np.asarray(inp["cmp_pos_v"], np.float32)], axis=1)
    pos = pos.transpose(0, 3, 1, 2)
    out["post"] = f(np.repeat(np.concatenate([pos, pos], axis=1).reshape(2, 128, 64), 2, axis=2))
    w2c = np.stack([np.asarray(inp["cmp_k_w2"], np.float32), np.asarray(inp["cmp_v_w2"], np.float32)], axis=1)
    out["w2c"] = f(w2c.transpose(0, 2, 1, 3).reshape(2, 128, 128))
    pw = np.asarray(inp["pool_w"], np.float32)
    out["poolw"] = f(pw.transpose(0, 2, 1, 3).reshape(2, 128, 512))
    cw1 = np.asarray(inp["conv_w_pw1"], np.float32).reshape(2, 8, 128, 2048)
    out["cw1"] = f(cw1.transpose(0, 2, 1, 3).reshape(2, 128, 8 * 2048))
    cw2 = np.asarray(inp["conv_w_pw2"], np.float32).reshape(2, 8, 128, 1024)
    out["cw2"] = f(cw2.transpose(0, 2, 1, 3).reshape(2, 128, 8 * 1024))
    return out


_CACHE = {}


def run(inputs, layers=(0, 1, 2, 3), final=True, n_cores=8):
    key = (tuple(layers), final)
    if key not in _CACHE:
        pr = Prog(layers, final)
        _CACHE[key] = (pr.build(), set(pr.dins.keys()))
    nc, names = _CACHE[key]
    w = _prep_weights(inputs)
    x = np.asarray(inputs["x"], np.float32)
    in_maps = []
    for b in range(n_cores):
        m = {k: v for k, v in w.items() if k in names}
        m["xT"] = np.ascontiguousarray(x[b].T)
        in_maps.append(m)
    res = run_bass_kernel_spmd(nc, in_maps, core_ids=list(range(n_cores)))
    out = np.stack([np.ascontiguousarray(res.results[b]["yT"].T) for b in range(n_cores)], axis=0)
    return out.astype(np.float32)


def kernel(**inputs):
    return run(inputs)
```

```python
from contextlib import ExitStack, contextmanager
import numpy as np
import concourse.bass as bass
import concourse.mybir as mybir
from concourse.bass_utils import run_bass_kernel_spmd

F32 = mybir.dt.float32
BF16 = mybir.dt.bfloat16
AF = mybir.ActivationFunctionType
ALU = mybir.AluOpType

S = 4096
D = 1024
T = 512
NT = S // T
HID = 2816
HC = HID // 128
TG = 2048
NEGB = -30000.0
ENGS = ("pe", "act", "dve", "pool", "sp")
SLOPES = [2.0 ** (-(h + 1)) for h in range(8)]


class Buf:
    __slots__ = ("name", "writes", "reads", "dsem", "psum")

    def __init__(self, name, psum=False):
        self.name = name
        self.writes = {}
        self.reads = {}
        self.dsem = None
        self.psum = psum


class TB:
    __slots__ = ("t", "b")

    def __init__(self, t, b):
        self.t = t
        self.b = b


class _Rec:
    def __init__(self):
        self.call = None

    def __getattr__(self, name):
        def f(*a, **k):
            self.call = (name, a, k)
            return self
        return f


class FW:
    def __init__(self, nc, stack):
        self.nc = nc
        self.stack = stack
        self.ops = {e: [] for e in ENGS}
        self.sems = {}
        self.count = {}
        self.seen = {e: {} for e in ENGS}
        self.same_engine_sync = True
        self.mute = False
        self.ekey = {}
        self.epoch = 0
        for e in ENGS:
            self.ekey[e] = self._newsem("E_" + e)
        self.free_dsems = {'sw': [], 'hw': []}

    def _newsem(self, key):
        h = self.stack.enter_context(self.nc.semaphore(key))
        self.sems[key] = h
        self.count[key] = 0
        return key

    def _deps(self, eng, reads, writes):
        deps = {}
        own = self.ekey[eng]
        for b in reads:
            for k, v in b.writes.items():
                if deps.get(k, 0) < v:
                    deps[k] = v
            if b.psum:
                for k, v in b.reads.items():
                    if k != own and deps.get(k, 0) < v:
                        deps[k] = v
        for b in writes:
            for d in (b.writes, b.reads):
                for k, v in d.items():
                    if deps.get(k, 0) < v:
                        deps[k] = v
        out = []
        seen = self.seen[eng]
        for k, v in deps.items():
            if k == own and (eng == "pe" or not self.same_engine_sync):
                continue
            if seen.get(k, 0) >= v:
                continue
            seen[k] = v
            out.append((k, v))
        return out

    def _record(self, key, val, reads, writes):
        for b in reads:
            b.reads[key] = val
        for b in writes:
            b.reads = {}
            b.writes[key] = val

    def op(self, eng, fn, reads=(), writes=()):
        if self.mute:
            return
        waits = self._deps(eng, reads, writes)
        key = self.ekey[eng]
        self.count[key] += 1
        val = self.count[key]
        sems = self.sems
        rec = _Rec()
        fn(rec)
        cname, cargs, ckw = rec.call

        def emit(e, waits=waits, key=key, cname=cname, cargs=cargs, ckw=ckw):
            for k, v in waits:
                e.wait_ge(sems[k], v)
            getattr(e, cname)(*cargs, **ckw).then_inc(sems[key], 1)
        self.ops[eng].append(emit)
        self._record(key, val, reads, writes)

    def dma(self, eng, out_ap, in_ap, reads=(), writes=(), sem_buf=None):
        if self.mute:
            return
        waits = self._deps(eng, reads, writes)
        sb = sem_buf
        qk = "sw" if eng == "pool" else "hw"
        if sb.dsem is None:
            sb.dsem = {}
        if qk not in sb.dsem:
            if self.free_dsems[qk]:
                sb.dsem[qk] = self.free_dsems[qk].pop()
            else:
                sb.dsem[qk] = self._newsem("D%s%d" % (qk, len(self.sems)))
        key = sb.dsem[qk]
        self.count[key] += 16
        val = self.count[key]
        sems = self.sems

        def emit(e, waits=waits, key=key):
            for k, v in waits:
                e.wait_ge(sems[k], v)
            e.dma_start(out=out_ap, in_=in_ap).then_inc(sems[key], 16)
        self.ops[eng].append(emit)
        self._record(key, val, reads, writes)

    def barrier(self):
        allb = Buf("ALL")
        allb.writes = {k: v for k, v in self.count.items() if v > 0}
        sems = self.sems
        for eng in ENGS:
            waits = self._deps(eng, [allb], ())

            def emit(e, waits=waits):
                for k, v in waits:
                    e.wait_ge(sems[k], v)
            self.ops[eng].append(emit)

    def new_epoch(self):
        self.epoch += 1
        for e in ENGS:
            self.ekey[e] = self._newsem("E_%s_%d" % (e, self.epoch))

    def finish(self):
        nc = self.nc
        ops = self.ops
        with nc.Block() as block:
            @block.tensor
            def _(e):
                for f in ops["pe"]:
                    f(e)

            @block.scalar
            def _(e):
                for f in ops["act"]:
                    f(e)

            @block.vector
            def _(e):
                for f in ops["dve"]:
                    f(e)

            @block.gpsimd
            def _(e):
                for f in ops["pool"]:
                    f(e)

            @block.sync
            def _(e):
                for f in ops["sp"]:
                    f(e)


class Phase:
    def __init__(self, A, name):
        self.A = A
        self.name = name
        self.stack = ExitStack()
        self.bufs = []

    def sb(self, name, shape, dtype):
        t = self.stack.enter_context(self.A.nc.sbuf_tensor(self.name + "_" + name, shape, dtype))
        tb = TB(t, Buf(name))
        self.bufs.append(tb)
        return tb

    def close(self):
        self.A.fw.barrier()
        self.A.fw.new_epoch()
        for tb in self.bufs:
            if tb.b.dsem is not None:
                for qk, key in tb.b.dsem.items():
                    self.A.fw.free_dsems[qk].append(key)
                tb.b.dsem = None
        self.stack.close()


def _small_layout():
    off = {}
    c = 0

    def add(name, n):
        nonlocal c
        off[name] = c
        c += n
    add("normg", 9 * 8)
    for i in range(2):
        add("cb1_%d" % i, 16)
        add("cbdw_%d" % i, 8)
        add("clng_%d" % i, 8)
        add("clnb_%d" % i, 8)
        add("cb2_%d" % i, 8)
        add("cdw_%d" % i, 31 * 8)
        add("pscale_%d" % i, 4)
    add("bias_sw", 8 * 32)
    add("bias_cmp", 8 * 2 * 8)
    add("fbw", 126)
    add("rc15", 4 * 16)
    add("arow", 512)
    add("eps6", 1)
    add("eps5", 1)
    add("eps30", 1)
    off["_n"] = c
    return off


SM = _small_layout()


def _bf_consts():
    off = {}
    c = 0

    def add(name, n):
        nonlocal c
        off[name] = c
        c += n
    add("caus", 896)
    add("far", 896)
    add("g1", 2048)
    add("ident", 128)
    add("ones", 128)
    add("ova", 2 * 65)
    add("eg", 24 * 64 + 64)
    off["_n"] = c
    return off


BC = _bf_consts()


def make_consts():
    sm = np.zeros((128, SM["_n"]), np.float32)
    p = np.arange(128)[:, None].astype(np.float64)
    for h in range(8):
        for di, delta in enumerate(range(-28, 4)):
            sm[:, SM["bias_sw"] + h * 32 + di] = (SLOPES[h] * (128 * delta + p[:, 0] - 256))
        for cc in range(2):
            for i in range(8):
                sm[:, SM["bias_cmp"] + (h * 2 + cc) * 8 + i] = SLOPES[h] * (16 * (128 * cc + p[:, 0]) + 15 - 512 * i - 256)
    hb = (np.arange(128) >= 64).astype(np.int64)[:, None]
    x = (np.arange(126) - 62)[None, :]
    fb = np.zeros((128, 126), np.float32)
    fb[(x == hb) | (x == hb - 1)] = 1e4
    fb[np.broadcast_to(x > hb, fb.shape)] = -1e30
    sm[:, SM["fbw"]:SM["fbw"] + 126] = fb
    for m, w in enumerate((2, 4, 8, 16)):
        t = np.arange(16)
        sm[:, SM["rc15"] + m * 16: SM["rc15"] + (m + 1) * 16] = (1.0 / np.minimum(t + 1.0, float(w)))[None, :]
    sm[:, SM["arow"]:SM["arow"] + 512] = (np.arange(512) - 256)[None, :]
    sm[:, SM["eps6"]] = 1024.0 * 1e-6
    sm[:, SM["eps5"]] = 1e-5
    sm[:, SM["eps30"]] = 1e-30

    bc = np.zeros((128, BC["_n"]), np.float32)
    xx = (np.arange(896) - 384)[None, :]
    pp = np.arange(128)[:, None]
    bc[:, BC["caus"]:BC["caus"] + 896] = np.where(xx >= pp, 0.0, NEGB)
    bc[:, BC["far"]:BC["far"] + 896] = np.where(xx < pp, 0.0, NEGB)
    x2 = np.arange(2048)[None, :]
    g = np.where(x2 >= 16 * pp + 15, 0.0, NEGB)
    bc[:, BC["g1"]:BC["g1"] + 2048] = g
    bc[:, BC["ident"]:BC["ident"] + 128] = np.eye(128)
    bc[:, BC["ones"]:BC["ones"] + 128] = 1.0
    ova = np.zeros((256, 65), np.float32)
    for n1 in range(1, 256):
        n = n1 - 1
        c0 = n * 16
        for j in range(64):
            s0 = j * 64
            ov = min(c0 + 32, s0 + 64) - max(c0, s0)
            if ov > 0:
                ova[n1, j] = ov / 32.0
        ova[n1, 64] = 1.0
    bc[:, BC["ova"]:BC["ova"] + 130] = ova.reshape(2, 128, 65).transpose(1, 0, 2).reshape(128, 130)
    eg = np.zeros((128, 24, 64), np.float32)
    for r in range(24):
        eg[r, r, :] = 1.0
    bc[:, BC["eg"]:BC["eg"] + 24 * 64] = eg.reshape(128, -1)
    ex = np.zeros((64, S), np.float32)
    for j in range(64):
        ex[j, j * 64:(j + 1) * 64] = 1.0
    e0 = np.zeros((64, S), np.float32)
    e0[0, :] = 1.0
    return sm, bc, ex, e0


class Prog:
    def __init__(self, layers=(0, 1, 2, 3), final=True, skip_ffn=False):
        self.layers = tuple(layers)
        self.final = final
        self.skip_ffn = skip_ffn
        self.nc = bass.Bass("TRN2", target_bir_lowering=False)
        self.stack = ExitStack()
        self.dins = {}

    def dram_in(self, name, shape):
        if name not in self.dins:
            self.dins[name] = self.nc.dram_tensor(name, list(shape), F32, kind="ExternalInput").ap()
        return self.dins[name]

    @property
    def d_wgu(self): return self.dram_in("wgu", (4, HC, 128, 2 * 8 * 128))
    @property
    def d_wd(self): return self.dram_in("wd", (4, 8, 128, HC * 128))
    @property
    def d_win(self): return self.dram_in("win", (2, 128, 8 * 1816))
    @property
    def d_wo(self): return self.dram_in("wo", (2, 128, 8 * 1024))
    @property
    def d_w1c(self): return self.dram_in("w1c", (2, 128, 2 * 2 * 32 * 128))
    @property
    def d_post(self): return self.dram_in("post", (2, 128, 2 * 32 * 2))
    @property
    def d_w2c(self): return self.dram_in("w2c", (2, 128, 2 * 64))
    @property
    def d_pw(self): return self.dram_in("poolw", (2, 128, 4 * 128))
    @property
    def d_cw1(self): return self.dram_in("cw1", (2, 128, 8 * 2048))
    @property
    def d_cw2(self): return self.dram_in("cw2", (2, 128, 8 * 1024))
    @property
    def d_ex(self): return self.dram_in("exrows", (64, S))
    @property
    def d_e0(self): return self.dram_in("e0rows", (64, S))

    @contextmanager
    def phase(self, name):
        ph = Phase(self, name)
        try:
            yield ph
        finally:
            ph.close()

    def build(self):
        nc = self.nc
        st = self.stack
        with st:
            self.fw = FW(nc, st)
            fw = self.fw
            self.x_in = self.dram_in("xT", (D, S))
            self.y_out = nc.dram_tensor("yT", [D, S], F32, kind="ExternalOutput").ap()
            self.xs = nc.dram_tensor("xs", [D, S], F32).ap()
            self.d_sm = self.dram_in("smalls", (128, SM["_n"]))
            self.d_bc = self.dram_in("bconst", (128, BC["_n"]))
            self.xs_b = [Buf("xs%d" % i) for i in range(NT)]
            self.xin_b = Buf("x_in")
            self.y_b = [Buf("y%d" % i) for i in range(NT)]
            sm_t = st.enter_context(nc.sbuf_tensor("smalls_sb", [128, SM["_n"]], F32))
            self.sm = TB(sm_t, Buf("smalls"))
            bc_t = st.enter_context(nc.sbuf_tensor("bconst_sb", [128, BC["_n"]], BF16))
            self.bc = TB(bc_t, Buf("bconst"))
            idf_t = st.enter_context(nc.sbuf_tensor("identf", [128, 128], F32))
            self.identf = TB(idf_t, Buf("identf"))
            self.ps = []
            for i in range(8):
                t = st.enter_context(nc.psum_tensor("ps%d" % i, [128, 512], F32))
                self.ps.append(TB(t, Buf("ps%d" % i, psum=True)))
            fw.dma("sp", sm_t[:], self.d_sm[:], writes=[self.sm.b], sem_buf=self.sm.b)
            half = BC["_n"] // 2
            fw.dma("pool", bc_t[:, 0:half], self.d_bc[:, 0:half], writes=[self.bc.b], sem_buf=self.bc.b)
            fw.dma("pool", bc_t[:, half:], self.d_bc[:, half:], writes=[self.bc.b], sem_buf=self.bc.b)
            ng = sm_t[:, SM["normg"]:SM["normg"] + 72]
            fw.op("dve", lambda e: e.tensor_scalar(ng, ng, 32.0, None, ALU.mult), writes=[self.sm.b])
            fw.op("dve", lambda e: e.tensor_copy(idf_t[:], bc_t[:, BC["ident"]:BC["ident"] + 128]),
                  reads=[self.bc.b], writes=[self.identf.b])
            fw.barrier()

            first = True
            for l in self.layers:
                src = (self.x_in, [self.xin_b] * NT) if first else (self.xs, self.xs_b)
                first = False
                if l % 2 == 0:
                    self.even_layer(l, src)
                else:
                    self.conv_layer(l, src)
                if not self.skip_ffn:
                    self.ffn_layer(l)
            if self.final:
                src = (self.x_in, [self.xin_b] * NT) if first else (self.xs, self.xs_b)
                self.final_norm(src)
            fw.barrier()
            fw.finish()
        return nc

    def tile_ap(self, dram, i):
        return dram.rearrange("(c p) t -> p c t", p=128)[:, :, i * T:(i + 1) * T]

    def ones_bf(self):
        return self.bc.t[:, BC["ones"]:BC["ones"] + 128]

    def ident_bf(self):
        return self.bc.t[:, BC["ident"]:BC["ident"] + 128]

    def smc(self, name, j=0, n=1):
        o = SM[name] + j
        return self.sm.t[:, o:o + n]

    def rmsnorm(self, x, h, sq, rstd, ps, gidx, sqc=8):
        fw = self.fw
        for q in range(8 // sqc):
            fw.op("act", lambda e: e.activation(sq.t[:, 0:sqc, :], x.t[:, q * sqc:(q + 1) * sqc, :], AF.Square), reads=[x.b], writes=[sq.b])
            for kk in range(sqc):
                k = q * sqc + kk
                fw.op("pe", lambda e: e.matmul(ps.t[:], self.ones_bf(), sq.t[:, kk, :], start=(k == 0), stop=(k == 7)),
                      reads=[sq.b, self.bc.b], writes=[ps.b])
        fw.op("act", lambda e: e.activation(rstd.t[:], ps.t[:], AF.Sqrt, bias=self.smc("eps6"), scale=1.0),
              reads=[ps.b, self.sm.b], writes=[rstd.b])
        fw.op("dve", lambda e: e.reciprocal(rstd.t[:], rstd.t[:]), reads=[rstd.b], writes=[rstd.b])
        hap, hb = h
        for k in range(8):
            fw.op("dve", lambda e, k=k: e.scalar_tensor_tensor(hap(k), x.t[:, k, :], self.smc("normg", gidx * 8 + k),
                                                             rstd.t[:], ALU.mult, ALU.mult),
                  reads=[x.b, rstd.b, self.sm.b], writes=[hb])

    def ffn_layer(self, l):
        fw = self.fw
        ntg = TG // T
        ngrp = S // TG
        with self.phase("ffn%d" % l) as ph:
            hT = ph.sb("hT", [128, 8, TG], BF16)
            hT_b = [Buf("hT%d" % i) for i in range(ntg)]
            act = ph.sb("act", [128, HC, TG], BF16)
            act_b = [Buf("act%d" % i) for i in range(ntg)]
            xin = [ph.sb("xin%d" % i, [128, 8, T], F32) for i in range(2)]
            sq = ph.sb("sq", [128, 4, T], BF16)
            rstd = ph.sb("rstd", [128, T], F32)
            wgu = [ph.sb("wgu%d" % i, [128, 2, 8, 128], BF16) for i in range(3)]
            wd = [ph.sb("wd%d" % i, [128, HC, 128], BF16) for i in range(2)]
            sg = [ph.sb("sg%d" % i, [128, T], F32) for i in range(2)]
            xc = [ph.sb("xc%d" % i, [128, T], F32) for i in range(2)]
            ps = self.ps
            ncnt = [0]

            def load_wgu(hc):
                w = wgu[hc % 3]
                fw.dma("pool", w.t[:].rearrange("p a k c -> p (a k c)"), self.d_wgu[l, hc], writes=[w.b], sem_buf=w.b)

            def load_wd(m):
                w = wd[m % 2]
                fw.dma("pool", w.t[:].rearrange("p k c -> p (k c)"), self.d_wd[l, m], writes=[w.b], sem_buf=w.b)

            def norm_tile(gi, tl):
                ti = gi * ntg + tl
                x = xin[ncnt[0] % 2]
                ncnt[0] += 1
                fw.dma("sp", x.t[:], self.tile_ap(self.xs, ti), reads=[self.xs_b[ti]], writes=[x.b], sem_buf=x.b)
                self.rmsnorm(x, (lambda k, tl=tl: hT.t[:, k, tl * T:(tl + 1) * T], hT_b[tl]), sq, rstd, ps[6], 4 + l, sqc=4)

            load_wgu(0)
            load_wgu(1)
            for tl in range(ntg):
                norm_tile(0, tl)
            for gi in range(ngrp):
                if gi > 0:
                    load_wgu(0)
                    load_wgu(1)
                it = 0
                for hc in range(HC):
                    if hc + 2 < HC:
                        load_wgu(hc + 2)
                    elif hc + 2 == HC:
                        load_wd(0)
                    w = wgu[hc % 3]
                    for tl in range(ntg):
                        pg = ps[it % 2]
                        pu = ps[2 + it % 2]
                        s_ = sg[it % 2]
                        it += 1
                        for a_, pp in ((0, pg), (1, pu)):
                            for k in range(8):
                                fw.op("pe", lambda e: e.matmul(
                                    pp.t[:], w.t[:, a_, k, :], hT.t[:, k, tl * T:(tl + 1) * T], start=(k == 0), stop=(k == 7)),
                                    reads=[w.b, hT_b[tl]], writes=[pp.b])
                        fw.op("act", lambda e: e.activation(s_.t[:], pg.t[:], AF.Silu), reads=[pg.b], writes=[s_.b])
                        fw.op("dve", lambda e: e.tensor_tensor(
                            act.t[:, hc, tl * T:(tl + 1) * T], pu.t[:], s_.t[:], ALU.mult),
                            reads=[pu.b, s_.b], writes=[act_b[tl]])
                it = 0
                for m in range(8):
                    if m + 1 < 8:
                        load_wd(m + 1)
                    w = wd[m % 2]
                    for tl in range(ntg):
                        ti = gi * ntg + tl
                        po = ps[4 + it % 2]
                        x = xc[it % 2]
                        it += 1
                        cap = self.xs[m * 128:(m + 1) * 128, ti * T:(ti + 1) * T]
                        fw.dma("sp", x.t[:], cap, reads=[self.xs_b[ti]], writes=[x.b], sem_buf=x.b)
                        for k in range(HC):
                            fw.op("pe", lambda e: e.matmul(
                                po.t[:], w.t[:, k, :], act.t[:, k, tl * T:(tl + 1) * T], start=(k == 0), stop=(k == HC - 1)),
                                reads=[w.b, act_b[tl]], writes=[po.b])
                        fw.op("dve", lambda e: e.tensor_tensor(x.t[:], po.t[:], x.t[:], ALU.add),
                              reads=[po.b, x.b], writes=[x.b])
                        fw.dma("sp", cap, x.t[:], reads=[x.b], writes=[self.xs_b[ti]], sem_buf=x.b)
                    if gi + 1 < ngrp and m % 2 == 1 and m // 2 < ntg:
                        norm_tile(gi + 1, m // 2)

    def final_norm(self, src):
        fw = self.fw
        sdram, sb = src
        with self.phase("fin") as ph:
            xin = [ph.sb("xin%d" % i, [128, 8, T], F32) for i in range(2)]
            yo = [ph.sb("yo%d" % i, [128, 8, T], F32) for i in range(2)]
            sq = ph.sb("sq", [128, 8, T], BF16)
            rstd = ph.sb("rstd", [128, T], F32)
            for ti in range(NT):
                x = xin[ti % 2]
                y = yo[ti % 2]
                fw.dma("sp", x.t[:], self.tile_ap(sdram, ti), reads=[sb[ti]], writes=[x.b], sem_buf=x.b)
                if self.final == "copy":
                    fw.dma("sp", self.tile_ap(self.y_out, ti), x.t[:], reads=[x.b], writes=[self.y_b[ti]], sem_buf=x.b)
                    continue
                self.rmsnorm(x, (lambda k, y=y: y.t[:, k, :], y.b), sq, rstd, self.ps[6], 8)
                fw.dma("sp", self.tile_ap(self.y_out, ti), y.t[:], reads=[y.b], writes=[self.y_b[ti]], sem_buf=y.b)

    def conv_layer(self, l, src):
        fw = self.fw
        i2 = l // 2
        sdram, sbufs = src
        ps = self.ps
        with self.phase("conv%d" % l) as ph:
            w1 = ph.sb("w1", [128, 8, 2048], BF16)
            w2 = ph.sb("w2", [128, 8, 1024], BF16)
            dg = ph.sb("dg", [128, 31 * 8, 128], BF16)
            dg_b = [Buf("dg0"), Buf("dg1")]
            xin = [ph.sb("xin0", [128, 8, T], F32)] * 2
            h = ph.sb("h", [128, 8, T], BF16)
            sq = h
            rstd = ph.sb("rstd", [128, T], F32)
            z = [ph.sb("z%d" % i, [128, 8, 30 + T], BF16) for i in range(2)]
            sg = [ph.sb("sg%d" % i, [128, T], F32) for i in range(2)]
            y = ph.sb("y", [128, 8, T], F32)
            ybf = h
            mean = ph.sb("mean", [128, T], F32)
            msq = ph.sb("msq", [128, T], F32)
            lrs = ph.sb("lrs", [128, T], F32)
            yn = sg
            s_bf = ph.sb("s_bf", [128, 8, T], BF16)
            ysq = s_bf
            for q in range(4):
                fw.dma("pool", w1.t[:, 2 * q:2 * q + 2, :].rearrange("p k c -> p (k c)"),
                       self.d_cw1[i2, :, 2 * q * 2048:(2 * q + 2) * 2048], writes=[w1.b], sem_buf=w1.b)
            for q in range(2):
                fw.dma("pool", w2.t[:, 4 * q:4 * q + 4, :].rearrange("p k c -> p (k c)"),
                       self.d_cw2[i2, :, 4 * q * 1024:(4 * q + 4) * 1024], writes=[w2.b], sem_buf=w2.b)
            for j in range(31 * 8):
                if j % 2 == 0:
                    fw.op("dve", lambda e, j=j: e.tensor_scalar(dg.t[:, j, :], self.identf.t[:], self.smc("cdw_%d" % i2, j), None, ALU.mult),
                          reads=[self.identf.b, self.sm.b], writes=[dg_b[0]])
                else:
                    fw.op("act", lambda e, j=j: e.activation(dg.t[:, j, :], self.identf.t[:], AF.Identity, scale=self.smc("cdw_%d" % i2, j)),
                          reads=[self.identf.b, self.sm.b], writes=[dg_b[1]])
            fw.op("pool", lambda e: e.memset(z[0].t[:, :, 0:30], 0.0), writes=[z[0].b])
            x = xin[0]
            y_b = [Buf("y%d" % m) for m in range(8)]
            sgl = sg[0]
            yns = [sg[1], msq]
            p1, p2 = ps[6], ps[7]

            def do_norm(ti):
                if ti == 0:
                    fw.dma("sp", x.t[:], self.tile_ap(sdram, ti), reads=[sbufs[ti]], writes=[x.b], sem_buf=x.b)
                self.rmsnorm(x, (lambda k: h.t[:, k, :], h.b), sq, rstd, ps[6], l)
                if ti + 1 < NT:
                    fw.dma("sp", x.t[:], self.tile_ap(sdram, ti + 1), reads=[sbufs[ti + 1]], writes=[x.b], sem_buf=x.b)

            def pw1_chunk(ti, m):
                zc = z[ti % 2]
                pa = ps[m % 2]
                pg = ps[2 + m % 2]
                for a_, pp in ((0, pa), (1, pg)):
                    for k in range(8):
                        fw.op("pe", lambda e: e.matmul(
                            pp.t[:], w1.t[:, k, a_ * 1024 + m * 128: a_ * 1024 + (m + 1) * 128], h.t[:, k, :],
                            start=(k == 0), stop=(k == 7)), reads=[w1.b, h.b], writes=[pp.b])
                return pa, pg

            def glu_chunk(ti, m, pa, pg):
                zc = z[ti % 2]
                fw.op("act", lambda e: e.activation(sgl.t[:], pg.t[:], AF.Sigmoid, bias=self.smc("cb1_%d" % i2, 8 + m)),
                      reads=[pg.b, self.sm.b], writes=[sgl.b])
                fw.op("dve", lambda e: e.scalar_tensor_tensor(
                    zc.t[:, m, 30:30 + T], pa.t[:], self.smc("cb1_%d" % i2, m), sgl.t[:], ALU.add, ALU.mult),
                    reads=[pa.b, sgl.b, self.sm.b], writes=[zc.b])

            def ln_chunk(ti, m):
                yy = yns[m % 2]
                fw.op("dve", lambda e: e.tensor_tensor(yy.t[:], y.t[:, m, :], mean.t[:], ALU.subtract),
                      reads=[y_b[m], mean.b], writes=[yy.b])
                fw.op("dve", lambda e: e.tensor_tensor(yy.t[:], yy.t[:], lrs.t[:], ALU.mult),
                      reads=[yy.b, lrs.b], writes=[yy.b])
                fw.op("act", lambda e: e.activation(s_bf.t[:, m, :], yy.t[:], AF.Silu,
                                                    bias=self.smc("clnb_%d" % i2, m), scale=self.smc("clng_%d" % i2, m)),
                      reads=[yy.b, self.sm.b], writes=[s_bf.b])
                src_c = sdram[m * 128:(m + 1) * 128, ti * T:(ti + 1) * T]
                fw.dma("sp", y.t[:, m, :], src_c, reads=[sbufs[ti]], writes=[y_b[m]], sem_buf=y_b[m])

            do_norm(0)
            for m in range(8):
                pa, pg = pw1_chunk(0, m)
                glu_chunk(0, m, pa, pg)
            for ti in range(NT):
                zc = z[ti % 2]
                zn = z[(ti + 1) % 2]
                last = (ti + 1 == NT)
                if not last:
                    fw.op("pool", lambda e: e.tensor_copy(zn.t[:, :, 0:30], zc.t[:, :, T:T + 30]), reads=[zc.b], writes=[zn.b])
                for m in range(8):
                    pc = ps[4 + m % 2]
                    for k in range(31):
                        fw.op("pe", lambda e: e.matmul(
                            pc.t[:], dg.t[:, k * 8 + m, :], zc.t[:, m, k:k + T], start=(k == 0), stop=(k == 30)),
                            reads=[dg_b[0], dg_b[1], zc.b], writes=[pc.b])
                    fw.op("act", lambda e: e.activation(y.t[:, m, :], pc.t[:], AF.Identity, bias=self.smc("cbdw_%d" % i2, m)),
                          reads=[pc.b, self.sm.b], writes=[y_b[m]])
                fw.op("dve", lambda e: e.tensor_copy(ybf.t[:], y.t[:]), reads=y_b, writes=[ybf.b])
                fw.op("act", lambda e: e.activation(ysq.t[:], y.t[:], AF.Square), reads=y_b, writes=[ysq.b])
                for k in range(8):
                    fw.op("pe", lambda e: e.matmul(p1.t[:], self.ones_bf(), ybf.t[:, k, :], start=(k == 0), stop=(k == 7)),
                          reads=[ybf.b, self.bc.b], writes=[p1.b])
                for k in range(8):
                    fw.op("pe", lambda e: e.matmul(p2.t[:], self.ones_bf(), ysq.t[:, k, :], start=(k == 0), stop=(k == 7)),
                          reads=[ysq.b, self.bc.b], writes=[p2.b])
                fw.op("dve", lambda e: e.tensor_scalar(mean.t[:], p1.t[:], 1.0 / 1024, None, ALU.mult), reads=[p1.b], writes=[mean.b])
                fw.op("dve", lambda e: e.tensor_tensor(lrs.t[:], mean.t[:], mean.t[:], ALU.mult), reads=[mean.b], writes=[lrs.b])
                fw.op("dve", lambda e: e.scalar_tensor_tensor(lrs.t[:], p2.t[:], 1.0 / 1024, lrs.t[:], ALU.mult, ALU.subtract),
                      reads=[p2.b, lrs.b], writes=[lrs.b])
                fw.op("act", lambda e: e.activation(lrs.t[:], lrs.t[:], AF.Sqrt, bias=self.smc("eps5"), scale=1.0),
                      reads=[lrs.b, self.sm.b], writes=[lrs.b])
                fw.op("dve", lambda e: e.reciprocal(lrs.t[:], lrs.t[:]), reads=[lrs.b], writes=[lrs.b])
                if not last:
                    do_norm(ti + 1)
                for m in range(8):
                    if not last:
                        pa, pg = pw1_chunk(ti + 1, m)
                    ln_chunk(ti, m)
                    if not last:
                        glu_chunk(ti + 1, m, pa, pg)
                for m in range(8):
                    po = ps[m % 2]
                    for k in range(8):
                        fw.op("pe", lambda e: e.matmul(po.t[:], w2.t[:, k, m * 128:(m + 1) * 128], s_bf.t[:, k, :],
                                                       start=(k == 0), stop=(k == 7)),
                              reads=[w2.b, s_bf.b], writes=[po.b])
                    dst_c = self.xs[m * 128:(m + 1) * 128, ti * T:(ti + 1) * T]
                    fw.op("dve", lambda e: e.scalar_tensor_tensor(
                        y.t[:, m, :], po.t[:], self.smc("cb2_%d" % i2, m), y.t[:, m, :], ALU.add, ALU.add),
                        reads=[po.b, y_b[m], self.sm.b], writes=[y_b[m]])
                    fw.dma("sp", dst_c, y.t[:, m, :], reads=[y_b[m]], writes=[self.xs_b[ti]], sem_buf=y_b[m])

    def even_layer(self, l, src):
        fw = self.fw
        import os
        STAGE = int(os.environ.get('EVEN_STAGE', '9'))
        SUB = os.environ.get('EVEN_SUB', 'NQMKGVP')
        BR = os.environ.get('EVEN_BR', 'csw')
        HEADS = [int(c) for c in os.environ.get('EVEN_HEADS', '01234567')]
        i2 = l // 2
        sdram, sbufs = src
        ps = self.ps
        bc = self.bc
        with self.phase("nsa%d" % l) as ph:
            win = ph.sb("win", [128, 8, 1816], BF16)
            wo = ph.sb("wo", [128, 8, 1024], BF16)
            w1c = ph.sb("w1c", [128, 2, 2, 32, 128], BF16)
            post = ph.sb("post", [128, 2, 32, 2], BF16)
            w2c = ph.sb("w2c", [128, 2, 64], BF16)
            pw = ph.sb("pw", [128, 4, 128], BF16)
            KS = [ph.sb("ks%d" % g, [128, S], BF16) for g in range(2)]
            KW = [ph.sb("kw%d" % g, [128, 1024], BF16) for g in range(2)]
            VS = ph.sb("vs", [128, 2, 32, 128], BF16)
            VW = ph.sb("vw", [128, 2, 8, 128], BF16)
            KCT = ph.sb("kct", [128, 16 + T], BF16)
            VCT = ph.sb("vct", [128, 16 + T], BF16)
            KC = [ph.sb("kc%d" % g, [128, 256], BF16) for g in range(2)]
            VC = ph.sb("vc", [128, 2, 2, 128], BF16)
            Qs = [ph.sb("q%d" % h, [128, T], BF16) for h in range(8)]
            x = ph.sb("x", [128, 8, T], F32)
            h_ = ph.sb("h", [128, 8, T], BF16)
            sgT = ph.sb("sgT", [128, T], BF16)
            pa = [ph.sb("pa%d" % i, [128, 16 + T], F32) for i in range(3)]
            rstd = TB(pa[2].t[:, 0:T], pa[2].b)
            halo = ph.sb("halo", [128, 4, 16], F32)
            dT = ph.sb("dT", [128, 4, T], BF16)
            yp = ph.sb("yp", [128, 4, T], BF16)
            oT = ph.sb("oT", [128, 4, T], BF16)
            E = [ph.sb("E%d" % i, [128, T], BF16) for i in range(3)]
            zc = ph.sb("zc", [128, T], F32)
            gz = ph.sb("gz", [128, T], F32)
            acc2 = [ph.sb("acc%d" % i, [128, T], F32) for i in range(2)]
            zc_b = [Buf("zc0"), Buf("zc1")]
            gz_b = [Buf("gz0"), Buf("gz1")]
            acc_b = [Buf("acc%d" % i) for i in range(4)]
            impa = ph.sb("impa", [128, 4, 64], F32)
            impf = ph.sb("impf", [128, 64], F32)
            top8 = ph.sb("top8", [128, 8], F32)
            selb = [ph.sb("selb0", [128, 128], F32)] * 2
            rz4 = ph.sb("rz4", [128, 4], F32)
            posb = ph.sb("posb", [128, 2], F32)
            gus = [ph.sb("gu%d" % i, [128, 32], F32) for i in range(4)]
            gvs = [ph.sb("gv%d" % i, [128, 32], F32) for i in range(4)]
            gss = [ph.sb("gs%d" % i, [128, 32], F32) for i in range(4)]
            Gs = [ph.sb("G%d" % i, [128, 128], BF16) for i in range(4)]
            gu = gus[0]
            for q in range(4):
                fw.dma("pool", win.t[:, 2 * q:2 * q + 2, :].rearrange("p k c -> p (k c)"),
                       self.d_win[i2, :, 2 * q * 1816:(2 * q + 2) * 1816], writes=[win.b], sem_buf=win.b)
            for q in range(2):
                fw.dma("pool", wo.t[:, 4 * q:4 * q + 4, :].rearrange("p k c -> p (k c)"),
                       self.d_wo[i2, :, 4 * q * 1024:(4 * q + 4) * 1024], writes=[wo.b], sem_buf=wo.b)
            for g_ in range(2):
                for s_ in range(2):
                    for q in range(2):
                        o_ = ((g_ * 2 + s_) * 32 + 16 * q) * 128
                        fw.dma("pool", w1c.t[:, g_, s_, 16 * q:16 * q + 16, :].rearrange("p l c -> p (l c)"),
                               self.d_w1c[i2, :, o_:o_ + 2048], writes=[w1c.b], sem_buf=w1c.b)
            fw.dma("pool", post.t[:].rearrange("p s l r -> p (s l r)"), self.d_post[i2], writes=[post.b], sem_buf=post.b)
            fw.dma("pool", w2c.t[:].rearrange("p s d -> p (s d)"), self.d_w2c[i2], writes=[w2c.b], sem_buf=w2c.b)
            fw.dma("pool", pw.t[:].rearrange("p m e -> p (m e)"), self.d_pw[i2], writes=[pw.b], sem_buf=pw.b)
            for g in range(2):
                fw.dma("pool", KS[g].t[64:128, :], self.d_ex[:, :], writes=[KS[g].b], sem_buf=KS[g].b)
                fw.dma("pool", KW[g].t[64:128, :], self.d_e0[:, 0:1024], writes=[KW[g].b], sem_buf=KW[g].b)
                fw.dma("pool", KC[g].t[64:128, :], self.d_e0[:, 0:256], writes=[KC[g].b], sem_buf=KC[g].b)
                fw.op("dve", lambda e, g=g: e.memset(KC[g].t[0:64, :], 0.0), writes=[KC[g].b])
            fw.op("dve", lambda e: e.memset(VS.t[:, :, :, 64:128], 1.0), writes=[VS.b])
            fw.op("dve", lambda e: e.memset(VW.t[:, :, :, 64:128], 1.0), writes=[VW.b])
            fw.op("dve", lambda e: e.memset(VC.t[:, :, :, 0:64], 0.0), writes=[VC.b])
            fw.op("dve", lambda e: e.memset(VC.t[:, :, :, 64:128], 1.0), writes=[VC.b])
            fw.op("dve", lambda e: e.memset(KCT.t[:, 0:16], 0.0), writes=[KCT.b])
            fw.op("dve", lambda e: e.memset(VCT.t[:, 0:16], 0.0), writes=[VCT.b])
            fw.op("dve", lambda e: e.memset(halo.t[:], 0.0), writes=[halo.b])
            fw.op("dve", lambda e: e.memset(oT.t[:], 0.0), writes=[oT.b])
            for G_ in Gs:
                fw.op("dve", lambda e: e.memset(G_.t[:], 0.0), writes=[G_.b])
            fw.op("dve", lambda e: e.memset(selb[0].t[:], 0.0), writes=[selb[0].b])
            for s_ in range(2):
                pp = ps[6 + s_]
                for ll in range(32):
                    fw.op("pe", lambda e, s_=s_, ll=ll, pp=pp: e.matmul(pp.t[:, 0:2], w1c.t[0:64, 0, s_, ll, :], post.t[0:64, s_, ll, :],
                                                                      start=(ll == 0), stop=(ll == 31)),
                          reads=[w1c.b, post.b], writes=[pp.b])
                fw.op("dve", lambda e, s_=s_, pp=pp: e.tensor_copy(posb.t[:, s_:s_ + 1], pp.t[:, 0:1]), reads=[pp.b], writes=[posb.b])

            fw.barrier()
            gen_i = [0]

            def gbank():
                gen_i[0] += 1
                return ps[6 + gen_i[0] % 2]

            ecnt = [0]
            import collections
            pend = collections.deque()
            scnt = [0]
            ocnt = [0]

            fw.op("dve", lambda e: e.memset(yp.t[:], 0.0), writes=[yp.b])
            for i in range(NT):
                c0 = i * T
                fw.mute = False
                if i == 0:
                    fw.dma("sp", x.t[:], self.tile_ap(sdram, i), reads=[sbufs[i]], writes=[x.b], sem_buf=x.b)
                fw.mute = STAGE < 1 or 'N' not in SUB
                self.rmsnorm(x, (lambda k: h_.t[:, k, :], h_.b), h_, rstd, ps[5], l)
                fw.mute = False
                if i + 1 < NT:
                    fw.dma("sp", x.t[:], self.tile_ap(sdram, i + 1), reads=[sbufs[i + 1]], writes=[x.b], sem_buf=x.b)

                def proj(col0, ncol):
                    pp = gbank()
                    for k in range(8):
                        fw.op("pe", lambda e, k=k, pp=pp: e.matmul(pp.t[0:ncol, :], win.t[:, k, col0:col0 + ncol], h_.t[:, k, :],
                                                                  start=(k == 0), stop=(k == 7)),
                              reads=[win.b, h_.b], writes=[pp.b])
                    return pp
                fw.mute = STAGE < 1 or 'Q' not in SUB
                for m in range(4):
                    pp = proj(m * 128, 128)
                    ha, hb2 = 2 * m, 2 * m + 1
                    fw.op("act", lambda e, pp=pp, ha=ha: e.activation(Qs[ha].t[0:64, :], pp.t[0:64, :], AF.Identity, scale=0.125),
                          reads=[pp.b], writes=[Qs[ha].b])
                    fw.op("dve", lambda e, pp=pp, hb2=hb2: e.tensor_scalar(Qs[hb2].t[0:64, :], pp.t[64:128, :], 0.125, None, ALU.mult),
                          reads=[pp.b], writes=[Qs[hb2].b])
                fw.mute = STAGE < 1 or 'M' not in SUB
                for hh in range(8):
                    fw.op("dve", lambda e, hh=hh: e.tensor_scalar(Qs[hh].t[64:128, :], self.smc("arow", 0, 512)[64:128, :], -SLOPES[hh], None, ALU.mult),
                          reads=[self.sm.b], writes=[Qs[hh].b])
                fw.mute = STAGE < 1 or 'K' not in SUB
                pp = proj(512, 128)
                fw.op("act", lambda e, pp=pp: e.activation(KCT.t[:, 16:16 + T], pp.t[:], AF.Identity), reads=[pp.b], writes=[KCT.b])
                pp = proj(640, 128)
                fw.op("dve", lambda e, pp=pp: e.tensor_copy(VCT.t[:, 16:16 + T], pp.t[:]), reads=[pp.b], writes=[VCT.b])
                pp = proj(768, 128)
                fw.op("act", lambda e, pp=pp: e.activation(KS[0].t[0:64, c0:c0 + T], pp.t[0:64, :], AF.Identity), reads=[pp.b], writes=[KS[0].b])
                fw.op("dve", lambda e, pp=pp: e.tensor_copy(KS[1].t[0:64, c0:c0 + T], pp.t[64:128, :]), reads=[pp.b], writes=[KS[1].b])
                pp = proj(896, 128)
                r0 = (i % 2) * T
                fw.op("act", lambda e, pp=pp: e.activation(KW[0].t[0:64, r0:r0 + T], pp.t[0:64, :], AF.Identity), reads=[pp.b], writes=[KW[0].b])
                fw.op("dve", lambda e, pp=pp: e.tensor_copy(KW[1].t[0:64, r0:r0 + T], pp.t[64:128, :]), reads=[pp.b], writes=[KW[1].b])
                fw.mute = STAGE < 1 or 'G' not in SUB
                pp = proj(1024, 128)
                fw.op("act", lambda e, pp=pp: e.activation(sgT.t[:], pp.t[:], AF.Sigmoid), reads=[pp.b], writes=[sgT.b])
                fw.mute = STAGE < 1 or 'V' not in SUB
                for s_ in range(4):
                    cs = 4 * i + s_
                    for half in range(2):
                        pp = gbank()
                        for k in range(8):
                            fw.op("pe", lambda e, k=k, pp=pp, s_=s_, half=half: e.matmul(
                                pp.t[:, 0:128], h_.t[:, k, s_ * 128:(s_ + 1) * 128],
                                win.t[:, k, 1560 + half * 128:1560 + (half + 1) * 128], start=(k == 0), stop=(k == 7)),
                                reads=[win.b, h_.b], writes=[pp.b])
                        if half == 0:
                            fw.op("act", lambda e, pp=pp, cs=cs: e.activation(VS.t[:, :, cs, 0:64], pp.t[:, 0:128].rearrange("p (g d) -> p g d", g=2), AF.Identity),
                                  reads=[pp.b], writes=[VS.b])
                        else:
                            fw.op("dve", lambda e, pp=pp, cs=cs: e.tensor_copy(VW.t[:, :, cs % 8, 0:64], pp.t[:, 0:128].rearrange("p (g d) -> p g d", g=2)),
                                  reads=[pp.b], writes=[VW.b])
                fw.mute = STAGE < 1 or 'P' not in SUB
                for m in range(4):
                    w_ = 2 ** (m + 1)
                    pp = proj(1048 + m * 128, 128)
                    A_, B_, C_ = pa
                    fw.op("act", lambda e, pp=pp: e.activation(A_.t[:, 16:16 + T], pp.t[:], AF.Identity), reads=[pp.b], writes=[A_.b])
                    fw.op("dve", lambda e, m=m: e.tensor_copy(A_.t[:, 0:16], halo.t[:, m, :]), reads=[halo.b], writes=[A_.b])
                    fw.op("dve", lambda e, m=m: e.tensor_copy(halo.t[:, m, :], A_.t[:, T:T + 16]), reads=[A_.b], writes=[halo.b])
                    cur = A_
                    for st_ in range(m + 1):
                        sh = 2 ** st_
                        lo = 2 ** (st_ + 1) - 1
                        dst = B_ if st_ % 2 == 0 else C_
                        fw.op("dve", lambda e, cur=cur, dst=dst, sh=sh, lo=lo: e.tensor_tensor(
                            dst.t[:, lo:16 + T], cur.t[:, lo:16 + T], cur.t[:, lo - sh:16 + T - sh], ALU.add),
                            reads=[cur.b], writes=[dst.b])
                        cur = dst
                    fw.op("dve", lambda e, cur=cur, m=m, w_=w_: e.scalar_tensor_tensor(dT.t[:, m, :], cur.t[:, 16:16 + T], 1.0 / w_, A_.t[:, 16:16 + T],
                                                                                     ALU.mult, ALU.subtract),
                          reads=[cur.b, A_.b], writes=[dT.b])
                    if i == 0:
                        fw.op("dve", lambda e, cur=cur, m=m: e.tensor_tensor(gu.t[:, 0:16], cur.t[:, 16:32], self.smc("rc15", m * 16, 16), ALU.mult),
                              reads=[cur.b, self.sm.b], writes=[gu.b])
                        fw.op("dve", lambda e, m=m: e.tensor_tensor(dT.t[:, m, 0:16], gu.t[:, 0:16], A_.t[:, 16:32], ALU.subtract),
                              reads=[gu.b, A_.b], writes=[dT.b])
                    pq = gbank()
                    fw.op("pe", lambda e, pq=pq, m=m: e.matmul(pq.t[:], pw.t[:, m, :], dT.t[:, m, :], start=True, stop=True),
                          reads=[pw.b, dT.b], writes=[pq.b])
                    fw.op("act", lambda e, pq=pq, m=m: e.activation(yp.t[:, m, :], pq.t[:], AF.Identity, scale=self.smc("pscale_%d" % i2, m)),
                          reads=[pq.b, self.sm.b], writes=[yp.b])
                fw.mute = STAGE < 2
                ccn = (32 * i) // 128
                pn = (32 * i) % 128
                for s_ in range(2):
                    XT = KCT if s_ == 0 else VCT
                    for g in range(2):
                        gu, gv, gs, G = gus[s_ * 2 + g], gvs[s_ * 2 + g], gss[s_ * 2 + g], Gs[s_ * 2 + g]
                        pp = gbank()
                        for ll in range(32):
                            fw.op("pe", lambda e, ll=ll, pp=pp, g=g, s_=s_, XT=XT: e.matmul(
                                pp.t[:, 0:32], w1c.t[:, g, s_, ll, :], XT.t[:, ll:ll + 497:16],
                                start=(ll == 0), stop=(ll == 31)), reads=[w1c.b, XT.b], writes=[pp.b])
                        fw.op("act", lambda e, pp=pp, s_=s_: e.activation(gu.t[:], pp.t[:, 0:32], AF.Identity, bias=posb.t[:, s_:s_ + 1]),
                              reads=[pp.b, posb.b], writes=[gu.b])
                        fw.op("dve", lambda e: e.tensor_tensor(gv.t[:], gu.t[:], gu.t[:], ALU.mult), reads=[gu.b], writes=[gv.b])
                        fw.op("dve", lambda e: e.tensor_scalar(gv.t[:], gv.t[:], 0.044715, 1.0, ALU.mult, ALU.add), reads=[gv.b], writes=[gv.b])
                        fw.op("dve", lambda e: e.tensor_tensor(gv.t[:], gv.t[:], gu.t[:], ALU.mult), reads=[gv.b, gu.b], writes=[gv.b])
                        fw.op("act", lambda e: e.activation(gs.t[:], gv.t[:], AF.Sigmoid, scale=1.5957691216057308), reads=[gv.b], writes=[gs.b])
                        fw.op("dve", lambda e: e.tensor_tensor(G.t[:, 0:32], gu.t[:], gs.t[:], ALU.mult), reads=[gu.b, gs.b], writes=[G.b])
                        pq = gbank()
                        if s_ == 0:
                            fw.op("pe", lambda e, pq=pq: e.matmul(pq.t[:, 0:32], w2c.t[:].rearrange("p s d -> p (s d)"), G.t[:, 0:32], start=True, stop=True),
                                  reads=[w2c.b, G.b], writes=[pq.b])
                            fw.op("dve", lambda e, pq=pq, g=g: e.tensor_copy(KC[g].t[0:64, 32 * i:32 * i + 32], pq.t[0:64, 0:32]),
                                  reads=[pq.b], writes=[KC[g].b])
                        else:
                            fw.op("pe", lambda e, pq=pq: e.matmul(pq.t[:, 0:64], G.t[:], w2c.t[:, 1, :], start=True, stop=True),
                                  reads=[w2c.b, G.b], writes=[pq.b])
                            fw.op("dve", lambda e, pq=pq, g=g: e.tensor_copy(VC.t[pn:pn + 32, g, ccn, 0:64], pq.t[0:32, 0:64]),
                                  reads=[pq.b], writes=[VC.b])
                            if i == 0:
                                fw.op("dve", lambda e, g=g: e.memset(VC.t[0:1, g, 0, :], 0.0), writes=[VC.b])
                fw.op("pool", lambda e: e.tensor_copy(KCT.t[:, 0:16], KCT.t[:, T:T + 16]), reads=[KCT.b], writes=[KCT.b])
                fw.op("pool", lambda e: e.tensor_copy(VCT.t[:, 0:16], VCT.t[:, T:T + 16]), reads=[VCT.b], writes=[VCT.b])

                fw.mute = STAGE < 3
                def pump(lag):
                    while pend and (pend[0][0] == "fin" or sum(1 for k_, _ in pend if k_ == "pv") > lag):
                        pend.popleft()[1]()

                def run_branch(hh, chunks, fin):
                    po = ps[3 + ocnt[0] % 2]
                    ocnt[0] += 1
                    n = len(chunks)
                    used = []
                    for idx, (kap, kb, vap, vb, bcol, mask) in enumerate(chunks):
                        sb_ = ps[scnt[0] % 3]
                        scnt[0] += 1
                        eb = E[ecnt[0] % len(E)]
                        ecnt[0] += 1
                        fw.op("pe", lambda e: e.matmul(sb_.t[:], kap, Qs[hh].t[:], start=True, stop=(mask is None)),
                              reads=[kb, Qs[hh].b], writes=[sb_.b])
                        if mask is not None:
                            fw.op("pe", lambda e: e.matmul(sb_.t[:], self.ident_bf(), mask, start=False, stop=True),
                                  reads=[bc.b], writes=[sb_.b])
                        fw.op("act", lambda e: e.activation(eb.t[:], sb_.t[:], AF.Exp, bias=self.sm.t[:, bcol:bcol + 1]),
                              reads=[sb_.b, self.sm.b], writes=[eb.b])
                        used.append(eb)

                        def pv(idx=idx, vap=vap, vb=vb, eb=eb):
                            fw.op("pe", lambda e: e.matmul(po.t[:], vap, eb.t[:], start=(idx == 0), stop=(idx == n - 1)),
                                  reads=[vb, eb.b], writes=[po.b])
                        pend.append(("pv", pv))
                        if idx == n - 1:
                            pend.append(("fin", lambda: fin(po, used)))
                        pump(2)

                def combine(hh, b, po):
                    pg = gbank()
                    hp = (hh % 2) * 64
                    hs = slice(hp, hp + 64)
                    zcb, gzb = zc_b[hh % 2], gz_b[hh % 2]
                    ac = acc2[(hh % 4) // 2]
                    acb = acc_b[hh % 4]
                    on = ("csw"[b] in BR)
                    if on:
                        fw.op("pe", lambda e: e.matmul(pg.t[:], bc.t[:, BC["eg"] + (hh * 3 + b) * 64: BC["eg"] + (hh * 3 + b) * 64 + 128], sgT.t[:],
                                                       start=True, stop=True), reads=[bc.b, sgT.b], writes=[pg.b])
                        fw.op("dve", lambda e: e.tensor_scalar(zc.t[hs, :], po.t[64:128, :], 1e-30, None, ALU.add), reads=[po.b], writes=[zcb])
                        fw.op("dve", lambda e: e.reciprocal(zc.t[hs, :], zc.t[hs, :]), reads=[zcb], writes=[zcb])
                        fw.op("dve", lambda e: e.tensor_tensor(gz.t[hs, :], pg.t[0:64, :], zc.t[hs, :], ALU.mult), reads=[pg.b, zcb], writes=[gzb])
                    if b == 0:
                        if on:
                            fw.op("dve", lambda e: e.tensor_tensor(ac.t[hs, :], po.t[0:64, :], gz.t[hs, :], ALU.mult), reads=[po.b, gzb], writes=[acb])
                        else:
                            fw.op("dve", lambda e: e.memset(ac.t[hs, :], 0.0), writes=[acb])
                    elif b == 1:
                        if on:
                            fw.op("dve", lambda e: e.tensor_tensor(gz.t[hs, :], po.t[0:64, :], gz.t[hs, :], ALU.mult), reads=[po.b, gzb], writes=[gzb])
                            fw.op("pool", lambda e: e.tensor_tensor(ac.t[hs, :], ac.t[hs, :], gz.t[hs, :], ALU.add), reads=[acb, gzb], writes=[acb])
                    else:
                        if on:
                            fw.op("dve", lambda e: e.tensor_tensor(gz.t[hs, :], po.t[0:64, :], gz.t[hs, :], ALU.mult), reads=[po.b, gzb], writes=[gzb])
                            fw.op("dve", lambda e: e.tensor_tensor(oT.t[hs, hh // 2, :], ac.t[hs, :], gz.t[hs, :], ALU.add),
                                  reads=[acb, gzb], writes=[oT.b])
                        else:
                            fw.op("dve", lambda e: e.tensor_copy(oT.t[hs, hh // 2, :], ac.t[hs, :]), reads=[acb], writes=[oT.b])

                for g in range(2):
                    fw.mute = STAGE < 3
                    ccs = [0] if i < 4 else [0, 1]
                    for hi_, hh in enumerate(range(4 * g, 4 * g + 4)):
                        chunks = []
                        for cc in ccs:
                            x0 = 512 * i - 2048 * cc
                            mask = bc.t[:, BC["g1"] + x0: BC["g1"] + x0 + 512] if x0 <= 1536 else None
                            chunks.append((KC[g].t[:, cc * 128:(cc + 1) * 128], KC[g].b, VC.t[:, g, cc, :], VC.b,
                                           SM["bias_cmp"] + (hh * 2 + cc) * 8 + i, mask))
                        def cmp_fin(po, used, hh=hh, hi_=hi_, ccs=ccs):
                            pu = ps[5]
                            for s_ in range(4):
                                for ci, cc in enumerate(ccs):
                                    fw.op("pe", lambda e: e.matmul(
                                        pu.t[:, s_ * 65:(s_ + 1) * 65], used[ci].t[:, s_ * 128:(s_ + 1) * 128],
                                        bc.t[:, BC["ova"] + cc * 65: BC["ova"] + (cc + 1) * 65], start=(ci == 0), stop=(ci == len(ccs) - 1)),
                                        reads=[used[ci].b, bc.b], writes=[pu.b])
                            combine(hh, 0, po)
                            zsl = pu.t[:, 0:260].rearrange("p (s c) -> p s c", c=65)[:, :, 64]
                            fw.op("dve", lambda e: e.tensor_scalar(rz4.t[:], zsl, 1e-30, None, ALU.add), reads=[pu.b], writes=[rz4.b])
                            fw.op("dve", lambda e: e.reciprocal(rz4.t[:], rz4.t[:]), reads=[rz4.b], writes=[rz4.b])
                            for s_ in range(4):
                                if hi_ == 0:
                                    fw.op("dve", lambda e: e.tensor_scalar(impa.t[:, s_, :], pu.t[:, s_ * 65:s_ * 65 + 64], rz4.t[:, s_:s_ + 1], None, ALU.mult),
                                          reads=[pu.b, rz4.b], writes=[impa.b])
                                else:
                                    fw.op("dve", lambda e: e.scalar_tensor_tensor(impa.t[:, s_, :], pu.t[:, s_ * 65:s_ * 65 + 64], rz4.t[:, s_:s_ + 1],
                                                                                 impa.t[:, s_, :], ALU.mult, ALU.add),
                                          reads=[pu.b, rz4.b, impa.b], writes=[impa.b])
                        run_branch(hh, chunks, cmp_fin)
                        pump(0)
                    pump(0)
                    fw.mute = STAGE < 4
                    pT = ps[5]
                    for s_ in range(4):
                        sub = 4 * i + s_
                        fo = SM["fbw"] + 62 - 2 * sub
                        sl_ = selb[s_ % 2]
                        fw.op("dve", lambda e, s_=s_, fo=fo: e.tensor_tensor(impf.t[:], impa.t[:, s_, :], self.sm.t[:, fo:fo + 64], ALU.add),
                              reads=[impa.b, self.sm.b], writes=[impf.b])
                        if sub >= 1:
                            fw.op("dve", lambda e: e.tensor_scalar(impf.t[:, 0:1], impf.t[:, 0:1], 1e4, None, ALU.add), reads=[impf.b], writes=[impf.b])
                        fw.op("dve", lambda e: e.max(top8.t[:], impf.t[:]), reads=[impf.b], writes=[top8.b])
                        fw.op("dve", lambda e, sl_=sl_: e.tensor_scalar(sl_.t[:, 0:64], impf.t[:], top8.t[:, 7:8], NEGB, ALU.is_lt, ALU.mult),
                              reads=[impf.b, top8.b], writes=[sl_.b])
                        fw.op("pe", lambda e, s_=s_, sl_=sl_: e.transpose(pT.t[:, s_ * 128:(s_ + 1) * 128], sl_.t[:], self.identf.t[:]),
                              reads=[sl_.b, self.identf.b], writes=[pT.b])
                    for hh in range(4 * g, 4 * g + 4):
                        fw.op("dve", lambda e, hh=hh: e.scalar_tensor_tensor(Qs[hh].t[64:128, :], self.smc("arow", 0, 512)[0:64, :], -SLOPES[hh], pT.t[0:64, :],
                                                                           ALU.mult, ALU.add),
                              reads=[self.sm.b, pT.b], writes=[Qs[hh].b])
                    fw.mute = STAGE < 5
                    for hh in range(4 * g, 4 * g + 4):
                        if hh not in HEADS:
                            continue
                        chunks = []
                        for c in range(0, 4 * i + 4):
                            dl = c - 4 * i
                            mask = None
                            if dl >= 0:
                                st0 = BC["caus"] + 384 - 128 * dl
                                mask = bc.t[:, st0:st0 + 512]
                            chunks.append((KS[g].t[:, c * 128:(c + 1) * 128], KS[g].b, VS.t[:, g, c, :], VS.b,
                                           SM["bias_sw"] + hh * 32 + dl + 28, mask))
                        run_branch(hh, chunks, lambda po, used, hh=hh: combine(hh, 1, po))
                        chunks = []
                        for c in range(max(0, 4 * i - 4), 4 * i + 4):
                            dl = c - 4 * i
                            if dl >= 0:
                                st0 = BC["caus"] + 384 - 128 * dl
                            else:
                                st0 = BC["far"] + 384 - 128 * (dl + 4)
                            mask = bc.t[:, st0:st0 + 512]
                            sl8 = c % 8
                            chunks.append((KW[g].t[:, sl8 * 128:(sl8 + 1) * 128], KW[g].b, VW.t[:, g, sl8, :], VW.b,
                                           SM["bias_sw"] + hh * 32 + dl + 28, mask))
                        run_branch(hh, chunks, lambda po, used, hh=hh: combine(hh, 2, po))
                pump(0)
                fw.mute = False
                for mo in range(8):
                    pp = gbank()
                    for k in range(8):
                        rhs = oT.t[:, k, :] if k < 4 else yp.t[:, k - 4, :]
                        rb = oT.b if k < 4 else yp.b
                        fw.op("pe", lambda e, k=k, pp=pp, rhs=rhs, mo=mo: e.matmul(pp.t[:], wo.t[:, k, mo * 128:(mo + 1) * 128], rhs,
                                                                                  start=(k == 0), stop=(k == 7)),
                              reads=[wo.b, rb], writes=[pp.b])
                    xq = pa[mo % 2]
                    src_c = sdram[mo * 128:(mo + 1) * 128, i * T:(i + 1) * T]
                    dst_c = self.xs[mo * 128:(mo + 1) * 128, i * T:(i + 1) * T]
                    fw.dma("sp", xq.t[:, 0:T], src_c, reads=[sbufs[i]], writes=[xq.b], sem_buf=xq.b)
                    fw.op("dve", lambda e, pp=pp, mo=mo: e.tensor_tensor(xq.t[:, 0:T], pp.t[:], xq.t[:, 0:T], ALU.add),
                          reads=[pp.b, xq.b], writes=[xq.b])
                    fw.dma("sp", dst_c, xq.t[:, 0:T], reads=[xq.b], writes=[self.xs_b[i]], sem_buf=xq.b)


def _prep_weights(inp):
    f = lambda a: np.ascontiguousarray(a, dtype=np.float32)
    out = {}
    sm, bc, ex, e0 = make_consts()

    def pc(v):
        return np.asarray(v, np.float32).reshape(-1, 128).T

    for l in range(4):
        sm[:, SM["normg"] + l * 8: SM["normg"] + (l + 1) * 8] = pc(inp["mix_norm"][l])
        sm[:, SM["normg"] + (4 + l) * 8: SM["normg"] + (5 + l) * 8] = pc(inp["ffn_norm"][l])
    sm[:, SM["normg"] + 64: SM["normg"] + 72] = pc(inp["final_norm"])
    for i in range(2):
        sm[:, SM["cb1_%d" % i]: SM["cb1_%d" % i] + 16] = pc(inp["conv_b_pw1"][i])
        sm[:, SM["cbdw_%d" % i]: SM["cbdw_%d" % i] + 8] = pc(inp["conv_b_dw"][i])
        sm[:, SM["clng_%d" % i]: SM["clng_%d" % i] + 8] = pc(inp["conv_ln_g"][i])
        sm[:, SM["clnb_%d" % i]: SM["clnb_%d" % i] + 8] = pc(inp["conv_ln_b"][i])
        sm[:, SM["cb2_%d" % i]: SM["cb2_%d" % i] + 8] = pc(inp["conv_b_pw2"][i])
        wdw = np.asarray(inp["conv_w_dw"][i], np.float32)
        sm[:, SM["cdw_%d" % i]: SM["cdw_%d" % i] + 248] = wdw.reshape(31, 8, 128).transpose(2, 0, 1).reshape(128, 248)
        sm[:, SM["pscale_%d" % i]: SM["pscale_%d" % i] + 4] = pc(inp["pool_scale"][i])
    out["smalls"] = f(sm)
    out["bconst"] = f(bc)
    out["exrows"] = f(ex)
    out["e0rows"] = f(e0)
    wg = np.asarray(inp["ffn_w_gate"], np.float32).reshape(4, 8, 128, HC, 128)
    wu = np.asarray(inp["ffn_w_up"], np.float32).reshape(4, 8, 128, HC, 128)
    wgu = np.stack([wg, wu], axis=0)
    out["wgu"] = f(wgu.transpose(1, 4, 3, 0, 2, 5).reshape(4, HC, 128, 2 * 8 * 128))
    wd = np.asarray(inp["ffn_w_down"], np.float32).reshape(4, HC, 128, 8, 128)
    out["wd"] = f(wd.transpose(0, 3, 2, 1, 4).reshape(4, 8, 128, HC * 128))
    perm = np.concatenate([np.arange(0, 512), np.arange(512, 640), np.arange(640, 768), np.arange(768, 896),
                           np.arange(1024, 1152), np.arange(1280, 1304), np.arange(1304, 1816),
                           np.arange(896, 1024), np.arange(1152, 1280)])
    win = np.asarray(inp["nsa_w_in"], np.float32)[:, :, perm].reshape(2, 8, 128, 1816)
    out["win"] = f(win.transpose(0, 2, 1, 3).reshape(2, 128, 8 * 1816))
    wo = np.asarray(inp["mix_w_out"], np.float32).reshape(2, 8, 128, 1024)
    out["wo"] = f(wo.transpose(0, 2, 1, 3).reshape(2, 128, 8 * 1024))
    w1c = np.stack([np.asarray(inp["cmp_k_w1"], np.float32), np.asarray(inp["cmp_v_w1"], np.float32)], axis=1)
    w1c = w1c.reshape(2, 2, 32, 64, 128).transpose(0, 3, 1, 2, 4)
    w1z = np.zeros((2, 128, 2, 2, 32, 128), np.float32)
    w1z[:, 0:64, 0] = w1c
    w1z[:, 64:128, 1] = w1c
    out["w1c"] = f(w1z.reshape(2, 128, 2 * 2 * 32 * 128))
    pos = np.stack([np.asarray(inp["cmp_pos_k"], np.float32), np.asarray(inp["cmp_pos_v"], np.float32)], axis=1)
    pos = pos.transpose(0, 3, 1, 2)
    out["post"] = f(np.repeat(np.concatenate([pos, pos], axis=1).reshape(2, 128, 64), 2, axis=2))
    w2c = np.stack([np.asarray(inp["cmp_k_w2"], np.float32), np.asarray(inp["cmp_v_w2"], np.float32)], axis=1)
    out["w2c"] = f(w2c.transpose(0, 2, 1, 3).reshape(2, 128, 128))
    pw = np.asarray(inp["pool_w"], np.float32)
    out["poolw"] = f(pw.transpose(0, 2, 1, 3).reshape(2, 128, 512))
    cw1 = np.asarray(inp["conv_w_pw1"], np.float32).reshape(2, 8, 128, 2048)
    out["cw1"] = f(cw1.transpose(0, 2, 1, 3).reshape(2, 128, 8 * 2048))
    cw2 = np.asarray(inp["conv_w_pw2"], np.float32).reshape(2, 8, 128, 1024)
    out["cw2"] = f(cw2.transpose(0, 2, 1, 3).reshape(2, 128, 8 * 1024))
    return out


_CACHE = {}


def run(inputs, layers=(0, 1, 2, 3), final=True, n_cores=8):
    key = (tuple(layers), final)
    if key not in _CACHE:
        pr = Prog(layers, final)
        _CACHE[key] = (pr.build(), set(pr.dins.keys()))
    nc, names = _CACHE[key]
    w = _prep_weights(inputs)
    x = np.asarray(inputs["x"], np.float32)
    in_maps = []
    for b in range(n_cores):
        m = {k: v for k, v in w.items() if k in names}
        m["xT"] = np.ascontiguousarray(x[b].T)
        in_maps.append(m)
    res = run_bass_kernel_spmd(nc, in_maps, core_ids=list(range(n_cores)))
    out = np.stack([np.ascontiguousarray(res.results[b]["yT"].T) for b in range(n_cores)], axis=0)
    return out.astype(np.float32)


def kernel(**inputs):
    return run(inputs)
```

```python
from contextlib import ExitStack, contextmanager
import numpy as np
import concourse.bass as bass
import concourse.mybir as mybir
from concourse.bass_utils import run_bass_kernel_spmd

F32 = mybir.dt.float32
BF16 = mybir.dt.bfloat16
AF = mybir.ActivationFunctionType
ALU = mybir.AluOpType

S = 4096
D = 1024
T = 512
NT = S // T
HID = 2816
HC = HID // 128
TG = 2048
NEGB = -30000.0
ENGS = ("pe", "act", "dve", "pool", "sp")
SLOPES = [2.0 ** (-(h + 1)) for h in range(8)]


class Buf:
    __slots__ = ("name", "writes", "reads", "dsem", "psum")

    def __init__(self, name, psum=False):
        self.name = name
        self.writes = {}
        self.reads = {}
        self.dsem = None
        self.psum = psum


class TB:
    __slots__ = ("t", "b")

    def __init__(self, t, b):
        self.t = t
        self.b = b


class _Rec:
    def __init__(self):
        self.call = None

    def __getattr__(self, name):
        def f(*a, **k):
            self.call = (name, a, k)
            return self
        return f


class FW:
    def __init__(self, nc, stack):
        self.nc = nc
        self.stack = stack
        self.ops = {e: [] for e in ENGS}
        self.sems = {}
        self.count = {}
        self.seen = {e: {} for e in ENGS}
        self.same_engine_sync = True
        self.mute = False
        self.ekey = {}
        self.epoch = 0
        for e in ENGS:
            self.ekey[e] = self._newsem("E_" + e)
        self.free_dsems = {'sw': [], 'hw': []}

    def _newsem(self, key):
        h = self.stack.enter_context(self.nc.semaphore(key))
        self.sems[key] = h
        self.count[key] = 0
        return key

    def _deps(self, eng, reads, writes):
        deps = {}
        own = self.ekey[eng]
        for b in reads:
            for k, v in b.writes.items():
                if deps.get(k, 0) < v:
                    deps[k] = v
            if b.psum:
                for k, v in b.reads.items():
                    if k != own and deps.get(k, 0) < v:
                        deps[k] = v
        for b in writes:
            for d in (b.writes, b.reads):
                for k, v in d.items():
                    if deps.get(k, 0) < v:
                        deps[k] = v
        out = []
        seen = self.seen[eng]
        for k, v in deps.items():
            if k == own and (eng == "pe" or not self.same_engine_sync):
                continue
            if seen.get(k, 0) >= v:
                continue
            seen[k] = v
            out.append((k, v))
        return out

    def _record(self, key, val, reads, writes):
        for b in reads:
            b.reads[key] = val
        for b in writes:
            b.reads = {}
            b.writes[key] = val

    def op(self, eng, fn, reads=(), writes=()):
        if self.mute:
            return
        waits = self._deps(eng, reads, writes)
        key = self.ekey[eng]
        self.count[key] += 1
        val = self.count[key]
        sems = self.sems
        rec = _Rec()
        fn(rec)
        cname, cargs, ckw = rec.call

        def emit(e, waits=waits, key=key, cname=cname, cargs=cargs, ckw=ckw):
            for k, v in waits:
                e.wait_ge(sems[k], v)
            getattr(e, cname)(*cargs, **ckw).then_inc(sems[key], 1)
        self.ops[eng].append(emit)
        self._record(key, val, reads, writes)

    def dma(self, eng, out_ap, in_ap, reads=(), writes=(), sem_buf=None):
        if self.mute:
            return
        waits = self._deps(eng, reads, writes)
        sb = sem_buf
        qk = "sw" if eng == "pool" else "hw"
        if sb.dsem is None:
            sb.dsem = {}
        if qk not in sb.dsem:
            if self.free_dsems[qk]:
                sb.dsem[qk] = self.free_dsems[qk].pop()
            else:
                sb.dsem[qk] = self._newsem("D%s%d" % (qk, len(self.sems)))
        key = sb.dsem[qk]
        self.count[key] += 16
        val = self.count[key]
        sems = self.sems

        def emit(e, waits=waits, key=key):
            for k, v in waits:
                e.wait_ge(sems[k], v)
            e.dma_start(out=out_ap, in_=in_ap).then_inc(sems[key], 16)
        self.ops[eng].append(emit)
        self._record(key, val, reads, writes)

    def barrier(self):
        allb = Buf("ALL")
        allb.writes = {k: v for k, v in self.count.items() if v > 0}
        sems = self.sems
        for eng in ENGS:
            waits = self._deps(eng, [allb], ())

            def emit(e, waits=waits):
                for k, v in waits:
                    e.wait_ge(sems[k], v)
            self.ops[eng].append(emit)

    def new_epoch(self):
        self.epoch += 1
        for e in ENGS:
            self.ekey[e] = self._newsem("E_%s_%d" % (e, self.epoch))

    def finish(self):
        nc = self.nc
        ops = self.ops
        with nc.Block() as block:
            @block.tensor
            def _(e):
                for f in ops["pe"]:
                    f(e)

            @block.scalar
            def _(e):
                for f in ops["act"]:
                    f(e)

            @block.vector
            def _(e):
                for f in ops["dve"]:
                    f(e)

            @block.gpsimd
            def _(e):
                for f in ops["pool"]:
                    f(e)

            @block.sync
            def _(e):
                for f in ops["sp"]:
                    f(e)


class Phase:
    def __init__(self, A, name):
        self.A = A
        self.name = name
        self.stack = ExitStack()
        self.bufs = []

    def sb(self, name, shape, dtype):
        t = self.stack.enter_context(self.A.nc.sbuf_tensor(self.name + "_" + name, shape, dtype))
        tb = TB(t, Buf(name))
        self.bufs.append(tb)
        return tb

    def close(self):
        self.A.fw.barrier()
        self.A.fw.new_epoch()
        for tb in self.bufs:
            if tb.b.dsem is not None:
                for qk, key in tb.b.dsem.items():
                    self.A.fw.free_dsems[qk].append(key)
                tb.b.dsem = None
        self.stack.close()


def _small_layout():
    off = {}
    c = 0

    def add(name, n):
        nonlocal c
        off[name] = c
        c += n
    add("normg", 9 * 8)
    for i in range(2):
        add("cb1_%d" % i, 16)
        add("cbdw_%d" % i, 8)
        add("clng_%d" % i, 8)
        add("clnb_%d" % i, 8)
        add("cb2_%d" % i, 8)
        add("cdw_%d" % i, 31 * 8)
        add("pscale_%d" % i, 4)
    add("bias_sw", 8 * 32)
    add("bias_cmp", 8 * 2 * 8)
    add("fbw", 126)
    add("rc15", 4 * 16)
    add("arow", 512)
    add("eps6", 1)
    add("eps5", 1)
    add("eps30", 1)
    add("eps18", 1)
    off["_n"] = c
    return off


SM = _small_layout()


def _bf_consts():
    off = {}
    c = 0

    def add(name, n):
        nonlocal c
        off[name] = c
        c += n
    add("caus", 896)
    add("far", 896)
    add("g1", 2048)
    add("ident", 128)
    add("ones", 128)
    add("ova", 2 * 65)
    add("eg", 24 * 64 + 64)
    off["_n"] = c
    return off


BC = _bf_consts()


def make_consts():
    sm = np.zeros((128, SM["_n"]), np.float32)
    p = np.arange(128)[:, None].astype(np.float64)
    for h in range(8):
        for di, delta in enumerate(range(-28, 4)):
            sm[:, SM["bias_sw"] + h * 32 + di] = (SLOPES[h] * (128 * delta + p[:, 0] - 256))
        for cc in range(2):
            for i in range(8):
                sm[:, SM["bias_cmp"] + (h * 2 + cc) * 8 + i] = SLOPES[h] * (16 * (128 * cc + p[:, 0]) + 15 - 512 * i - 256)
    hb = (np.arange(128) >= 64).astype(np.int64)[:, None]
    x = (np.arange(126) - 62)[None, :]
    fb = np.zeros((128, 126), np.float32)
    fb[(x == hb) | (x == hb - 1)] = 1e4
    fb[np.broadcast_to(x > hb, fb.shape)] = -1e30
    sm[:, SM["fbw"]:SM["fbw"] + 126] = fb
    for m, w in enumerate((2, 4, 8, 16)):
        t = np.arange(16)
        sm[:, SM["rc15"] + m * 16: SM["rc15"] + (m + 1) * 16] = (1.0 / np.minimum(t + 1.0, float(w)))[None, :]
    sm[:, SM["arow"]:SM["arow"] + 512] = (np.arange(512) - 256)[None, :]
    sm[:, SM["eps6"]] = 1024.0 * 1e-6
    sm[:, SM["eps5"]] = 1e-5
    sm[:, SM["eps30"]] = 1e-30
    sm[:, SM["eps18"]] = 1e-18

    bc = np.zeros((128, BC["_n"]), np.float32)
    xx = (np.arange(896) - 384)[None, :]
    pp = np.arange(128)[:, None]
    bc[:, BC["caus"]:BC["caus"] + 896] = np.where(xx >= pp, 0.0, NEGB)
    bc[:, BC["far"]:BC["far"] + 896] = np.where(xx < pp, 0.0, NEGB)
    x2 = np.arange(2048)[None, :]
    g = np.where(x2 >= 16 * pp + 15, 0.0, NEGB)
    bc[:, BC["g1"]:BC["g1"] + 2048] = g
    bc[:, BC["ident"]:BC["ident"] + 128] = np.eye(128)
    bc[:, BC["ones"]:BC["ones"] + 128] = 1.0
    ova = np.zeros((256, 65), np.float32)
    for n1 in range(1, 256):
        n = n1 - 1
        c0 = n * 16
        for j in range(64):
            s0 = j * 64
            ov = min(c0 + 32, s0 + 64) - max(c0, s0)
            if ov > 0:
                ova[n1, j] = ov / 32.0
        ova[n1, 64] = 1.0
    bc[:, BC["ova"]:BC["ova"] + 130] = ova.reshape(2, 128, 65).transpose(1, 0, 2).reshape(128, 130)
    eg = np.zeros((128, 24, 64), np.float32)
    for r in range(24):
        eg[r, r, :] = 1.0
    bc[:, BC["eg"]:BC["eg"] + 24 * 64] = eg.reshape(128, -1)
    ex = np.zeros((64, S), np.float32)
    for j in range(64):
        ex[j, j * 64:(j + 1) * 64] = 1.0
    e0 = np.zeros((64, S), np.float32)
    e0[0, :] = 1.0
    return sm, bc, ex, e0


class Prog:
    def __init__(self, layers=(0, 1, 2, 3), final=True, skip_ffn=False):
        self.layers = tuple(layers)
        self.final = final
        self.skip_ffn = skip_ffn
        self.nc = bass.Bass("TRN2", target_bir_lowering=False)
        self.stack = ExitStack()
        self.dins = {}

    def dram_in(self, name, shape):
        if name not in self.dins:
            self.dins[name] = self.nc.dram_tensor(name, list(shape), F32, kind="ExternalInput").ap()
        return self.dins[name]

    @property
    def d_wgu(self): return self.dram_in("wgu", (4, HC, 128, 2 * 8 * 128))
    @property
    def d_wd(self): return self.dram_in("wd", (4, 8, 128, HC * 128))
    @property
    def d_win(self): return self.dram_in("win", (2, 128, 8 * 1816))
    @property
    def d_wo(self): return self.dram_in("wo", (2, 128, 8 * 1024))
    @property
    def d_w1c(self): return self.dram_in("w1c", (2, 128, 2 * 2 * 32 * 128))
    @property
    def d_post(self): return self.dram_in("post", (2, 128, 2 * 32 * 2))
    @property
    def d_w2c(self): return self.dram_in("w2c", (2, 128, 2 * 64))
    @property
    def d_pw(self): return self.dram_in("poolw", (2, 128, 4 * 128))
    @property
    def d_cw1(self): return self.dram_in("cw1", (2, 128, 8 * 2048))
    @property
    def d_cw2(self): return self.dram_in("cw2", (2, 128, 8 * 1024))
    @property
    def d_ex(self): return self.dram_in("exrows", (64, S))
    @property
    def d_e0(self): return self.dram_in("e0rows", (64, S))

    @contextmanager
    def phase(self, name):
        ph = Phase(self, name)
        try:
            yield ph
        finally:
            ph.close()

    def build(self):
        nc = self.nc
        st = self.stack
        with st:
            self.fw = FW(nc, st)
            fw = self.fw
            self.x_in = self.dram_in("xT", (D, S))
            self.y_out = nc.dram_tensor("yT", [D, S], F32, kind="ExternalOutput").ap()
            self.xs = nc.dram_tensor("xs", [D, S], F32).ap()
            self.d_sm = self.dram_in("smalls", (128, SM["_n"]))
            self.d_bc = self.dram_in("bconst", (128, BC["_n"]))
            self.xs_b = [Buf("xs%d" % i) for i in range(NT)]
            self.xin_b = Buf("x_in")
            self.y_b = [Buf("y%d" % i) for i in range(NT)]
            sm_t = st.enter_context(nc.sbuf_tensor("smalls_sb", [128, SM["_n"]], F32))
            self.sm = TB(sm_t, Buf("smalls"))
            bc_t = st.enter_context(nc.sbuf_tensor("bconst_sb", [128, BC["_n"]], BF16))
            self.bc = TB(bc_t, Buf("bconst"))
            idf_t = st.enter_context(nc.sbuf_tensor("identf", [128, 128], F32))
            self.identf = TB(idf_t, Buf("identf"))
            self.ps = []
            for i in range(8):
                t = st.enter_context(nc.psum_tensor("ps%d" % i, [128, 512], F32))
                self.ps.append(TB(t, Buf("ps%d" % i, psum=True)))
            fw.dma("sp", sm_t[:], self.d_sm[:], writes=[self.sm.b], sem_buf=self.sm.b)
            half = BC["_n"] // 2
            fw.dma("pool", bc_t[:, 0:half], self.d_bc[:, 0:half], writes=[self.bc.b], sem_buf=self.bc.b)
            fw.dma("pool", bc_t[:, half:], self.d_bc[:, half:], writes=[self.bc.b], sem_buf=self.bc.b)
            ng = sm_t[:, SM["normg"]:SM["normg"] + 72]
            fw.op("dve", lambda e: e.tensor_scalar(ng, ng, 32.0, None, ALU.mult), writes=[self.sm.b])
            fw.op("dve", lambda e: e.tensor_copy(idf_t[:], bc_t[:, BC["ident"]:BC["ident"] + 128]),
                  reads=[self.bc.b], writes=[self.identf.b])
            fw.barrier()

            first = True
            for l in self.layers:
                src = (self.x_in, [self.xin_b] * NT) if first else (self.xs, self.xs_b)
                first = False
                if l % 2 == 0:
                    self.even_layer(l, src)
                else:
                    self.conv_layer(l, src)
                if not self.skip_ffn:
                    self.ffn_layer(l)
            if self.final:
                src = (self.x_in, [self.xin_b] * NT) if first else (self.xs, self.xs_b)
                self.final_norm(src)
            fw.barrier()
            fw.finish()
        return nc

    def tile_ap(self, dram, i):
        return dram.rearrange("(c p) t -> p c t", p=128)[:, :, i * T:(i + 1) * T]

    def ones_bf(self):
        return self.bc.t[:, BC["ones"]:BC["ones"] + 128]

    def ident_bf(self):
        return self.bc.t[:, BC["ident"]:BC["ident"] + 128]

    def smc(self, name, j=0, n=1):
        o = SM[name] + j
        return self.sm.t[:, o:o + n]

    def rmsnorm(self, x, h, sq, rstd, ps, gidx, sqc=8):
        fw = self.fw
        for q in range(8 // sqc):
            fw.op("act", lambda e: e.activation(sq.t[:, 0:sqc, :], x.t[:, q * sqc:(q + 1) * sqc, :], AF.Square), reads=[x.b], writes=[sq.b])
            for kk in range(sqc):
                k = q * sqc + kk
                fw.op("pe", lambda e: e.matmul(ps.t[:], self.ones_bf(), sq.t[:, kk, :], start=(k == 0), stop=(k == 7)),
                      reads=[sq.b, self.bc.b], writes=[ps.b])
        fw.op("act", lambda e: e.activation(rstd.t[:], ps.t[:], AF.Sqrt, bias=self.smc("eps6"), scale=1.0),
              reads=[ps.b, self.sm.b], writes=[rstd.b])
        fw.op("dve", lambda e: e.reciprocal(rstd.t[:], rstd.t[:]), reads=[rstd.b], writes=[rstd.b])
        hap, hb = h
        for k in range(8):
            fw.op("dve", lambda e, k=k: e.scalar_tensor_tensor(hap(k), x.t[:, k, :], self.smc("normg", gidx * 8 + k),
                                                             rstd.t[:], ALU.mult, ALU.mult),
                  reads=[x.b, rstd.b, self.sm.b], writes=[hb])

    def ffn_layer(self, l):
        fw = self.fw
        ntg = TG // T
        ngrp = S // TG
        with self.phase("ffn%d" % l) as ph:
            hT = ph.sb("hT", [128, 8, TG], BF16)
            hT_b = [Buf("hT%d" % i) for i in range(ntg)]
            act = ph.sb("act", [128, HC, TG], BF16)
            act_b = [Buf("act%d" % i) for i in range(ntg)]
            xin = [ph.sb("xin%d" % i, [128, 8, T], F32) for i in range(2)]
            sq = ph.sb("sq", [128, 4, T], BF16)
            rstd = ph.sb("rstd", [128, T], F32)
            wgu = [ph.sb("wgu%d" % i, [128, 2, 8, 128], BF16) for i in range(3)]
            wd = [ph.sb("wd%d" % i, [128, HC, 128], BF16) for i in range(2)]
            sg = [ph.sb("sg%d" % i, [128, T], F32) for i in range(2)]
            xc = [ph.sb("xc%d" % i, [128, T], F32) for i in range(2)]
            ps = self.ps
            ncnt = [0]

            def load_wgu(hc):
                w = wgu[hc % 3]
                fw.dma("pool", w.t[:].rearrange("p a k c -> p (a k c)"), self.d_wgu[l, hc], writes=[w.b], sem_buf=w.b)

            def load_wd(m):
                w = wd[m % 2]
                fw.dma("pool", w.t[:].rearrange("p k c -> p (k c)"), self.d_wd[l, m], writes=[w.b], sem_buf=w.b)

            def norm_tile(gi, tl):
                ti = gi * ntg + tl
                x = xin[ncnt[0] % 2]
                ncnt[0] += 1
                fw.dma("sp", x.t[:], self.tile_ap(self.xs, ti), reads=[self.xs_b[ti]], writes=[x.b], sem_buf=x.b)
                self.rmsnorm(x, (lambda k, tl=tl: hT.t[:, k, tl * T:(tl + 1) * T], hT_b[tl]), sq, rstd, ps[6], 4 + l, sqc=4)

            load_wgu(0)
            load_wgu(1)
            for tl in range(ntg):
                norm_tile(0, tl)
            for gi in range(ngrp):
                if gi > 0:
                    load_wgu(0)
                    load_wgu(1)
                it = 0
                for hc in range(HC):
                    if hc + 2 < HC:
                        load_wgu(hc + 2)
                    elif hc + 2 == HC:
                        load_wd(0)
                    w = wgu[hc % 3]
                    for tl in range(ntg):
                        pg = ps[it % 2]
                        pu = ps[2 + it % 2]
                        s_ = sg[it % 2]
                        it += 1
                        for a_, pp in ((0, pg), (1, pu)):
                            for k in range(8):
                                fw.op("pe", lambda e: e.matmul(
                                    pp.t[:], w.t[:, a_, k, :], hT.t[:, k, tl * T:(tl + 1) * T], start=(k == 0), stop=(k == 7)),
                                    reads=[w.b, hT_b[tl]], writes=[pp.b])
                        fw.op("act", lambda e: e.activation(s_.t[:], pg.t[:], AF.Silu), reads=[pg.b], writes=[s_.b])
                        fw.op("dve", lambda e: e.tensor_tensor(
                            act.t[:, hc, tl * T:(tl + 1) * T], pu.t[:], s_.t[:], ALU.mult),
                            reads=[pu.b, s_.b], writes=[act_b[tl]])
                it = 0
                for m in range(8):
                    if m + 1 < 8:
                        load_wd(m + 1)
                    w = wd[m % 2]
                    for tl in range(ntg):
                        ti = gi * ntg + tl
                        po = ps[4 + it % 2]
                        x = xc[it % 2]
                        it += 1
                        cap = self.xs[m * 128:(m + 1) * 128, ti * T:(ti + 1) * T]
                        fw.dma("sp", x.t[:], cap, reads=[self.xs_b[ti]], writes=[x.b], sem_buf=x.b)
                        for k in range(HC):
                            fw.op("pe", lambda e: e.matmul(
                                po.t[:], w.t[:, k, :], act.t[:, k, tl * T:(tl + 1) * T], start=(k == 0), stop=(k == HC - 1)),
                                reads=[w.b, act_b[tl]], writes=[po.b])
                        fw.op("dve", lambda e: e.tensor_tensor(x.t[:], po.t[:], x.t[:], ALU.add),
                              reads=[po.b, x.b], writes=[x.b])
                        fw.dma("sp", cap, x.t[:], reads=[x.b], writes=[self.xs_b[ti]], sem_buf=x.b)
                    if gi + 1 < ngrp and m % 2 == 1 and m // 2 < ntg:
                        norm_tile(gi + 1, m // 2)

    def final_norm(self, src):
        fw = self.fw
        sdram, sb = src
        with self.phase("fin") as ph:
            xin = [ph.sb("xin%d" % i, [128, 8, T], F32) for i in range(2)]
            yo = [ph.sb("yo%d" % i, [128, 8, T], F32) for i in range(2)]
            sq = ph.sb("sq", [128, 8, T], BF16)
            rstd = ph.sb("rstd", [128, T], F32)
            for ti in range(NT):
                x = xin[ti % 2]
                y = yo[ti % 2]
                fw.dma("sp", x.t[:], self.tile_ap(sdram, ti), reads=[sb[ti]], writes=[x.b], sem_buf=x.b)
                if self.final == "copy":
                    fw.dma("sp", self.tile_ap(self.y_out, ti), x.t[:], reads=[x.b], writes=[self.y_b[ti]], sem_buf=x.b)
                    continue
                self.rmsnorm(x, (lambda k, y=y: y.t[:, k, :], y.b), sq, rstd, self.ps[6], 8)
                fw.dma("sp", self.tile_ap(self.y_out, ti), y.t[:], reads=[y.b], writes=[self.y_b[ti]], sem_buf=y.b)

    def conv_layer(self, l, src):
        fw = self.fw
        i2 = l // 2
        sdram, sbufs = src
        ps = self.ps
        with self.phase("conv%d" % l) as ph:
            w1 = ph.sb("w1", [128, 8, 2048], BF16)
            w2 = ph.sb("w2", [128, 8, 1024], BF16)
            dg = ph.sb("dg", [128, 31 * 8, 128], BF16)
            dg_b = [Buf("dg0"), Buf("dg1")]
            xin = [ph.sb("xin0", [128, 8, T], F32)] * 2
            h = ph.sb("h", [128, 8, T], BF16)
            sq = h
            rstd = ph.sb("rstd", [128, T], F32)
            z = [ph.sb("z%d" % i, [128, 8, 30 + T], BF16) for i in range(2)]
            sg = [ph.sb("sg%d" % i, [128, T], F32) for i in range(2)]
            y = ph.sb("y", [128, 8, T], F32)
            ybf = h
            mean = ph.sb("mean", [128, T], F32)
            msq = ph.sb("msq", [128, T], F32)
            lrs = ph.sb("lrs", [128, T], F32)
            yn = sg
            s_bf = ph.sb("s_bf", [128, 8, T], BF16)
            ysq = s_bf
            for q in range(4):
                fw.dma("pool", w1.t[:, 2 * q:2 * q + 2, :].rearrange("p k c -> p (k c)"),
                       self.d_cw1[i2, :, 2 * q * 2048:(2 * q + 2) * 2048], writes=[w1.b], sem_buf=w1.b)
            for q in range(2):
                fw.dma("pool", w2.t[:, 4 * q:4 * q + 4, :].rearrange("p k c -> p (k c)"),
                       self.d_cw2[i2, :, 4 * q * 1024:(4 * q + 4) * 1024], writes=[w2.b], sem_buf=w2.b)
            for j in range(31 * 8):
                if j % 2 == 0:
                    fw.op("dve", lambda e, j=j: e.tensor_scalar(dg.t[:, j, :], self.identf.t[:], self.smc("cdw_%d" % i2, j), None, ALU.mult),
                          reads=[self.identf.b, self.sm.b], writes=[dg_b[0]])
                else:
                    fw.op("act", lambda e, j=j: e.activation(dg.t[:, j, :], self.identf.t[:], AF.Identity, scale=self.smc("cdw_%d" % i2, j)),
                          reads=[self.identf.b, self.sm.b], writes=[dg_b[1]])
            fw.op("pool", lambda e: e.memset(z[0].t[:, :, 0:30], 0.0), writes=[z[0].b])
            x = xin[0]
            y_b = [Buf("y%d" % m) for m in range(8)]
            sgl = sg[0]
            yns = [sg[1], msq]
            p1, p2 = ps[6], ps[7]

            def do_norm(ti):
                if ti == 0:
                    fw.dma("sp", x.t[:], self.tile_ap(sdram, ti), reads=[sbufs[ti]], writes=[x.b], sem_buf=x.b)
                self.rmsnorm(x, (lambda k: h.t[:, k, :], h.b), sq, rstd, ps[6], l)
                if ti + 1 < NT:
                    fw.dma("sp", x.t[:], self.tile_ap(sdram, ti + 1), reads=[sbufs[ti + 1]], writes=[x.b], sem_buf=x.b)

            def pw1_chunk(ti, m):
                zc = z[ti % 2]
                pa = ps[m % 2]
                pg = ps[2 + m % 2]
                for a_, pp in ((0, pa), (1, pg)):
                    for k in range(8):
                        fw.op("pe", lambda e: e.matmul(
                            pp.t[:], w1.t[:, k, a_ * 1024 + m * 128: a_ * 1024 + (m + 1) * 128], h.t[:, k, :],
                            start=(k == 0), stop=(k == 7)), reads=[w1.b, h.b], writes=[pp.b])
                return pa, pg

            def glu_chunk(ti, m, pa, pg):
                zc = z[ti % 2]
                fw.op("act", lambda e: e.activation(sgl.t[:], pg.t[:], AF.Sigmoid, bias=self.smc("cb1_%d" % i2, 8 + m)),
                      reads=[pg.b, self.sm.b], writes=[sgl.b])
                fw.op("dve", lambda e: e.scalar_tensor_tensor(
                    zc.t[:, m, 30:30 + T], pa.t[:], self.smc("cb1_%d" % i2, m), sgl.t[:], ALU.add, ALU.mult),
                    reads=[pa.b, sgl.b, self.sm.b], writes=[zc.b])

            def ln_chunk(ti, m):
                yy = yns[m % 2]
                fw.op("dve", lambda e: e.tensor_tensor(yy.t[:], y.t[:, m, :], mean.t[:], ALU.subtract),
                      reads=[y_b[m], mean.b], writes=[yy.b])
                fw.op("dve", lambda e: e.tensor_tensor(yy.t[:], yy.t[:], lrs.t[:], ALU.mult),
                      reads=[yy.b, lrs.b], writes=[yy.b])
                fw.op("act", lambda e: e.activation(s_bf.t[:, m, :], yy.t[:], AF.Silu,
                                                    bias=self.smc("clnb_%d" % i2, m), scale=self.smc("clng_%d" % i2, m)),
                      reads=[yy.b, self.sm.b], writes=[s_bf.b])
                src_c = sdram[m * 128:(m + 1) * 128, ti * T:(ti + 1) * T]
                fw.dma("sp", y.t[:, m, :], src_c, reads=[sbufs[ti]], writes=[y_b[m]], sem_buf=y_b[m])

            do_norm(0)
            for m in range(8):
                pa, pg = pw1_chunk(0, m)
                glu_chunk(0, m, pa, pg)
            for ti in range(NT):
                zc = z[ti % 2]
                zn = z[(ti + 1) % 2]
                last = (ti + 1 == NT)
                if not last:
                    fw.op("pool", lambda e: e.tensor_copy(zn.t[:, :, 0:30], zc.t[:, :, T:T + 30]), reads=[zc.b], writes=[zn.b])
                for m in range(8):
                    pc = ps[4 + m % 2]
                    for k in range(31):
                        fw.op("pe", lambda e: e.matmul(
                            pc.t[:], dg.t[:, k * 8 + m, :], zc.t[:, m, k:k + T], start=(k == 0), stop=(k == 30)),
                            reads=[dg_b[0], dg_b[1], zc.b], writes=[pc.b])
                    fw.op("act", lambda e: e.activation(y.t[:, m, :], pc.t[:], AF.Identity, bias=self.smc("cbdw_%d" % i2, m)),
                          reads=[pc.b, self.sm.b], writes=[y_b[m]])
                fw.op("dve", lambda e: e.tensor_copy(ybf.t[:], y.t[:]), reads=y_b, writes=[ybf.b])
                fw.op("act", lambda e: e.activation(ysq.t[:], y.t[:], AF.Square), reads=y_b, writes=[ysq.b])
                for k in range(8):
                    fw.op("pe", lambda e: e.matmul(p1.t[:], self.ones_bf(), ybf.t[:, k, :], start=(k == 0), stop=(k == 7)),
                          reads=[ybf.b, self.bc.b], writes=[p1.b])
                for k in range(8):
                    fw.op("pe", lambda e: e.matmul(p2.t[:], self.ones_bf(), ysq.t[:, k, :], start=(k == 0), stop=(k == 7)),
                          reads=[ysq.b, self.bc.b], writes=[p2.b])
                fw.op("dve", lambda e: e.tensor_scalar(mean.t[:], p1.t[:], 1.0 / 1024, None, ALU.mult), reads=[p1.b], writes=[mean.b])
                fw.op("dve", lambda e: e.tensor_tensor(lrs.t[:], mean.t[:], mean.t[:], ALU.mult), reads=[mean.b], writes=[lrs.b])
                fw.op("dve", lambda e: e.scalar_tensor_tensor(lrs.t[:], p2.t[:], 1.0 / 1024, lrs.t[:], ALU.mult, ALU.subtract),
                      reads=[p2.b, lrs.b], writes=[lrs.b])
                fw.op("act", lambda e: e.activation(lrs.t[:], lrs.t[:], AF.Sqrt, bias=self.smc("eps5"), scale=1.0),
                      reads=[lrs.b, self.sm.b], writes=[lrs.b])
                fw.op("dve", lambda e: e.reciprocal(lrs.t[:], lrs.t[:]), reads=[lrs.b], writes=[lrs.b])
                if not last:
                    do_norm(ti + 1)
                for m in range(8):
                    if not last:
                        pa, pg = pw1_chunk(ti + 1, m)
                    ln_chunk(ti, m)
                    if not last:
                        glu_chunk(ti + 1, m, pa, pg)
                for m in range(8):
                    po = ps[m % 2]
                    for k in range(8):
                        fw.op("pe", lambda e: e.matmul(po.t[:], w2.t[:, k, m * 128:(m + 1) * 128], s_bf.t[:, k, :],
                                                       start=(k == 0), stop=(k == 7)),
                              reads=[w2.b, s_bf.b], writes=[po.b])
                    dst_c = self.xs[m * 128:(m + 1) * 128, ti * T:(ti + 1) * T]
                    fw.op("dve", lambda e: e.scalar_tensor_tensor(
                        y.t[:, m, :], po.t[:], self.smc("cb2_%d" % i2, m), y.t[:, m, :], ALU.add, ALU.add),
                        reads=[po.b, y_b[m], self.sm.b], writes=[y_b[m]])
                    fw.dma("sp", dst_c, y.t[:, m, :], reads=[y_b[m]], writes=[self.xs_b[ti]], sem_buf=y_b[m])

    def even_layer(self, l, src):
        fw = self.fw
        import os
        STAGE = int(os.environ.get('EVEN_STAGE', '9'))
        SUB = os.environ.get('EVEN_SUB', 'NQMKGVP')
        BR = os.environ.get('EVEN_BR', 'csw')
        HEADS = [int(c) for c in os.environ.get('EVEN_HEADS', '01234567')]
        i2 = l // 2
        sdram, sbufs = src
        ps = self.ps
        bc = self.bc
        with self.phase("nsa%d" % l) as ph:
            win = ph.sb("win", [128, 8, 1816], BF16)
            wo = ph.sb("wo", [128, 8, 1024], BF16)
            w1c = ph.sb("w1c", [128, 2, 2, 32, 128], BF16)
            post = ph.sb("post", [128, 2, 32, 2], BF16)
            w2c = ph.sb("w2c", [128, 2, 64], BF16)
            pw = ph.sb("pw", [128, 4, 128], BF16)
            KS = [ph.sb("ks%d" % g, [128, S], BF16) for g in range(2)]
            KW = [ph.sb("kw%d" % g, [128, 1024], BF16) for g in range(2)]
            VS = ph.sb("vs", [128, 2, 32, 128], BF16)
            VW = ph.sb("vw", [128, 2, 8, 128], BF16)
            KCT = ph.sb("kct", [128, 16 + T], BF16)
            VCT = ph.sb("vct", [128, 16 + T], BF16)
            KC = [ph.sb("kc%d" % g, [128, 256], BF16) for g in range(2)]
            VC = ph.sb("vc", [128, 2, 2, 128], BF16)
            Qs = [ph.sb("q%d" % h, [128, T], BF16) for h in range(8)]
            x = ph.sb("x", [128, 8, T], F32)
            h_ = ph.sb("h", [128, 8, T], BF16)
            sgT = ph.sb("sgT", [128, T], BF16)
            pa = [ph.sb("pa%d" % i, [128, 16 + T], F32) for i in range(3)]
            rstd = TB(pa[2].t[:, 0:T], pa[2].b)
            halo = ph.sb("halo", [128, 4, 16], F32)
            dT = ph.sb("dT", [128, 4, T], BF16)
            yp = ph.sb("yp", [128, 4, T], BF16)
            oT = ph.sb("oT", [128, 4, T], BF16)
            E = [ph.sb("E%d" % i, [128, T], BF16) for i in range(3)]
            zc = ph.sb("zc", [128, T], F32)
            gz = ph.sb("gz", [128, T], F32)
            acc2 = [ph.sb("acc%d" % i, [128, T], F32) for i in range(2)]
            zc_b = [Buf("zc0"), Buf("zc1")]
            gz_b = [Buf("gz0"), Buf("gz1")]
            acc_b = [Buf("acc%d" % i) for i in range(4)]
            impa = ph.sb("impa", [128, 4, 64], F32)
            impf = ph.sb("impf", [128, 64], F32)
            top8 = ph.sb("top8", [128, 8], F32)
            selb = [ph.sb("selb0", [128, 128], F32)] * 2
            rz4 = ph.sb("rz4", [128, 4], F32)
            posb = ph.sb("posb", [128, 2], F32)
            gus = [ph.sb("gu%d" % i, [128, 32], F32) for i in range(4)]
            gvs = [ph.sb("gv%d" % i, [128, 32], F32) for i in range(4)]
            gss = [ph.sb("gs%d" % i, [128, 32], F32) for i in range(4)]
            Gs = [ph.sb("G%d" % i, [128, 128], BF16) for i in range(4)]
            gu = gus[0]
            for q in range(4):
                fw.dma("pool", win.t[:, 2 * q:2 * q + 2, :].rearrange("p k c -> p (k c)"),
                       self.d_win[i2, :, 2 * q * 1816:(2 * q + 2) * 1816], writes=[win.b], sem_buf=win.b)
            for q in range(2):
                fw.dma("pool", wo.t[:, 4 * q:4 * q + 4, :].rearrange("p k c -> p (k c)"),
                       self.d_wo[i2, :, 4 * q * 1024:(4 * q + 4) * 1024], writes=[wo.b], sem_buf=wo.b)
            for g_ in range(2):
                for s_ in range(2):
                    for q in range(2):
                        o_ = ((g_ * 2 + s_) * 32 + 16 * q) * 128
                        fw.dma("pool", w1c.t[:, g_, s_, 16 * q:16 * q + 16, :].rearrange("p l c -> p (l c)"),
                               self.d_w1c[i2, :, o_:o_ + 2048], writes=[w1c.b], sem_buf=w1c.b)
            fw.dma("pool", post.t[:].rearrange("p s l r -> p (s l r)"), self.d_post[i2], writes=[post.b], sem_buf=post.b)
            fw.dma("pool", w2c.t[:].rearrange("p s d -> p (s d)"), self.d_w2c[i2], writes=[w2c.b], sem_buf=w2c.b)
            fw.dma("pool", pw.t[:].rearrange("p m e -> p (m e)"), self.d_pw[i2], writes=[pw.b], sem_buf=pw.b)
            for g in range(2):
                fw.dma("pool", KS[g].t[64:128, :], self.d_ex[:, :], writes=[KS[g].b], sem_buf=KS[g].b)
                fw.dma("pool", KW[g].t[64:128, :], self.d_e0[:, 0:1024], writes=[KW[g].b], sem_buf=KW[g].b)
                fw.dma("pool", KC[g].t[64:128, :], self.d_e0[:, 0:256], writes=[KC[g].b], sem_buf=KC[g].b)
                fw.op("dve", lambda e, g=g: e.memset(KC[g].t[0:64, :], 0.0), writes=[KC[g].b])
            fw.op("dve", lambda e: e.memset(VS.t[:, :, :, 64:128], 1.0), writes=[VS.b])
            fw.op("dve", lambda e: e.memset(VW.t[:, :, :, 64:128], 1.0), writes=[VW.b])
            fw.op("dve", lambda e: e.memset(VC.t[:, :, :, 0:64], 0.0), writes=[VC.b])
            fw.op("dve", lambda e: e.memset(VC.t[:, :, :, 64:128], 1.0), writes=[VC.b])
            fw.op("dve", lambda e: e.memset(KCT.t[:, 0:16], 0.0), writes=[KCT.b])
            fw.op("dve", lambda e: e.memset(VCT.t[:, 0:16], 0.0), writes=[VCT.b])
            fw.op("dve", lambda e: e.memset(halo.t[:], 0.0), writes=[halo.b])
            fw.op("dve", lambda e: e.memset(oT.t[:], 0.0), writes=[oT.b])
            for G_ in Gs:
                fw.op("dve", lambda e: e.memset(G_.t[:], 0.0), writes=[G_.b])
            fw.op("dve", lambda e: e.memset(selb[0].t[:], 0.0), writes=[selb[0].b])
            for s_ in range(2):
                pp = ps[6 + s_]
                for ll in range(32):
                    fw.op("pe", lambda e, s_=s_, ll=ll, pp=pp: e.matmul(pp.t[:, 0:2], w1c.t[0:64, 0, s_, ll, :], post.t[0:64, s_, ll, :],
                                                                      start=(ll == 0), stop=(ll == 31)),
                          reads=[w1c.b, post.b], writes=[pp.b])
                fw.op("dve", lambda e, s_=s_, pp=pp: e.tensor_copy(posb.t[:, s_:s_ + 1], pp.t[:, 0:1]), reads=[pp.b], writes=[posb.b])

            fw.barrier()
            gen_i = [0]

            def gbank():
                gen_i[0] += 1
                return ps[6 + gen_i[0] % 2]

            ecnt = [0]
            import collections
            pend = collections.deque()
            scnt = [0]
            ocnt = [0]

            fw.op("dve", lambda e: e.memset(yp.t[:], 0.0), writes=[yp.b])
            for i in range(NT):
                c0 = i * T
                fw.mute = False
                if i == 0:
                    fw.dma("sp", x.t[:], self.tile_ap(sdram, i), reads=[sbufs[i]], writes=[x.b], sem_buf=x.b)
                fw.mute = STAGE < 1 or 'N' not in SUB
                self.rmsnorm(x, (lambda k: h_.t[:, k, :], h_.b), h_, rstd, ps[5], l)
                fw.mute = False
                if i + 1 < NT:
                    fw.dma("sp", x.t[:], self.tile_ap(sdram, i + 1), reads=[sbufs[i + 1]], writes=[x.b], sem_buf=x.b)

                def proj(col0, ncol):
                    pp = gbank()
                    for k in range(8):
                        fw.op("pe", lambda e, k=k, pp=pp: e.matmul(pp.t[0:ncol, :], win.t[:, k, col0:col0 + ncol], h_.t[:, k, :],
                                                                  start=(k == 0), stop=(k == 7)),
                              reads=[win.b, h_.b], writes=[pp.b])
                    return pp
                fw.mute = STAGE < 1 or 'Q' not in SUB
                for m in range(4):
                    pp = proj(m * 128, 128)
                    ha, hb2 = 2 * m, 2 * m + 1
                    fw.op("act", lambda e, pp=pp, ha=ha: e.activation(Qs[ha].t[0:64, :], pp.t[0:64, :], AF.Identity, scale=0.125),
                          reads=[pp.b], writes=[Qs[ha].b])
                    fw.op("dve", lambda e, pp=pp, hb2=hb2: e.tensor_scalar(Qs[hb2].t[0:64, :], pp.t[64:128, :], 0.125, None, ALU.mult),
                          reads=[pp.b], writes=[Qs[hb2].b])
                fw.mute = STAGE < 1 or 'M' not in SUB
                for hh in range(8):
                    fw.op("dve", lambda e, hh=hh: e.tensor_scalar(Qs[hh].t[64:128, :], self.smc("arow", 0, 512)[64:128, :], -SLOPES[hh], None, ALU.mult),
                          reads=[self.sm.b], writes=[Qs[hh].b])
                fw.mute = STAGE < 1 or 'K' not in SUB
                pp = proj(512, 128)
                fw.op("act", lambda e, pp=pp: e.activation(KCT.t[:, 16:16 + T], pp.t[:], AF.Identity), reads=[pp.b], writes=[KCT.b])
                pp = proj(640, 128)
                fw.op("dve", lambda e, pp=pp: e.tensor_copy(VCT.t[:, 16:16 + T], pp.t[:]), reads=[pp.b], writes=[VCT.b])
                pp = proj(768, 128)
                fw.op("act", lambda e, pp=pp: e.activation(KS[0].t[0:64, c0:c0 + T], pp.t[0:64, :], AF.Identity), reads=[pp.b], writes=[KS[0].b])
                fw.op("dve", lambda e, pp=pp: e.tensor_copy(KS[1].t[0:64, c0:c0 + T], pp.t[64:128, :]), reads=[pp.b], writes=[KS[1].b])
                pp = proj(896, 128)
                r0 = (i % 2) * T
                fw.op("act", lambda e, pp=pp: e.activation(KW[0].t[0:64, r0:r0 + T], pp.t[0:64, :], AF.Identity), reads=[pp.b], writes=[KW[0].b])
                fw.op("dve", lambda e, pp=pp: e.tensor_copy(KW[1].t[0:64, r0:r0 + T], pp.t[64:128, :]), reads=[pp.b], writes=[KW[1].b])
                fw.mute = STAGE < 1 or 'G' not in SUB
                pp = proj(1024, 128)
                fw.op("act", lambda e, pp=pp: e.activation(sgT.t[:], pp.t[:], AF.Sigmoid), reads=[pp.b], writes=[sgT.b])
                fw.mute = STAGE < 1 or 'V' not in SUB
                for s_ in range(4):
                    cs = 4 * i + s_
                    for half in range(2):
                        pp = gbank()
                        for k in range(8):
                            fw.op("pe", lambda e, k=k, pp=pp, s_=s_, half=half: e.matmul(
                                pp.t[:, 0:128], h_.t[:, k, s_ * 128:(s_ + 1) * 128],
                                win.t[:, k, 1560 + half * 128:1560 + (half + 1) * 128], start=(k == 0), stop=(k == 7)),
                                reads=[win.b, h_.b], writes=[pp.b])
                        if half == 0:
                            fw.op("act", lambda e, pp=pp, cs=cs: e.activation(VS.t[:, :, cs, 0:64], pp.t[:, 0:128].rearrange("p (g d) -> p g d", g=2), AF.Identity),
                                  reads=[pp.b], writes=[VS.b])
                        else:
                            fw.op("dve", lambda e, pp=pp, cs=cs: e.tensor_copy(VW.t[:, :, cs % 8, 0:64], pp.t[:, 0:128].rearrange("p (g d) -> p g d", g=2)),
                                  reads=[pp.b], writes=[VW.b])
                fw.mute = STAGE < 1 or 'P' not in SUB
                for m in range(4):
                    w_ = 2 ** (m + 1)
                    pp = proj(1048 + m * 128, 128)
                    A_, B_, C_ = pa
                    fw.op("act", lambda e, pp=pp: e.activation(A_.t[:, 16:16 + T], pp.t[:], AF.Identity), reads=[pp.b], writes=[A_.b])
                    fw.op("dve", lambda e, m=m: e.tensor_copy(A_.t[:, 0:16], halo.t[:, m, :]), reads=[halo.b], writes=[A_.b])
                    fw.op("dve", lambda e, m=m: e.tensor_copy(halo.t[:, m, :], A_.t[:, T:T + 16]), reads=[A_.b], writes=[halo.b])
                    cur = A_
                    for st_ in range(m + 1):
                        sh = 2 ** st_
                        lo = 2 ** (st_ + 1) - 1
                        dst = B_ if st_ % 2 == 0 else C_
                        fw.op("dve", lambda e, cur=cur, dst=dst, sh=sh, lo=lo: e.tensor_tensor(
                            dst.t[:, lo:16 + T], cur.t[:, lo:16 + T], cur.t[:, lo - sh:16 + T - sh], ALU.add),
                            reads=[cur.b], writes=[dst.b])
                        cur = dst
                    fw.op("dve", lambda e, cur=cur, m=m, w_=w_: e.scalar_tensor_tensor(dT.t[:, m, :], cur.t[:, 16:16 + T], 1.0 / w_, A_.t[:, 16:16 + T],
                                                                                     ALU.mult, ALU.subtract),
                          reads=[cur.b, A_.b], writes=[dT.b])
                    if i == 0:
                        fw.op("dve", lambda e, cur=cur, m=m: e.tensor_tensor(gu.t[:, 0:16], cur.t[:, 16:32], self.smc("rc15", m * 16, 16), ALU.mult),
                              reads=[cur.b, self.sm.b], writes=[gu.b])
                        fw.op("dve", lambda e, m=m: e.tensor_tensor(dT.t[:, m, 0:16], gu.t[:, 0:16], A_.t[:, 16:32], ALU.subtract),
                              reads=[gu.b, A_.b], writes=[dT.b])
                    pq = gbank()
                    fw.op("pe", lambda e, pq=pq, m=m: e.matmul(pq.t[:], pw.t[:, m, :], dT.t[:, m, :], start=True, stop=True),
                          reads=[pw.b, dT.b], writes=[pq.b])
                    fw.op("act", lambda e, pq=pq, m=m: e.activation(yp.t[:, m, :], pq.t[:], AF.Identity, scale=self.smc("pscale_%d" % i2, m)),
                          reads=[pq.b, self.sm.b], writes=[yp.b])
                fw.mute = STAGE < 2
                ccn = (32 * i) // 128
                pn = (32 * i) % 128
                for s_ in range(2):
                    XT = KCT if s_ == 0 else VCT
                    for g in range(2):
                        gu, gv, gs, G = gus[s_ * 2 + g], gvs[s_ * 2 + g], gss[s_ * 2 + g], Gs[s_ * 2 + g]
                        pp = gbank()
                        for ll in range(32):
                            fw.op("pe", lambda e, ll=ll, pp=pp, g=g, s_=s_, XT=XT: e.matmul(
                                pp.t[:, 0:32], w1c.t[:, g, s_, ll, :], XT.t[:, ll:ll + 497:16],
                                start=(ll == 0), stop=(ll == 31)), reads=[w1c.b, XT.b], writes=[pp.b])
                        fw.op("act", lambda e, pp=pp, s_=s_: e.activation(gu.t[:], pp.t[:, 0:32], AF.Identity, bias=posb.t[:, s_:s_ + 1]),
                              reads=[pp.b, posb.b], writes=[gu.b])
                        fw.op("dve", lambda e: e.tensor_tensor(gv.t[:], gu.t[:], gu.t[:], ALU.mult), reads=[gu.b], writes=[gv.b])
                        fw.op("dve", lambda e: e.tensor_scalar(gv.t[:], gv.t[:], 0.044715, 1.0, ALU.mult, ALU.add), reads=[gv.b], writes=[gv.b])
                        fw.op("dve", lambda e: e.tensor_tensor(gv.t[:], gv.t[:], gu.t[:], ALU.mult), reads=[gv.b, gu.b], writes=[gv.b])
                        fw.op("act", lambda e: e.activation(gs.t[:], gv.t[:], AF.Sigmoid, scale=1.5957691216057308), reads=[gv.b], writes=[gs.b])
                        fw.op("dve", lambda e: e.tensor_tensor(G.t[:, 0:32], gu.t[:], gs.t[:], ALU.mult), reads=[gu.b, gs.b], writes=[G.b])
                        pq = gbank()
                        if s_ == 0:
                            fw.op("pe", lambda e, pq=pq: e.matmul(pq.t[:, 0:32], w2c.t[:].rearrange("p s d -> p (s d)"), G.t[:, 0:32], start=True, stop=True),
                                  reads=[w2c.b, G.b], writes=[pq.b])
                            fw.op("dve", lambda e, pq=pq, g=g: e.tensor_copy(KC[g].t[0:64, 32 * i:32 * i + 32], pq.t[0:64, 0:32]),
                                  reads=[pq.b], writes=[KC[g].b])
                        else:
                            fw.op("pe", lambda e, pq=pq: e.matmul(pq.t[:, 0:64], G.t[:], w2c.t[:, 1, :], start=True, stop=True),
                                  reads=[w2c.b, G.b], writes=[pq.b])
                            fw.op("dve", lambda e, pq=pq, g=g: e.tensor_copy(VC.t[pn:pn + 32, g, ccn, 0:64], pq.t[0:32, 0:64]),
                                  reads=[pq.b], writes=[VC.b])
                            if i == 0:
                                fw.op("dve", lambda e, g=g: e.memset(VC.t[0:1, g, 0, :], 0.0), writes=[VC.b])
                fw.op("pool", lambda e: e.tensor_copy(KCT.t[:, 0:16], KCT.t[:, T:T + 16]), reads=[KCT.b], writes=[KCT.b])
                fw.op("pool", lambda e: e.tensor_copy(VCT.t[:, 0:16], VCT.t[:, T:T + 16]), reads=[VCT.b], writes=[VCT.b])

                fw.mute = STAGE < 3
                def pump(lag):
                    while pend and (pend[0][0] == "fin" or sum(1 for k_, _ in pend if k_ == "pv") > lag):
                        pend.popleft()[1]()

                def run_branch(hh, chunks, fin):
                    po = ps[3 + ocnt[0] % 2]
                    ocnt[0] += 1
                    n = len(chunks)
                    used = []
                    for idx, (kap, kb, vap, vb, bcol, mask) in enumerate(chunks):
                        sb_ = ps[scnt[0] % 3]
                        scnt[0] += 1
                        eb = E[ecnt[0] % len(E)]
                        ecnt[0] += 1
                        fw.op("pe", lambda e: e.matmul(sb_.t[:], kap, Qs[hh].t[:], start=True, stop=(mask is None)),
                              reads=[kb, Qs[hh].b], writes=[sb_.b])
                        if mask is not None:
                            fw.op("pe", lambda e: e.matmul(sb_.t[:], self.ident_bf(), mask, start=False, stop=True),
                                  reads=[bc.b], writes=[sb_.b])
                        fw.op("act", lambda e: e.activation(eb.t[:], sb_.t[:], AF.Exp, bias=self.sm.t[:, bcol:bcol + 1]),
                              reads=[sb_.b, self.sm.b], writes=[eb.b])
                        used.append(eb)

                        def pv(idx=idx, vap=vap, vb=vb, eb=eb):
                            fw.op("pe", lambda e: e.matmul(po.t[:], vap, eb.t[:], start=(idx == 0), stop=(idx == n - 1)),
                                  reads=[vb, eb.b], writes=[po.b])
                        pend.append(("pv", pv))
                        if idx == n - 1:
                            pend.append(("fin", lambda: fin(po, used)))
                        pump(2)

                def combine(hh, b, po):
                    pg = gbank()
                    hp = (hh % 2) * 64
                    hs = slice(hp, hp + 64)
                    zcb, gzb = zc_b[hh % 2], gz_b[hh % 2]
                    ac = acc2[(hh % 4) // 2]
                    acb = acc_b[hh % 4]
                    on = ("csw"[b] in BR)
                    if on:
                        fw.op("pe", lambda e: e.matmul(pg.t[:], bc.t[:, BC["eg"] + (hh * 3 + b) * 64: BC["eg"] + (hh * 3 + b) * 64 + 128], sgT.t[:],
                                                       start=True, stop=True), reads=[bc.b, sgT.b], writes=[pg.b])
                        if b == 0 or (b == 1 and i <= 1):
                            fw.op("act", lambda e: e.activation(zc.t[hs, :], po.t[64:128, :], AF.Ln, bias=self.smc("eps18")[hs, :]),
                                  reads=[po.b, self.sm.b], writes=[zcb])
                            fw.op("act", lambda e: e.activation(zc.t[hs, :], zc.t[hs, :], AF.Exp, scale=-1.0), reads=[zcb], writes=[zcb])
                        else:
                            fw.op("dve", lambda e: e.tensor_scalar(zc.t[hs, :], po.t[64:128, :], 1e-30, None, ALU.add), reads=[po.b], writes=[zcb])
                            fw.op("dve", lambda e: e.reciprocal(zc.t[hs, :], zc.t[hs, :]), reads=[zcb], writes=[zcb])
                        fw.op("dve", lambda e: e.tensor_tensor(gz.t[hs, :], pg.t[0:64, :], zc.t[hs, :], ALU.mult), reads=[pg.b, zcb], writes=[gzb])
                    if b == 0:
                        if on:
                            fw.op("dve", lambda e: e.tensor_tensor(ac.t[hs, :], po.t[0:64, :], gz.t[hs, :], ALU.mult), reads=[po.b, gzb], writes=[acb])
                        else:
                            fw.op("dve", lambda e: e.memset(ac.t[hs, :], 0.0), writes=[acb])
                    elif b == 1:
                        if on:
                            fw.op("dve", lambda e: e.tensor_tensor(gz.t[hs, :], po.t[0:64, :], gz.t[hs, :], ALU.mult), reads=[po.b, gzb], writes=[gzb])
                            fw.op("pool", lambda e: e.tensor_tensor(ac.t[hs, :], ac.t[hs, :], gz.t[hs, :], ALU.add), reads=[acb, gzb], writes=[acb])
                    else:
                        if on:
                            fw.op("dve", lambda e: e.tensor_tensor(gz.t[hs, :], po.t[0:64, :], gz.t[hs, :], ALU.mult), reads=[po.b, gzb], writes=[gzb])
                            fw.op("dve", lambda e: e.tensor_tensor(oT.t[hs, hh // 2, :], ac.t[hs, :], gz.t[hs, :], ALU.add),
                                  reads=[acb, gzb], writes=[oT.b])
                        else:
                            fw.op("dve", lambda e: e.tensor_copy(oT.t[hs, hh // 2, :], ac.t[hs, :]), reads=[acb], writes=[oT.b])

                for g in range(2):
                    fw.mute = STAGE < 3
                    ccs = [0] if i < 4 else [0, 1]
                    for hi_, hh in enumerate(range(4 * g, 4 * g + 4)):
                        chunks = []
                        for cc in ccs:
                            x0 = 512 * i - 2048 * cc
                            mask = bc.t[:, BC["g1"] + x0: BC["g1"] + x0 + 512] if x0 <= 1536 else None
                            chunks.append((KC[g].t[:, cc * 128:(cc + 1) * 128], KC[g].b, VC.t[:, g, cc, :], VC.b,
                                           SM["bias_cmp"] + (hh * 2 + cc) * 8 + i, mask))
                        def cmp_fin(po, used, hh=hh, hi_=hi_, ccs=ccs):
                            pu = ps[5]
                            for s_ in range(4):
                                for ci, cc in enumerate(ccs):
                                    fw.op("pe", lambda e: e.matmul(
                                        pu.t[:, s_ * 65:(s_ + 1) * 65], used[ci].t[:, s_ * 128:(s_ + 1) * 128],
                                        bc.t[:, BC["ova"] + cc * 65: BC["ova"] + (cc + 1) * 65], start=(ci == 0), stop=(ci == len(ccs) - 1)),
                                        reads=[used[ci].b, bc.b], writes=[pu.b])
                            combine(hh, 0, po)
                            zsl = pu.t[:, 0:260].rearrange("p (s c) -> p s c", c=65)[:, :, 64]
                            fw.op("dve", lambda e: e.tensor_scalar(rz4.t[:], zsl, 1e-30, None, ALU.add), reads=[pu.b], writes=[rz4.b])
                            fw.op("dve", lambda e: e.reciprocal(rz4.t[:], rz4.t[:]), reads=[rz4.b], writes=[rz4.b])
                            for s_ in range(4):
                                if hi_ == 0:
                                    fw.op("dve", lambda e: e.tensor_scalar(impa.t[:, s_, :], pu.t[:, s_ * 65:s_ * 65 + 64], rz4.t[:, s_:s_ + 1], None, ALU.mult),
                                          reads=[pu.b, rz4.b], writes=[impa.b])
                                else:
                                    fw.op("dve", lambda e: e.scalar_tensor_tensor(impa.t[:, s_, :], pu.t[:, s_ * 65:s_ * 65 + 64], rz4.t[:, s_:s_ + 1],
                                                                                 impa.t[:, s_, :], ALU.mult, ALU.add),
                                          reads=[pu.b, rz4.b, impa.b], writes=[impa.b])
                        run_branch(hh, chunks, cmp_fin)
                        pump(0)
                    pump(0)
                    fw.mute = STAGE < 4
                    pT = ps[5]
                    for s_ in range(4):
                        sub = 4 * i + s_
                        fo = SM["fbw"] + 62 - 2 * sub
                        sl_ = selb[s_ % 2]
                        fw.op("dve", lambda e, s_=s_, fo=fo: e.tensor_tensor(impf.t[:], impa.t[:, s_, :], self.sm.t[:, fo:fo + 64], ALU.add),
                              reads=[impa.b, self.sm.b], writes=[impf.b])
                        if sub >= 1:
                            fw.op("dve", lambda e: e.tensor_scalar(impf.t[:, 0:1], impf.t[:, 0:1], 1e4, None, ALU.add), reads=[impf.b], writes=[impf.b])
                        fw.op("dve", lambda e: e.max(top8.t[:], impf.t[:]), reads=[impf.b], writes=[top8.b])
                        fw.op("dve", lambda e, sl_=sl_: e.tensor_scalar(sl_.t[:, 0:64], impf.t[:], top8.t[:, 7:8], NEGB, ALU.is_lt, ALU.mult),
                              reads=[impf.b, top8.b], writes=[sl_.b])
                        fw.op("pe", lambda e, s_=s_, sl_=sl_: e.transpose(pT.t[:, s_ * 128:(s_ + 1) * 128], sl_.t[:], self.identf.t[:]),
                              reads=[sl_.b, self.identf.b], writes=[pT.b])
                    for hh in range(4 * g, 4 * g + 4):
                        fw.op("dve", lambda e, hh=hh: e.scalar_tensor_tensor(Qs[hh].t[64:128, :], self.smc("arow", 0, 512)[0:64, :], -SLOPES[hh], pT.t[0:64, :],
                                                                           ALU.mult, ALU.add),
                              reads=[self.sm.b, pT.b], writes=[Qs[hh].b])
                    fw.mute = STAGE < 5
                    for hh in range(4 * g, 4 * g + 4):
                        if hh not in HEADS:
                            continue
                        chunks = []
                        for c in range(0, 4 * i + 4):
                            dl = c - 4 * i
                            mask = None
                            if dl >= 0:
                                st0 = BC["caus"] + 384 - 128 * dl
                                mask = bc.t[:, st0:st0 + 512]
                            chunks.append((KS[g].t[:, c * 128:(c + 1) * 128], KS[g].b, VS.t[:, g, c, :], VS.b,
                                           SM["bias_sw"] + hh * 32 + dl + 28, mask))
                        run_branch(hh, chunks, lambda po, used, hh=hh: combine(hh, 1, po))
                        chunks = []
                        for c in range(max(0, 4 * i - 4), 4 * i + 4):
                            dl = c - 4 * i
                            if dl >= 0:
                                st0 = BC["caus"] + 384 - 128 * dl
                            else:
                                st0 = BC["far"] + 384 - 128 * (dl + 4)
                            mask = bc.t[:, st0:st0 + 512]
                            sl8 = c % 8
                            chunks.append((KW[g].t[:, sl8 * 128:(sl8 + 1) * 128], KW[g].b, VW.t[:, g, sl8, :], VW.b,
                                           SM["bias_sw"] + hh * 32 + dl + 28, mask))
                        run_branch(hh, chunks, lambda po, used, hh=hh: combine(hh, 2, po))
                pump(0)
                fw.mute = False
                for mo in range(8):
                    pp = gbank()
                    for k in range(8):
                        rhs = oT.t[:, k, :] if k < 4 else yp.t[:, k - 4, :]
                        rb = oT.b if k < 4 else yp.b
                        fw.op("pe", lambda e, k=k, pp=pp, rhs=rhs, mo=mo: e.matmul(pp.t[:], wo.t[:, k, mo * 128:(mo + 1) * 128], rhs,
                                                                                  start=(k == 0), stop=(k == 7)),
                              reads=[wo.b, rb], writes=[pp.b])
                    xq = pa[mo % 2]
                    src_c = sdram[mo * 128:(mo + 1) * 128, i * T:(i + 1) * T]
                    dst_c = self.xs[mo * 128:(mo + 1) * 128, i * T:(i + 1) * T]
                    fw.dma("sp", xq.t[:, 0:T], src_c, reads=[sbufs[i]], writes=[xq.b], sem_buf=xq.b)
                    fw.op("dve", lambda e, pp=pp, mo=mo: e.tensor_tensor(xq.t[:, 0:T], pp.t[:], xq.t[:, 0:T], ALU.add),
                          reads=[pp.b, xq.b], writes=[xq.b])
                    fw.dma("sp", dst_c, xq.t[:, 0:T], reads=[xq.b], writes=[self.xs_b[i]], sem_buf=xq.b)


def _prep_weights(inp):
    f = lambda a: np.ascontiguousarray(a, dtype=np.float32)
    out = {}
    sm, bc, ex, e0 = make_consts()

    def pc(v):
        return np.asarray(v, np.float32).reshape(-1, 128).T

    for l in range(4):
        sm[:, SM["normg"] + l * 8: SM["normg"] + (l + 1) * 8] = pc(inp["mix_norm"][l])
        sm[:, SM["normg"] + (4 + l) * 8: SM["normg"] + (5 + l) * 8] = pc(inp["ffn_norm"][l])
    sm[:, SM["normg"] + 64: SM["normg"] + 72] = pc(inp["final_norm"])
    for i in range(2):
        sm[:, SM["cb1_%d" % i]: SM["cb1_%d" % i] + 16] = pc(inp["conv_b_pw1"][i])
        sm[:, SM["cbdw_%d" % i]: SM["cbdw_%d" % i] + 8] = pc(inp["conv_b_dw"][i])
        sm[:, SM["clng_%d" % i]: SM["clng_%d" % i] + 8] = pc(inp["conv_ln_g"][i])
        sm[:, SM["clnb_%d" % i]: SM["clnb_%d" % i] + 8] = pc(inp["conv_ln_b"][i])
        sm[:, SM["cb2_%d" % i]: SM["cb2_%d" % i] + 8] = pc(inp["conv_b_pw2"][i])
        wdw = np.asarray(inp["conv_w_dw"][i], np.float32)
        sm[:, SM["cdw_%d" % i]: SM["cdw_%d" % i] + 248] = wdw.reshape(31, 8, 128).transpose(2, 0, 1).reshape(128, 248)
        sm[:, SM["pscale_%d" % i]: SM["pscale_%d" % i] + 4] = pc(inp["pool_scale"][i])
    out["smalls"] = f(sm)
    out["bconst"] = f(bc)
    out["exrows"] = f(ex)
    out["e0rows"] = f(e0)
    wg = np.asarray(inp["ffn_w_gate"], np.float32).reshape(4, 8, 128, HC, 128)
    wu = np.asarray(inp["ffn_w_up"], np.float32).reshape(4, 8, 128, HC, 128)
    wgu = np.stack([wg, wu], axis=0)
    out["wgu"] = f(wgu.transpose(1, 4, 3, 0, 2, 5).reshape(4, HC, 128, 2 * 8 * 128))
    wd = np.asarray(inp["ffn_w_down"], np.float32).reshape(4, HC, 128, 8, 128)
    out["wd"] = f(wd.transpose(0, 3, 2, 1, 4).reshape(4, 8, 128, HC * 128))
    perm = np.concatenate([np.arange(0, 512), np.arange(512, 640), np.arange(640, 768), np.arange(768, 896),
                           np.arange(1024, 1152), np.arange(1280, 1304), np.arange(1304, 1816),
                           np.arange(896, 1024), np.arange(1152, 1280)])
    win = np.asarray(inp["nsa_w_in"], np.float32)[:, :, perm].reshape(2, 8, 128, 1816)
    out["win"] = f(win.transpose(0, 2, 1, 3).reshape(2, 128, 8 * 1816))
    wo = np.asarray(inp["mix_w_out"], np.float32).reshape(2, 8, 128, 1024)
    out["wo"] = f(wo.transpose(0, 2, 1, 3).reshape(2, 128, 8 * 1024))
    w1c = np.stack([np.asarray(inp["cmp_k_w1"], np.float32), np.asarray(inp["cmp_v_w1"], np.float32)], axis=1)
    w1c = w1c.reshape(2, 2, 32, 64, 128).transpose(0, 3, 1, 2, 4)
    w1z = np.zeros((2, 128, 2, 2, 32, 128), np.float32)
    w1z[:, 0:64, 0] = w1c
    w1z[:, 64:128, 1] = w1c
    out["w1c"] = f(w1z.reshape(2, 128, 2 * 2 * 32 * 128))
    pos = np.stack([np.asarray(inp["cmp_pos_k"], np.float32), np.asarray(inp["cmp_pos_v"], np.float32)], axis=1)
    pos = pos.transpose(0, 3, 1, 2)
    out["post"] = f(np.repeat(np.concatenate([pos, pos], axis=1).reshape(2, 128, 64), 2, axis=2))
    w2c = np.stack([np.asarray(inp["cmp_k_w2"], np.float32), np.asarray(inp["cmp_v_w2"], np.float32)], axis=1)
    out["w2c"] = f(w2c.transpose(0, 2, 1, 3).reshape(2, 128, 128))
    pw = np.asarray(inp["pool_w"], np.float32)
    out["poolw"] = f(pw.transpose(0, 2, 1, 3).reshape(2, 128, 512))
    cw1 = np.asarray(inp["conv_w_pw1"], np.float32).reshape(2, 8, 128, 2048)
    out["cw1"] = f(cw1.transpose(0, 2, 1, 3).reshape(2, 128, 8 * 2048))
    cw2 = np.asarray(inp["conv_w_pw2"], np.float32).reshape(2, 8, 128, 1024)
    out["cw2"] = f(cw2.transpose(0, 2, 1, 3).reshape(2, 128, 8 * 1024))
    return out


_CACHE = {}


def run(inputs, layers=(0, 1, 2, 3), final=True, n_cores=8):
    key = (tuple(layers), final)
    if key not in _CACHE:
        pr = Prog(layers, final)
        _CACHE[key] = (pr.build(), set(pr.dins.keys()))
    nc, names = _CACHE[key]
    w = _prep_weights(inputs)
    x = np.asarray(inputs["x"], np.float32)
    in_maps = []
    for b in range(n_cores):
        m = {k: v for k, v in w.items() if k in names}
        m["xT"] = np.ascontiguousarray(x[b].T)
        in_maps.append(m)
    res = run_bass_kernel_spmd(nc, in_maps, core_ids=list(range(n_cores)))
    out = np.stack([np.ascontiguousarray(res.results[b]["yT"].T) for b in range(n_cores)], axis=0)
    return out.astype(np.float32)


def kernel(**inputs):
    return run(inputs)
```

```python
from contextlib import ExitStack, contextmanager
import numpy as np
import concourse.bass as bass
import concourse.mybir as mybir
from concourse.bass_utils import run_bass_kernel_spmd

F32 = mybir.dt.float32
BF16 = mybir.dt.bfloat16
AF = mybir.ActivationFunctionType
ALU = mybir.AluOpType

S = 4096
D = 1024
T = 512
NT = S // T
HID = 2816
HC = HID // 128
TG = 2048
NEGB = -30000.0
ENGS = ("pe", "act", "dve", "pool", "sp")
SLOPES = [2.0 ** (-(h + 1)) for h in range(8)]


class Buf:
    __slots__ = ("name", "writes", "reads", "dsem", "psum")

    def __init__(self, name, psum=False):
        self.name = name
        self.writes = {}
        self.reads = {}
        self.dsem = None
        self.psum = psum


class TB:
    __slots__ = ("t", "b")

    def __init__(self, t, b):
        self.t = t
        self.b = b


class _Rec:
    def __init__(self):
        self.call = None

    def __getattr__(self, name):
        def f(*a, **k):
            self.call = (name, a, k)
            return self
        return f


class FW:
    def __init__(self, nc, stack):
        self.nc = nc
        self.stack = stack
        self.ops = {e: [] for e in ENGS}
        self.sems = {}
        self.count = {}
        self.seen = {e: {} for e in ENGS}
        self.same_engine_sync = True
        self.mute = False
        self.ekey = {}
        self.epoch = 0
        for e in ENGS:
            self.ekey[e] = self._newsem("E_" + e)
        self.free_dsems = {'sw': [], 'hw': []}

    def _newsem(self, key):
        h = self.stack.enter_context(self.nc.semaphore(key))
        self.sems[key] = h
        self.count[key] = 0
        return key

    def _deps(self, eng, reads, writes):
        deps = {}
        own = self.ekey[eng]
        for b in reads:
            for k, v in b.writes.items():
                if deps.get(k, 0) < v:
                    deps[k] = v
            if b.psum:
                for k, v in b.reads.items():
                    if k != own and deps.get(k, 0) < v:
                        deps[k] = v
        for b in writes:
            for d in (b.writes, b.reads):
                for k, v in d.items():
                    if deps.get(k, 0) < v:
                        deps[k] = v
        out = []
        seen = self.seen[eng]
        for k, v in deps.items():
            if k == own and (eng == "pe" or not self.same_engine_sync):
                continue
            if seen.get(k, 0) >= v:
                continue
            seen[k] = v
            out.append((k, v))
        return out

    def _record(self, key, val, reads, writes):
        for b in reads:
            b.reads[key] = val
        for b in writes:
            b.reads = {}
            b.writes[key] = val

    def op(self, eng, fn, reads=(), writes=()):
        if self.mute:
            return
        waits = self._deps(eng, reads, writes)
        key = self.ekey[eng]
        self.count[key] += 1
        val = self.count[key]
        sems = self.sems
        rec = _Rec()
        fn(rec)
        cname, cargs, ckw = rec.call

        def emit(e, waits=waits, key=key, cname=cname, cargs=cargs, ckw=ckw):
            for k, v in waits:
                e.wait_ge(sems[k], v)
            getattr(e, cname)(*cargs, **ckw).then_inc(sems[key], 1)
        self.ops[eng].append(emit)
        self._record(key, val, reads, writes)

    def dma(self, eng, out_ap, in_ap, reads=(), writes=(), sem_buf=None):
        if self.mute:
            return
        waits = self._deps(eng, reads, writes)
        sb = sem_buf
        qk = "sw" if eng == "pool" else "hw"
        if sb.dsem is None:
            sb.dsem = {}
        if qk not in sb.dsem:
            if self.free_dsems[qk]:
                sb.dsem[qk] = self.free_dsems[qk].pop()
            else:
                sb.dsem[qk] = self._newsem("D%s%d" % (qk, len(self.sems)))
        key = sb.dsem[qk]
        self.count[key] += 16
        val = self.count[key]
        sems = self.sems

        def emit(e, waits=waits, key=key):
            for k, v in waits:
                e.wait_ge(sems[k], v)
            e.dma_start(out=out_ap, in_=in_ap).then_inc(sems[key], 16)
        self.ops[eng].append(emit)
        self._record(key, val, reads, writes)

    def barrier(self):
        allb = Buf("ALL")
        allb.writes = {k: v for k, v in self.count.items() if v > 0}
        sems = self.sems
        for eng in ENGS:
            waits = self._deps(eng, [allb], ())

            def emit(e, waits=waits):
                for k, v in waits:
                    e.wait_ge(sems[k], v)
            self.ops[eng].append(emit)

    def new_epoch(self):
        self.epoch += 1
        for e in ENGS:
            self.ekey[e] = self._newsem("E_%s_%d" % (e, self.epoch))

    def finish(self):
        nc = self.nc
        ops = self.ops
        with nc.Block() as block:
            @block.tensor
            def _(e):
                for f in ops["pe"]:
                    f(e)

            @block.scalar
            def _(e):
                for f in ops["act"]:
                    f(e)

            @block.vector
            def _(e):
                for f in ops["dve"]:
                    f(e)

            @block.gpsimd
            def _(e):
                for f in ops["pool"]:
                    f(e)

            @block.sync
            def _(e):
                for f in ops["sp"]:
                    f(e)


class Phase:
    def __init__(self, A, name):
        self.A = A
        self.name = name
        self.stack = ExitStack()
        self.bufs = []

    def sb(self, name, shape, dtype):
        t = self.stack.enter_context(self.A.nc.sbuf_tensor(self.name + "_" + name, shape, dtype))
        tb = TB(t, Buf(name))
        self.bufs.append(tb)
        return tb

    def close(self):
        self.A.fw.barrier()
        self.A.fw.new_epoch()
        for tb in self.bufs:
            if tb.b.dsem is not None:
                for qk, key in tb.b.dsem.items():
                    self.A.fw.free_dsems[qk].append(key)
                tb.b.dsem = None
        self.stack.close()


def _small_layout():
    off = {}
    c = 0

    def add(name, n):
        nonlocal c
        off[name] = c
        c += n
    add("normg", 9 * 8)
    for i in range(2):
        add("cb1_%d" % i, 16)
        add("cbdw_%d" % i, 8)
        add("clng_%d" % i, 8)
        add("clnb_%d" % i, 8)
        add("cb2_%d" % i, 8)
        add("cdw_%d" % i, 31 * 8)
        add("pscale_%d" % i, 4)
    add("bias_sw", 8 * 32)
    add("bias_cmp", 8 * 2 * 8)
    add("fbw", 126)
    add("rc15", 4 * 16)
    add("arow", 512)
    add("eps6", 1)
    add("eps5", 1)
    add("eps30", 1)
    add("eps18", 1)
    off["_n"] = c
    return off


SM = _small_layout()


def _bf_consts():
    off = {}
    c = 0

    def add(name, n):
        nonlocal c
        off[name] = c
        c += n
    add("caus", 896)
    add("far", 896)
    add("g1", 2048)
    add("ident", 128)
    add("ones", 128)
    add("ova", 2 * 65)
    add("eg", 24 * 64 + 64)
    off["_n"] = c
    return off


BC = _bf_consts()


def make_consts():
    sm = np.zeros((128, SM["_n"]), np.float32)
    p = np.arange(128)[:, None].astype(np.float64)
    for h in range(8):
        for di, delta in enumerate(range(-28, 4)):
            sm[:, SM["bias_sw"] + h * 32 + di] = (SLOPES[h] * (128 * delta + p[:, 0] - 256))
        for cc in range(2):
            for i in range(8):
                sm[:, SM["bias_cmp"] + (h * 2 + cc) * 8 + i] = SLOPES[h] * (16 * (128 * cc + p[:, 0]) + 15 - 512 * i - 256)
    hb = (np.arange(128) >= 64).astype(np.int64)[:, None]
    x = (np.arange(126) - 62)[None, :]
    fb = np.zeros((128, 126), np.float32)
    fb[(x == hb) | (x == hb - 1)] = 1e4
    fb[np.broadcast_to(x > hb, fb.shape)] = -1e30
    sm[:, SM["fbw"]:SM["fbw"] + 126] = fb
    for m, w in enumerate((2, 4, 8, 16)):
        t = np.arange(16)
        sm[:, SM["rc15"] + m * 16: SM["rc15"] + (m + 1) * 16] = (1.0 / np.minimum(t + 1.0, float(w)))[None, :]
    sm[:, SM["arow"]:SM["arow"] + 512] = (np.arange(512) - 256)[None, :]
    sm[:, SM["eps6"]] = 1024.0 * 1e-6
    sm[:, SM["eps5"]] = 1e-5
    sm[:, SM["eps30"]] = 1e-30
    sm[:, SM["eps18"]] = 1e-18

    bc = np.zeros((128, BC["_n"]), np.float32)
    xx = (np.arange(896) - 384)[None, :]
    pp = np.arange(128)[:, None]
    bc[:, BC["caus"]:BC["caus"] + 896] = np.where(xx >= pp, 0.0, NEGB)
    bc[:, BC["far"]:BC["far"] + 896] = np.where(xx < pp, 0.0, NEGB)
    x2 = np.arange(2048)[None, :]
    g = np.where(x2 >= 16 * pp + 15, 0.0, NEGB)
    bc[:, BC["g1"]:BC["g1"] + 2048] = g
    bc[:, BC["ident"]:BC["ident"] + 128] = np.eye(128)
    bc[:, BC["ones"]:BC["ones"] + 128] = 1.0
    ova = np.zeros((256, 65), np.float32)
    for n1 in range(1, 256):
        n = n1 - 1
        c0 = n * 16
        for j in range(64):
            s0 = j * 64
            ov = min(c0 + 32, s0 + 64) - max(c0, s0)
            if ov > 0:
                ova[n1, j] = ov / 32.0
        ova[n1, 64] = 1.0
    bc[:, BC["ova"]:BC["ova"] + 130] = ova.reshape(2, 128, 65).transpose(1, 0, 2).reshape(128, 130)
    eg = np.zeros((128, 24, 64), np.float32)
    for r in range(24):
        eg[r, r, :] = 1.0
    bc[:, BC["eg"]:BC["eg"] + 24 * 64] = eg.reshape(128, -1)
    ex = np.zeros((64, S), np.float32)
    for j in range(64):
        ex[j, j * 64:(j + 1) * 64] = 1.0
    e0 = np.zeros((64, S), np.float32)
    e0[0, :] = 1.0
    return sm, bc, ex, e0


class Prog:
    def __init__(self, layers=(0, 1, 2, 3), final=True, skip_ffn=False):
        self.layers = tuple(layers)
        self.final = final
        self.skip_ffn = skip_ffn
        self.nc = bass.Bass("TRN2", target_bir_lowering=False)
        self.stack = ExitStack()
        self.dins = {}

    def dram_in(self, name, shape):
        if name not in self.dins:
            self.dins[name] = self.nc.dram_tensor(name, list(shape), F32, kind="ExternalInput").ap()
        return self.dins[name]

    @property
    def d_wgu(self): return self.dram_in("wgu", (4, HC, 128, 2 * 8 * 128))
    @property
    def d_wd(self): return self.dram_in("wd", (4, 8, 128, HC * 128))
    @property
    def d_win(self): return self.dram_in("win", (2, 128, 8 * 1816))
    @property
    def d_wo(self): return self.dram_in("wo", (2, 128, 8 * 1024))
    @property
    def d_w1c(self): return self.dram_in("w1c", (2, 128, 2 * 2 * 32 * 128))
    @property
    def d_post(self): return self.dram_in("post", (2, 128, 2 * 32 * 2))
    @property
    def d_w2c(self): return self.dram_in("w2c", (2, 128, 2 * 64))
    @property
    def d_pw(self): return self.dram_in("poolw", (2, 128, 4 * 128))
    @property
    def d_cw1(self): return self.dram_in("cw1", (2, 128, 8 * 2048))
    @property
    def d_cw2(self): return self.dram_in("cw2", (2, 128, 8 * 1024))
    @property
    def d_ex(self): return self.dram_in("exrows", (64, S))
    @property
    def d_e0(self): return self.dram_in("e0rows", (64, S))

    @contextmanager
    def phase(self, name):
        ph = Phase(self, name)
        try:
            yield ph
        finally:
            ph.close()

    def build(self):
        nc = self.nc
        st = self.stack
        with st:
            self.fw = FW(nc, st)
            fw = self.fw
            self.x_in = self.dram_in("xT", (D, S))
            self.y_out = nc.dram_tensor("yT", [D, S], F32, kind="ExternalOutput").ap()
            self.xs = nc.dram_tensor("xs", [D, S], F32).ap()
            self.d_sm = self.dram_in("smalls", (128, SM["_n"]))
            self.d_bc = self.dram_in("bconst", (128, BC["_n"]))
            self.xs_b = [Buf("xs%d" % i) for i in range(NT)]
            self.xin_b = Buf("x_in")
            self.y_b = [Buf("y%d" % i) for i in range(NT)]
            sm_t = st.enter_context(nc.sbuf_tensor("smalls_sb", [128, SM["_n"]], F32))
            self.sm = TB(sm_t, Buf("smalls"))
            bc_t = st.enter_context(nc.sbuf_tensor("bconst_sb", [128, BC["_n"]], BF16))
            self.bc = TB(bc_t, Buf("bconst"))
            idf_t = st.enter_context(nc.sbuf_tensor("identf", [128, 128], F32))
            self.identf = TB(idf_t, Buf("identf"))
            self.ps = []
            for i in range(8):
                t = st.enter_context(nc.psum_tensor("ps%d" % i, [128, 512], F32))
                self.ps.append(TB(t, Buf("ps%d" % i, psum=True)))
            fw.dma("sp", sm_t[:], self.d_sm[:], writes=[self.sm.b], sem_buf=self.sm.b)
            half = BC["_n"] // 2
            fw.dma("pool", bc_t[:, 0:half], self.d_bc[:, 0:half], writes=[self.bc.b], sem_buf=self.bc.b)
            fw.dma("pool", bc_t[:, half:], self.d_bc[:, half:], writes=[self.bc.b], sem_buf=self.bc.b)
            ng = sm_t[:, SM["normg"]:SM["normg"] + 72]
            fw.op("dve", lambda e: e.tensor_scalar(ng, ng, 32.0, None, ALU.mult), writes=[self.sm.b])
            fw.op("dve", lambda e: e.tensor_copy(idf_t[:], bc_t[:, BC["ident"]:BC["ident"] + 128]),
                  reads=[self.bc.b], writes=[self.identf.b])
            fw.barrier()

            first = True
            for l in self.layers:
                src = (self.x_in, [self.xin_b] * NT) if first else (self.xs, self.xs_b)
                first = False
                if l % 2 == 0:
                    self.even_layer(l, src)
                else:
                    self.conv_layer(l, src)
                if not self.skip_ffn:
                    self.ffn_layer(l)
            if self.final:
                src = (self.x_in, [self.xin_b] * NT) if first else (self.xs, self.xs_b)
                self.final_norm(src)
            fw.barrier()
            fw.finish()
        return nc

    def tile_ap(self, dram, i):
        return dram.rearrange("(c p) t -> p c t", p=128)[:, :, i * T:(i + 1) * T]

    def ones_bf(self):
        return self.bc.t[:, BC["ones"]:BC["ones"] + 128]

    def ident_bf(self):
        return self.bc.t[:, BC["ident"]:BC["ident"] + 128]

    def smc(self, name, j=0, n=1):
        o = SM[name] + j
        return self.sm.t[:, o:o + n]

    def rmsnorm(self, x, h, sq, rstd, ps, gidx, sqc=8):
        fw = self.fw
        for q in range(8 // sqc):
            fw.op("act", lambda e: e.activation(sq.t[:, 0:sqc, :], x.t[:, q * sqc:(q + 1) * sqc, :], AF.Square), reads=[x.b], writes=[sq.b])
            for kk in range(sqc):
                k = q * sqc + kk
                fw.op("pe", lambda e: e.matmul(ps.t[:], self.ones_bf(), sq.t[:, kk, :], start=(k == 0), stop=(k == 7)),
                      reads=[sq.b, self.bc.b], writes=[ps.b])
        fw.op("act", lambda e: e.activation(rstd.t[:], ps.t[:], AF.Ln, bias=self.smc("eps6"), scale=1.0),
              reads=[ps.b, self.sm.b], writes=[rstd.b])
        fw.op("act", lambda e: e.activation(rstd.t[:], rstd.t[:], AF.Exp, scale=-0.5), reads=[rstd.b], writes=[rstd.b])
        hap, hb = h
        for k in range(8):
            fw.op("dve", lambda e, k=k: e.scalar_tensor_tensor(hap(k), x.t[:, k, :], self.smc("normg", gidx * 8 + k),
                                                             rstd.t[:], ALU.mult, ALU.mult),
                  reads=[x.b, rstd.b, self.sm.b], writes=[hb])

    def ffn_layer(self, l):
        fw = self.fw
        ntg = TG // T
        ngrp = S // TG
        with self.phase("ffn%d" % l) as ph:
            hT = ph.sb("hT", [128, 8, TG], BF16)
            hT_b = [Buf("hT%d" % i) for i in range(ntg)]
            act = ph.sb("act", [128, HC, TG], BF16)
            act_b = [Buf("act%d" % i) for i in range(ntg)]
            xin = [ph.sb("xin%d" % i, [128, 8, T], F32) for i in range(2)]
            sq = ph.sb("sq", [128, 4, T], BF16)
            rstd = ph.sb("rstd", [128, T], F32)
            wgu = [ph.sb("wgu%d" % i, [128, 2, 8, 128], BF16) for i in range(3)]
            wd = [ph.sb("wd%d" % i, [128, HC, 128], BF16) for i in range(2)]
            sg = [ph.sb("sg%d" % i, [128, T], F32) for i in range(2)]
            xc = [ph.sb("xc%d" % i, [128, T], F32) for i in range(2)]
            ps = self.ps
            ncnt = [0]

            def load_wgu(hc):
                w = wgu[hc % 3]
                fw.dma("pool", w.t[:].rearrange("p a k c -> p (a k c)"), self.d_wgu[l, hc], writes=[w.b], sem_buf=w.b)

            def load_wd(m):
                w = wd[m % 2]
                fw.dma("pool", w.t[:].rearrange("p k c -> p (k c)"), self.d_wd[l, m], writes=[w.b], sem_buf=w.b)

            def norm_tile(gi, tl):
                ti = gi * ntg + tl
                x = xin[ncnt[0] % 2]
                ncnt[0] += 1
                fw.dma("sp", x.t[:], self.tile_ap(self.xs, ti), reads=[self.xs_b[ti]], writes=[x.b], sem_buf=x.b)
                self.rmsnorm(x, (lambda k, tl=tl: hT.t[:, k, tl * T:(tl + 1) * T], hT_b[tl]), sq, rstd, ps[6], 4 + l, sqc=4)

            load_wgu(0)
            load_wgu(1)
            for tl in range(ntg):
                norm_tile(0, tl)
            for gi in range(ngrp):
                if gi > 0:
                    load_wgu(0)
                    load_wgu(1)
                it = 0
                for hc in range(HC):
                    if hc + 2 < HC:
                        load_wgu(hc + 2)
                    elif hc + 2 == HC:
                        load_wd(0)
                    w = wgu[hc % 3]
                    for tl in range(ntg):
                        pg = ps[it % 2]
                        pu = ps[2 + it % 2]
                        s_ = sg[it % 2]
                        it += 1
                        for a_, pp in ((0, pg), (1, pu)):
                            for k in range(8):
                                fw.op("pe", lambda e: e.matmul(
                                    pp.t[:], w.t[:, a_, k, :], hT.t[:, k, tl * T:(tl + 1) * T], start=(k == 0), stop=(k == 7)),
                                    reads=[w.b, hT_b[tl]], writes=[pp.b])
                        fw.op("act", lambda e: e.activation(s_.t[:], pg.t[:], AF.Silu), reads=[pg.b], writes=[s_.b])
                        fw.op("dve", lambda e: e.tensor_tensor(
                            act.t[:, hc, tl * T:(tl + 1) * T], pu.t[:], s_.t[:], ALU.mult),
                            reads=[pu.b, s_.b], writes=[act_b[tl]])
                it = 0
                for m in range(8):
                    if m + 1 < 8:
                        load_wd(m + 1)
                    w = wd[m % 2]
                    for tl in range(ntg):
                        ti = gi * ntg + tl
                        po = ps[4 + it % 2]
                        x = xc[it % 2]
                        it += 1
                        cap = self.xs[m * 128:(m + 1) * 128, ti * T:(ti + 1) * T]
                        fw.dma("sp", x.t[:], cap, reads=[self.xs_b[ti]], writes=[x.b], sem_buf=x.b)
                        for k in range(HC):
                            fw.op("pe", lambda e: e.matmul(
                                po.t[:], w.t[:, k, :], act.t[:, k, tl * T:(tl + 1) * T], start=(k == 0), stop=(k == HC - 1)),
                                reads=[w.b, act_b[tl]], writes=[po.b])
                        fw.op("dve", lambda e: e.tensor_tensor(x.t[:], po.t[:], x.t[:], ALU.add),
                              reads=[po.b, x.b], writes=[x.b])
                        fw.dma("sp", cap, x.t[:], reads=[x.b], writes=[self.xs_b[ti]], sem_buf=x.b)
                    if gi + 1 < ngrp and m % 2 == 1 and m // 2 < ntg:
                        norm_tile(gi + 1, m // 2)

    def final_norm(self, src):
        fw = self.fw
        sdram, sb = src
        with self.phase("fin") as ph:
            xin = [ph.sb("xin%d" % i, [128, 8, T], F32) for i in range(2)]
            yo = [ph.sb("yo%d" % i, [128, 8, T], F32) for i in range(2)]
            sq = ph.sb("sq", [128, 8, T], BF16)
            rstd = ph.sb("rstd", [128, T], F32)
            for ti in range(NT):
                x = xin[ti % 2]
                y = yo[ti % 2]
                fw.dma("sp", x.t[:], self.tile_ap(sdram, ti), reads=[sb[ti]], writes=[x.b], sem_buf=x.b)
                if self.final == "copy":
                    fw.dma("sp", self.tile_ap(self.y_out, ti), x.t[:], reads=[x.b], writes=[self.y_b[ti]], sem_buf=x.b)
                    continue
                self.rmsnorm(x, (lambda k, y=y: y.t[:, k, :], y.b), sq, rstd, self.ps[6], 8)
                fw.dma("sp", self.tile_ap(self.y_out, ti), y.t[:], reads=[y.b], writes=[self.y_b[ti]], sem_buf=y.b)

    def conv_layer(self, l, src):
        fw = self.fw
        i2 = l // 2
        sdram, sbufs = src
        ps = self.ps
        with self.phase("conv%d" % l) as ph:
            w1 = ph.sb("w1", [128, 8, 2048], BF16)
            w2 = ph.sb("w2", [128, 8, 1024], BF16)
            dg = ph.sb("dg", [128, 31 * 8, 128], BF16)
            dg_b = [Buf("dg0"), Buf("dg1")]
            xin = [ph.sb("xin0", [128, 8, T], F32)] * 2
            h = ph.sb("h", [128, 8, T], BF16)
            sq = h
            rstd = ph.sb("rstd", [128, T], F32)
            z = [ph.sb("z%d" % i, [128, 8, 30 + T], BF16) for i in range(2)]
            sg = [ph.sb("sg%d" % i, [128, T], F32) for i in range(2)]
            y = ph.sb("y", [128, 8, T], F32)
            ybf = h
            mean = ph.sb("mean", [128, T], F32)
            msq = ph.sb("msq", [128, T], F32)
            lrs = ph.sb("lrs", [128, T], F32)
            yn = sg
            s_bf = ph.sb("s_bf", [128, 8, T], BF16)
            ysq = s_bf
            for q in range(4):
                fw.dma("pool", w1.t[:, 2 * q:2 * q + 2, :].rearrange("p k c -> p (k c)"),
                       self.d_cw1[i2, :, 2 * q * 2048:(2 * q + 2) * 2048], writes=[w1.b], sem_buf=w1.b)
            for q in range(2):
                fw.dma("pool", w2.t[:, 4 * q:4 * q + 4, :].rearrange("p k c -> p (k c)"),
                       self.d_cw2[i2, :, 4 * q * 1024:(4 * q + 4) * 1024], writes=[w2.b], sem_buf=w2.b)
            for j in range(31 * 8):
                if j % 2 == 0:
                    fw.op("dve", lambda e, j=j: e.tensor_scalar(dg.t[:, j, :], self.identf.t[:], self.smc("cdw_%d" % i2, j), None, ALU.mult),
                          reads=[self.identf.b, self.sm.b], writes=[dg_b[0]])
                else:
                    fw.op("act", lambda e, j=j: e.activation(dg.t[:, j, :], self.identf.t[:], AF.Identity, scale=self.smc("cdw_%d" % i2, j)),
                          reads=[self.identf.b, self.sm.b], writes=[dg_b[1]])
            fw.op("pool", lambda e: e.memset(z[0].t[:, :, 0:30], 0.0), writes=[z[0].b])
            x = xin[0]
            y_b = [Buf("y%d" % m) for m in range(8)]
            sgl = sg[0]
            yns = [sg[1], msq]
            p1, p2 = ps[6], ps[7]

            def do_norm(ti):
                if ti == 0:
                    fw.dma("sp", x.t[:], self.tile_ap(sdram, ti), reads=[sbufs[ti]], writes=[x.b], sem_buf=x.b)
                self.rmsnorm(x, (lambda k: h.t[:, k, :], h.b), sq, rstd, ps[6], l)
                if ti + 1 < NT:
                    fw.dma("sp", x.t[:], self.tile_ap(sdram, ti + 1), reads=[sbufs[ti + 1]], writes=[x.b], sem_buf=x.b)

            def pw1_chunk(ti, m):
                zc = z[ti % 2]
                pa = ps[m % 2]
                pg = ps[2 + m % 2]
                for a_, pp in ((0, pa), (1, pg)):
                    for k in range(8):
                        fw.op("pe", lambda e: e.matmul(
                            pp.t[:], w1.t[:, k, a_ * 1024 + m * 128: a_ * 1024 + (m + 1) * 128], h.t[:, k, :],
                            start=(k == 0), stop=(k == 7)), reads=[w1.b, h.b], writes=[pp.b])
                return pa, pg

            def glu_chunk(ti, m, pa, pg):
                zc = z[ti % 2]
                fw.op("act", lambda e: e.activation(sgl.t[:], pg.t[:], AF.Sigmoid, bias=self.smc("cb1_%d" % i2, 8 + m)),
                      reads=[pg.b, self.sm.b], writes=[sgl.b])
                fw.op("dve", lambda e: e.scalar_tensor_tensor(
                    zc.t[:, m, 30:30 + T], pa.t[:], self.smc("cb1_%d" % i2, m), sgl.t[:], ALU.add, ALU.mult),
                    reads=[pa.b, sgl.b, self.sm.b], writes=[zc.b])

            def ln_chunk(ti, m):
                yy = yns[m % 2]
                fw.op("dve", lambda e: e.tensor_tensor(yy.t[:], y.t[:, m, :], mean.t[:], ALU.subtract),
                      reads=[y_b[m], mean.b], writes=[yy.b])
                fw.op("dve", lambda e: e.tensor_tensor(yy.t[:], yy.t[:], lrs.t[:], ALU.mult),
                      reads=[yy.b, lrs.b], writes=[yy.b])
                fw.op("act", lambda e: e.activation(s_bf.t[:, m, :], yy.t[:], AF.Silu,
                                                    bias=self.smc("clnb_%d" % i2, m), scale=self.smc("clng_%d" % i2, m)),
                      reads=[yy.b, self.sm.b], writes=[s_bf.b])
                src_c = sdram[m * 128:(m + 1) * 128, ti * T:(ti + 1) * T]
                fw.dma("sp", y.t[:, m, :], src_c, reads=[sbufs[ti]], writes=[y_b[m]], sem_buf=y_b[m])

            do_norm(0)
            for m in range(8):
                pa, pg = pw1_chunk(0, m)
                glu_chunk(0, m, pa, pg)
            for ti in range(NT):
                zc = z[ti % 2]
                zn = z[(ti + 1) % 2]
                last = (ti + 1 == NT)
                if not last:
                    fw.op("pool", lambda e: e.tensor_copy(zn.t[:, :, 0:30], zc.t[:, :, T:T + 30]), reads=[zc.b], writes=[zn.b])
                for m in range(8):
                    pc = ps[4 + m % 2]
                    for k in range(31):
                        fw.op("pe", lambda e: e.matmul(
                            pc.t[:], dg.t[:, k * 8 + m, :], zc.t[:, m, k:k + T], start=(k == 0), stop=(k == 30)),
                            reads=[dg_b[0], dg_b[1], zc.b], writes=[pc.b])
                    fw.op("act", lambda e: e.activation(y.t[:, m, :], pc.t[:], AF.Identity, bias=self.smc("cbdw_%d" % i2, m)),
                          reads=[pc.b, self.sm.b], writes=[y_b[m]])
                fw.op("dve", lambda e: e.tensor_copy(ybf.t[:], y.t[:]), reads=y_b, writes=[ybf.b])
                fw.op("act", lambda e: e.activation(ysq.t[:], y.t[:], AF.Square), reads=y_b, writes=[ysq.b])
                for k in range(8):
                    fw.op("pe", lambda e: e.matmul(p1.t[:], self.ones_bf(), ybf.t[:, k, :], start=(k == 0), stop=(k == 7)),
                          reads=[ybf.b, self.bc.b], writes=[p1.b])
                for k in range(8):
                    fw.op("pe", lambda e: e.matmul(p2.t[:], self.ones_bf(), ysq.t[:, k, :], start=(k == 0), stop=(k == 7)),
                          reads=[ysq.b, self.bc.b], writes=[p2.b])
                fw.op("dve", lambda e: e.tensor_scalar(mean.t[:], p1.t[:], 1.0 / 1024, None, ALU.mult), reads=[p1.b], writes=[mean.b])
                fw.op("dve", lambda e: e.tensor_tensor(lrs.t[:], mean.t[:], mean.t[:], ALU.mult), reads=[mean.b], writes=[lrs.b])
                fw.op("dve", lambda e: e.scalar_tensor_tensor(lrs.t[:], p2.t[:], 1.0 / 1024, lrs.t[:], ALU.mult, ALU.subtract),
                      reads=[p2.b, lrs.b], writes=[lrs.b])
                fw.op("act", lambda e: e.activation(lrs.t[:], lrs.t[:], AF.Ln, bias=self.smc("eps5"), scale=1.0),
                      reads=[lrs.b, self.sm.b], writes=[lrs.b])
                fw.op("act", lambda e: e.activation(lrs.t[:], lrs.t[:], AF.Exp, scale=-0.5), reads=[lrs.b], writes=[lrs.b])
                if not last:
                    do_norm(ti + 1)
                for m in range(8):
                    if not last:
                        pa, pg = pw1_chunk(ti + 1, m)
                    ln_chunk(ti, m)
                    if not last:
                        glu_chunk(ti + 1, m, pa, pg)
                for m in range(8):
                    po = ps[m % 2]
                    for k in range(8):
                        fw.op("pe", lambda e: e.matmul(po.t[:], w2.t[:, k, m * 128:(m + 1) * 128], s_bf.t[:, k, :],
                                                       start=(k == 0), stop=(k == 7)),
                              reads=[w2.b, s_bf.b], writes=[po.b])
                    dst_c = self.xs[m * 128:(m + 1) * 128, ti * T:(ti + 1) * T]
                    fw.op("dve", lambda e: e.scalar_tensor_tensor(
                        y.t[:, m, :], po.t[:], self.smc("cb2_%d" % i2, m), y.t[:, m, :], ALU.add, ALU.add),
                        reads=[po.b, y_b[m], self.sm.b], writes=[y_b[m]])
                    fw.dma("sp", dst_c, y.t[:, m, :], reads=[y_b[m]], writes=[self.xs_b[ti]], sem_buf=y_b[m])

    def even_layer(self, l, src):
        fw = self.fw
        import os
        STAGE = int(os.environ.get('EVEN_STAGE', '9'))
        SUB = os.environ.get('EVEN_SUB', 'NQMKGVP')
        BR = os.environ.get('EVEN_BR', 'csw')
        HEADS = [int(c) for c in os.environ.get('EVEN_HEADS', '01234567')]
        i2 = l // 2
        sdram, sbufs = src
        ps = self.ps
        bc = self.bc
        with self.phase("nsa%d" % l) as ph:
            win = ph.sb("win", [128, 8, 1816], BF16)
            wo = ph.sb("wo", [128, 8, 1024], BF16)
            w1c = ph.sb("w1c", [128, 2, 2, 32, 128], BF16)
            post = ph.sb("post", [128, 2, 32, 2], BF16)
            w2c = ph.sb("w2c", [128, 2, 64], BF16)
            pw = ph.sb("pw", [128, 4, 128], BF16)
            KS = [ph.sb("ks%d" % g, [128, S], BF16) for g in range(2)]
            KW = [ph.sb("kw%d" % g, [128, 1024], BF16) for g in range(2)]
            VS = ph.sb("vs", [128, 2, 32, 128], BF16)
            VW = ph.sb("vw", [128, 2, 8, 128], BF16)
            KCT = ph.sb("kct", [128, 16 + T], BF16)
            VCT = ph.sb("vct", [128, 16 + T], BF16)
            KC = [ph.sb("kc%d" % g, [128, 256], BF16) for g in range(2)]
            VC = ph.sb("vc", [128, 2, 2, 128], BF16)
            Qs = [ph.sb("q%d" % h, [128, T], BF16) for h in range(8)]
            x = ph.sb("x", [128, 8, T], F32)
            h_ = ph.sb("h", [128, 8, T], BF16)
            sgT = ph.sb("sgT", [128, T], BF16)
            pa = [ph.sb("pa%d" % i, [128, 16 + T], F32) for i in range(3)]
            rstd = TB(pa[2].t[:, 0:T], pa[2].b)
            halo = ph.sb("halo", [128, 4, 16], F32)
            dT = ph.sb("dT", [128, 4, T], BF16)
            yp = ph.sb("yp", [128, 4, T], BF16)
            oT = ph.sb("oT", [128, 4, T], BF16)
            E = [ph.sb("E%d" % i, [128, T], BF16) for i in range(3)]
            zc = ph.sb("zc", [128, T], F32)
            gz = ph.sb("gz", [128, T], F32)
            acc2 = [ph.sb("acc%d" % i, [128, T], F32) for i in range(2)]
            zc_b = [Buf("zc0"), Buf("zc1")]
            gz_b = [Buf("gz0"), Buf("gz1")]
            acc_b = [Buf("acc%d" % i) for i in range(4)]
            impa = ph.sb("impa", [128, 4, 64], F32)
            impf = ph.sb("impf", [128, 64], F32)
            top8 = ph.sb("top8", [128, 8], F32)
            selb = [ph.sb("selb0", [128, 128], F32)] * 2
            rz4 = ph.sb("rz4", [128, 4], F32)
            posb = ph.sb("posb", [128, 2], F32)
            gus = [ph.sb("gu%d" % i, [128, 32], F32) for i in range(4)]
            gvs = [ph.sb("gv%d" % i, [128, 32], F32) for i in range(4)]
            gss = [ph.sb("gs%d" % i, [128, 32], F32) for i in range(4)]
            Gs = [ph.sb("G%d" % i, [128, 128], BF16) for i in range(4)]
            gu = gus[0]
            for q in range(4):
                fw.dma("pool", win.t[:, 2 * q:2 * q + 2, :].rearrange("p k c -> p (k c)"),
                       self.d_win[i2, :, 2 * q * 1816:(2 * q + 2) * 1816], writes=[win.b], sem_buf=win.b)
            for q in range(2):
                fw.dma("pool", wo.t[:, 4 * q:4 * q + 4, :].rearrange("p k c -> p (k c)"),
                       self.d_wo[i2, :, 4 * q * 1024:(4 * q + 4) * 1024], writes=[wo.b], sem_buf=wo.b)
            for g_ in range(2):
                for s_ in range(2):
                    for q in range(2):
                        o_ = ((g_ * 2 + s_) * 32 + 16 * q) * 128
                        fw.dma("pool", w1c.t[:, g_, s_, 16 * q:16 * q + 16, :].rearrange("p l c -> p (l c)"),
                               self.d_w1c[i2, :, o_:o_ + 2048], writes=[w1c.b], sem_buf=w1c.b)
            fw.dma("pool", post.t[:].rearrange("p s l r -> p (s l r)"), self.d_post[i2], writes=[post.b], sem_buf=post.b)
            fw.dma("pool", w2c.t[:].rearrange("p s d -> p (s d)"), self.d_w2c[i2], writes=[w2c.b], sem_buf=w2c.b)
            fw.dma("pool", pw.t[:].rearrange("p m e -> p (m e)"), self.d_pw[i2], writes=[pw.b], sem_buf=pw.b)
            for g in range(2):
                fw.dma("pool", KS[g].t[64:128, :], self.d_ex[:, :], writes=[KS[g].b], sem_buf=KS[g].b)
                fw.dma("pool", KW[g].t[64:128, :], self.d_e0[:, 0:1024], writes=[KW[g].b], sem_buf=KW[g].b)
                fw.dma("pool", KC[g].t[64:128, :], self.d_e0[:, 0:256], writes=[KC[g].b], sem_buf=KC[g].b)
                fw.op("dve", lambda e, g=g: e.memset(KC[g].t[0:64, :], 0.0), writes=[KC[g].b])
            fw.op("dve", lambda e: e.memset(VS.t[:, :, :, 64:128], 1.0), writes=[VS.b])
            fw.op("dve", lambda e: e.memset(VW.t[:, :, :, 64:128], 1.0), writes=[VW.b])
            fw.op("dve", lambda e: e.memset(VC.t[:, :, :, 0:64], 0.0), writes=[VC.b])
            fw.op("dve", lambda e: e.memset(VC.t[:, :, :, 64:128], 1.0), writes=[VC.b])
            fw.op("dve", lambda e: e.memset(KCT.t[:, 0:16], 0.0), writes=[KCT.b])
            fw.op("dve", lambda e: e.memset(VCT.t[:, 0:16], 0.0), writes=[VCT.b])
            fw.op("dve", lambda e: e.memset(halo.t[:], 0.0), writes=[halo.b])
            fw.op("dve", lambda e: e.memset(oT.t[:], 0.0), writes=[oT.b])
            for G_ in Gs:
                fw.op("dve", lambda e: e.memset(G_.t[:], 0.0), writes=[G_.b])
            fw.op("dve", lambda e: e.memset(selb[0].t[:], 0.0), writes=[selb[0].b])
            for s_ in range(2):
                pp = ps[6 + s_]
                for ll in range(32):
                    fw.op("pe", lambda e, s_=s_, ll=ll, pp=pp: e.matmul(pp.t[:, 0:2], w1c.t[0:64, 0, s_, ll, :], post.t[0:64, s_, ll, :],
                                                                      start=(ll == 0), stop=(ll == 31)),
                          reads=[w1c.b, post.b], writes=[pp.b])
                fw.op("dve", lambda e, s_=s_, pp=pp: e.tensor_copy(posb.t[:, s_:s_ + 1], pp.t[:, 0:1]), reads=[pp.b], writes=[posb.b])

            fw.barrier()
            gen_i = [0]

            def gbank():
                gen_i[0] += 1
                return ps[6 + gen_i[0] % 2]

            ecnt = [0]
            import collections
            pend = collections.deque()
            scnt = [0]
            ocnt = [0]

            fw.op("dve", lambda e: e.memset(yp.t[:], 0.0), writes=[yp.b])
            for i in range(NT):
                c0 = i * T
                fw.mute = False
                if i == 0:
                    fw.dma("sp", x.t[:], self.tile_ap(sdram, i), reads=[sbufs[i]], writes=[x.b], sem_buf=x.b)
                fw.mute = STAGE < 1 or 'N' not in SUB
                self.rmsnorm(x, (lambda k: h_.t[:, k, :], h_.b), h_, rstd, ps[5], l)
                fw.mute = False
                if i + 1 < NT:
                    fw.dma("sp", x.t[:], self.tile_ap(sdram, i + 1), reads=[sbufs[i + 1]], writes=[x.b], sem_buf=x.b)

                def proj(col0, ncol):
                    pp = gbank()
                    for k in range(8):
                        fw.op("pe", lambda e, k=k, pp=pp: e.matmul(pp.t[0:ncol, :], win.t[:, k, col0:col0 + ncol], h_.t[:, k, :],
                                                                  start=(k == 0), stop=(k == 7)),
                              reads=[win.b, h_.b], writes=[pp.b])
                    return pp
                fw.mute = STAGE < 1 or 'Q' not in SUB
                for m in range(4):
                    pp = proj(m * 128, 128)
                    ha, hb2 = 2 * m, 2 * m + 1
                    fw.op("act", lambda e, pp=pp, ha=ha: e.activation(Qs[ha].t[0:64, :], pp.t[0:64, :], AF.Identity, scale=0.125),
                          reads=[pp.b], writes=[Qs[ha].b])
                    fw.op("dve", lambda e, pp=pp, hb2=hb2: e.tensor_scalar(Qs[hb2].t[0:64, :], pp.t[64:128, :], 0.125, None, ALU.mult),
                          reads=[pp.b], writes=[Qs[hb2].b])
                fw.mute = STAGE < 1 or 'M' not in SUB
                for hh in range(8):
                    fw.op("dve", lambda e, hh=hh: e.tensor_scalar(Qs[hh].t[64:128, :], self.smc("arow", 0, 512)[64:128, :], -SLOPES[hh], None, ALU.mult),
                          reads=[self.sm.b], writes=[Qs[hh].b])
                fw.mute = STAGE < 1 or 'K' not in SUB
                pp = proj(512, 128)
                fw.op("act", lambda e, pp=pp: e.activation(KCT.t[:, 16:16 + T], pp.t[:], AF.Identity), reads=[pp.b], writes=[KCT.b])
                pp = proj(640, 128)
                fw.op("dve", lambda e, pp=pp: e.tensor_copy(VCT.t[:, 16:16 + T], pp.t[:]), reads=[pp.b], writes=[VCT.b])
                pp = proj(768, 128)
                fw.op("act", lambda e, pp=pp: e.activation(KS[0].t[0:64, c0:c0 + T], pp.t[0:64, :], AF.Identity), reads=[pp.b], writes=[KS[0].b])
                fw.op("dve", lambda e, pp=pp: e.tensor_copy(KS[1].t[0:64, c0:c0 + T], pp.t[64:128, :]), reads=[pp.b], writes=[KS[1].b])
                pp = proj(896, 128)
                r0 = (i % 2) * T
                fw.op("act", lambda e, pp=pp: e.activation(KW[0].t[0:64, r0:r0 + T], pp.t[0:64, :], AF.Identity), reads=[pp.b], writes=[KW[0].b])
                fw.op("dve", lambda e, pp=pp: e.tensor_copy(KW[1].t[0:64, r0:r0 + T], pp.t[64:128, :]), reads=[pp.b], writes=[KW[1].b])
                fw.mute = STAGE < 1 or 'G' not in SUB
                pp = proj(1024, 128)
                fw.op("act", lambda e, pp=pp: e.activation(sgT.t[:], pp.t[:], AF.Sigmoid), reads=[pp.b], writes=[sgT.b])
                fw.mute = STAGE < 1 or 'V' not in SUB
                for s_ in range(4):
                    cs = 4 * i + s_
                    for half in range(2):
                        pp = gbank()
                        for k in range(8):
                            fw.op("pe", lambda e, k=k, pp=pp, s_=s_, half=half: e.matmul(
                                pp.t[:, 0:128], h_.t[:, k, s_ * 128:(s_ + 1) * 128],
                                win.t[:, k, 1560 + half * 128:1560 + (half + 1) * 128], start=(k == 0), stop=(k == 7)),
                                reads=[win.b, h_.b], writes=[pp.b])
                        if half == 0:
                            fw.op("act", lambda e, pp=pp, cs=cs: e.activation(VS.t[:, :, cs, 0:64], pp.t[:, 0:128].rearrange("p (g d) -> p g d", g=2), AF.Identity),
                                  reads=[pp.b], writes=[VS.b])
                        else:
                            fw.op("dve", lambda e, pp=pp, cs=cs: e.tensor_copy(VW.t[:, :, cs % 8, 0:64], pp.t[:, 0:128].rearrange("p (g d) -> p g d", g=2)),
                                  reads=[pp.b], writes=[VW.b])
                fw.mute = STAGE < 1 or 'P' not in SUB
                for m in range(4):
                    w_ = 2 ** (m + 1)
                    pp = proj(1048 + m * 128, 128)
                    A_, B_, C_ = pa
                    fw.op("act", lambda e, pp=pp: e.activation(A_.t[:, 16:16 + T], pp.t[:], AF.Identity), reads=[pp.b], writes=[A_.b])
                    fw.op("dve", lambda e, m=m: e.tensor_copy(A_.t[:, 0:16], halo.t[:, m, :]), reads=[halo.b], writes=[A_.b])
                    fw.op("dve", lambda e, m=m: e.tensor_copy(halo.t[:, m, :], A_.t[:, T:T + 16]), reads=[A_.b], writes=[halo.b])
                    cur = A_
                    for st_ in range(m + 1):
                        sh = 2 ** st_
                        lo = 2 ** (st_ + 1) - 1
                        dst = B_ if st_ % 2 == 0 else C_
                        fw.op("dve", lambda e, cur=cur, dst=dst, sh=sh, lo=lo: e.tensor_tensor(
                            dst.t[:, lo:16 + T], cur.t[:, lo:16 + T], cur.t[:, lo - sh:16 + T - sh], ALU.add),
                            reads=[cur.b], writes=[dst.b])
                        cur = dst
                    fw.op("dve", lambda e, cur=cur, m=m, w_=w_: e.scalar_tensor_tensor(dT.t[:, m, :], cur.t[:, 16:16 + T], 1.0 / w_, A_.t[:, 16:16 + T],
                                                                                     ALU.mult, ALU.subtract),
                          reads=[cur.b, A_.b], writes=[dT.b])
                    if i == 0:
                        fw.op("dve", lambda e, cur=cur, m=m: e.tensor_tensor(gu.t[:, 0:16], cur.t[:, 16:32], self.smc("rc15", m * 16, 16), ALU.mult),
                              reads=[cur.b, self.sm.b], writes=[gu.b])
                        fw.op("dve", lambda e, m=m: e.tensor_tensor(dT.t[:, m, 0:16], gu.t[:, 0:16], A_.t[:, 16:32], ALU.subtract),
                              reads=[gu.b, A_.b], writes=[dT.b])
                    pq = gbank()
                    fw.op("pe", lambda e, pq=pq, m=m: e.matmul(pq.t[:], pw.t[:, m, :], dT.t[:, m, :], start=True, stop=True),
                          reads=[pw.b, dT.b], writes=[pq.b])
                    fw.op("act", lambda e, pq=pq, m=m: e.activation(yp.t[:, m, :], pq.t[:], AF.Identity, scale=self.smc("pscale_%d" % i2, m)),
                          reads=[pq.b, self.sm.b], writes=[yp.b])
                fw.mute = STAGE < 2
                ccn = (32 * i) // 128
                pn = (32 * i) % 128
                for s_ in range(2):
                    XT = KCT if s_ == 0 else VCT
                    for g in range(2):
                        gu, gv, gs, G = gus[s_ * 2 + g], gvs[s_ * 2 + g], gss[s_ * 2 + g], Gs[s_ * 2 + g]
                        pp = gbank()
                        for ll in range(32):
                            fw.op("pe", lambda e, ll=ll, pp=pp, g=g, s_=s_, XT=XT: e.matmul(
                                pp.t[:, 0:32], w1c.t[:, g, s_, ll, :], XT.t[:, ll:ll + 497:16],
                                start=(ll == 0), stop=(ll == 31)), reads=[w1c.b, XT.b], writes=[pp.b])
                        fw.op("act", lambda e, pp=pp, s_=s_: e.activation(gu.t[:], pp.t[:, 0:32], AF.Identity, bias=posb.t[:, s_:s_ + 1]),
                              reads=[pp.b, posb.b], writes=[gu.b])
                        fw.op("dve", lambda e: e.tensor_tensor(gv.t[:], gu.t[:], gu.t[:], ALU.mult), reads=[gu.b], writes=[gv.b])
                        fw.op("dve", lambda e: e.tensor_scalar(gv.t[:], gv.t[:], 0.044715, 1.0, ALU.mult, ALU.add), reads=[gv.b], writes=[gv.b])
                        fw.op("dve", lambda e: e.tensor_tensor(gv.t[:], gv.t[:], gu.t[:], ALU.mult), reads=[gv.b, gu.b], writes=[gv.b])
                        fw.op("act", lambda e: e.activation(gs.t[:], gv.t[:], AF.Sigmoid, scale=1.5957691216057308), reads=[gv.b], writes=[gs.b])
                        fw.op("dve", lambda e: e.tensor_tensor(G.t[:, 0:32], gu.t[:], gs.t[:], ALU.mult), reads=[gu.b, gs.b], writes=[G.b])
                        pq = gbank()
                        if s_ == 0:
                            fw.op("pe", lambda e, pq=pq: e.matmul(pq.t[:, 0:32], w2c.t[:].rearrange("p s d -> p (s d)"), G.t[:, 0:32], start=True, stop=True),
                                  reads=[w2c.b, G.b], writes=[pq.b])
                            fw.op("dve", lambda e, pq=pq, g=g: e.tensor_copy(KC[g].t[0:64, 32 * i:32 * i + 32], pq.t[0:64, 0:32]),
                                  reads=[pq.b], writes=[KC[g].b])
                        else:
                            fw.op("pe", lambda e, pq=pq: e.matmul(pq.t[:, 0:64], G.t[:], w2c.t[:, 1, :], start=True, stop=True),
                                  reads=[w2c.b, G.b], writes=[pq.b])
                            fw.op("dve", lambda e, pq=pq, g=g: e.tensor_copy(VC.t[pn:pn + 32, g, ccn, 0:64], pq.t[0:32, 0:64]),
                                  reads=[pq.b], writes=[VC.b])
                            if i == 0:
                                fw.op("dve", lambda e, g=g: e.memset(VC.t[0:1, g, 0, :], 0.0), writes=[VC.b])
                fw.op("pool", lambda e: e.tensor_copy(KCT.t[:, 0:16], KCT.t[:, T:T + 16]), reads=[KCT.b], writes=[KCT.b])
                fw.op("pool", lambda e: e.tensor_copy(VCT.t[:, 0:16], VCT.t[:, T:T + 16]), reads=[VCT.b], writes=[VCT.b])

                fw.mute = STAGE < 3
                def pump(lag):
                    while pend and (pend[0][0] == "fin" or sum(1 for k_, _ in pend if k_ == "pv") > lag):
                        pend.popleft()[1]()

                def run_branch(hh, chunks, fin):
                    po = ps[3 + ocnt[0] % 2]
                    ocnt[0] += 1
                    n = len(chunks)
                    used = []
                    for idx, (kap, kb, vap, vb, bcol, mask) in enumerate(chunks):
                        sb_ = ps[scnt[0] % 3]
                        scnt[0] += 1
                        eb = E[ecnt[0] % len(E)]
                        ecnt[0] += 1
                        fw.op("pe", lambda e: e.matmul(sb_.t[:], kap, Qs[hh].t[:], start=True, stop=(mask is None)),
                              reads=[kb, Qs[hh].b], writes=[sb_.b])
                        if mask is not None:
                            fw.op("pe", lambda e: e.matmul(sb_.t[:], self.ident_bf(), mask, start=False, stop=True),
                                  reads=[bc.b], writes=[sb_.b])
                        fw.op("act", lambda e: e.activation(eb.t[:], sb_.t[:], AF.Exp, bias=self.sm.t[:, bcol:bcol + 1]),
                              reads=[sb_.b, self.sm.b], writes=[eb.b])
                        used.append(eb)

                        def pv(idx=idx, vap=vap, vb=vb, eb=eb):
                            fw.op("pe", lambda e: e.matmul(po.t[:], vap, eb.t[:], start=(idx == 0), stop=(idx == n - 1)),
                                  reads=[vb, eb.b], writes=[po.b])
                        pend.append(("pv", pv))
                        if idx == n - 1:
                            pend.append(("fin", lambda: fin(po, used)))
                        pump(2)

                def combine(hh, b, po):
                    pg = gbank()
                    hp = (hh % 2) * 64
                    hs = slice(hp, hp + 64)
                    zcb, gzb = zc_b[hh % 2], gz_b[hh % 2]
                    ac = acc2[(hh % 4) // 2]
                    acb = acc_b[hh % 4]
                    on = ("csw"[b] in BR)
                    if on:
                        fw.op("pe", lambda e: e.matmul(pg.t[:], bc.t[:, BC["eg"] + (hh * 3 + b) * 64: BC["eg"] + (hh * 3 + b) * 64 + 128], sgT.t[:],
                                                       start=True, stop=True), reads=[bc.b, sgT.b], writes=[pg.b])
                        if b == 0 or (b == 1 and i <= 1):
                            fw.op("act", lambda e: e.activation(zc.t[hs, :], po.t[64:128, :], AF.Ln, bias=self.smc("eps18")[hs, :]),
                                  reads=[po.b, self.sm.b], writes=[zcb])
                            fw.op("act", lambda e: e.activation(zc.t[hs, :], zc.t[hs, :], AF.Exp, scale=-1.0), reads=[zcb], writes=[zcb])
                        else:
                            fw.op("dve", lambda e: e.tensor_scalar(zc.t[hs, :], po.t[64:128, :], 1e-30, None, ALU.add), reads=[po.b], writes=[zcb])
                            fw.op("dve", lambda e: e.reciprocal(zc.t[hs, :], zc.t[hs, :]), reads=[zcb], writes=[zcb])
                        fw.op("dve", lambda e: e.tensor_tensor(gz.t[hs, :], pg.t[0:64, :], zc.t[hs, :], ALU.mult), reads=[pg.b, zcb], writes=[gzb])
                    if b == 0:
                        if on:
                            fw.op("dve", lambda e: e.tensor_tensor(ac.t[hs, :], po.t[0:64, :], gz.t[hs, :], ALU.mult), reads=[po.b, gzb], writes=[acb])
                        else:
                            fw.op("dve", lambda e: e.memset(ac.t[hs, :], 0.0), writes=[acb])
                    elif b == 1:
                        if on:
                            fw.op("dve", lambda e: e.tensor_tensor(gz.t[hs, :], po.t[0:64, :], gz.t[hs, :], ALU.mult), reads=[po.b, gzb], writes=[gzb])
                            fw.op("pool", lambda e: e.tensor_tensor(ac.t[hs, :], ac.t[hs, :], gz.t[hs, :], ALU.add), reads=[acb, gzb], writes=[acb])
                    else:
                        if on:
                            fw.op("dve", lambda e: e.tensor_tensor(gz.t[hs, :], po.t[0:64, :], gz.t[hs, :], ALU.mult), reads=[po.b, gzb], writes=[gzb])
                            fw.op("dve", lambda e: e.tensor_tensor(oT.t[hs, hh // 2, :], ac.t[hs, :], gz.t[hs, :], ALU.add),
                                  reads=[acb, gzb], writes=[oT.b])
                        else:
                            fw.op("dve", lambda e: e.tensor_copy(oT.t[hs, hh // 2, :], ac.t[hs, :]), reads=[acb], writes=[oT.b])

                for g in range(2):
                    fw.mute = STAGE < 3
                    ccs = [0] if i < 4 else [0, 1]
                    for hi_, hh in enumerate(range(4 * g, 4 * g + 4)):
                        chunks = []
                        for cc in ccs:
                            x0 = 512 * i - 2048 * cc
                            mask = bc.t[:, BC["g1"] + x0: BC["g1"] + x0 + 512] if x0 <= 1536 else None
                            chunks.append((KC[g].t[:, cc * 128:(cc + 1) * 128], KC[g].b, VC.t[:, g, cc, :], VC.b,
                                           SM["bias_cmp"] + (hh * 2 + cc) * 8 + i, mask))
                        def cmp_fin(po, used, hh=hh, hi_=hi_, ccs=ccs):
                            pu = ps[5]
                            for s_ in range(4):
                                for ci, cc in enumerate(ccs):
                                    fw.op("pe", lambda e: e.matmul(
                                        pu.t[:, s_ * 65:(s_ + 1) * 65], used[ci].t[:, s_ * 128:(s_ + 1) * 128],
                                        bc.t[:, BC["ova"] + cc * 65: BC["ova"] + (cc + 1) * 65], start=(ci == 0), stop=(ci == len(ccs) - 1)),
                                        reads=[used[ci].b, bc.b], writes=[pu.b])
                            combine(hh, 0, po)
                            zsl = pu.t[:, 0:260].rearrange("p (s c) -> p s c", c=65)[:, :, 64]
                            fw.op("dve", lambda e: e.tensor_scalar(rz4.t[:], zsl, 1e-30, None, ALU.add), reads=[pu.b], writes=[rz4.b])
                            fw.op("dve", lambda e: e.reciprocal(rz4.t[:], rz4.t[:]), reads=[rz4.b], writes=[rz4.b])
                            for s_ in range(4):
                                if hi_ == 0:
                                    fw.op("dve", lambda e: e.tensor_scalar(impa.t[:, s_, :], pu.t[:, s_ * 65:s_ * 65 + 64], rz4.t[:, s_:s_ + 1], None, ALU.mult),
                                          reads=[pu.b, rz4.b], writes=[impa.b])
                                else:
                                    fw.op("dve", lambda e: e.scalar_tensor_tensor(impa.t[:, s_, :], pu.t[:, s_ * 65:s_ * 65 + 64], rz4.t[:, s_:s_ + 1],
                                                                                 impa.t[:, s_, :], ALU.mult, ALU.add),
                                          reads=[pu.b, rz4.b, impa.b], writes=[impa.b])
                        run_branch(hh, chunks, cmp_fin)
                        pump(0)
                    pump(0)
                    fw.mute = STAGE < 4
                    pT = ps[5]
                    for s_ in range(4):
                        sub = 4 * i + s_
                        fo = SM["fbw"] + 62 - 2 * sub
                        sl_ = selb[s_ % 2]
                        fw.op("dve", lambda e, s_=s_, fo=fo: e.tensor_tensor(impf.t[:], impa.t[:, s_, :], self.sm.t[:, fo:fo + 64], ALU.add),
                              reads=[impa.b, self.sm.b], writes=[impf.b])
                        if sub >= 1:
                            fw.op("dve", lambda e: e.tensor_scalar(impf.t[:, 0:1], impf.t[:, 0:1], 1e4, None, ALU.add), reads=[impf.b], writes=[impf.b])
                        fw.op("dve", lambda e: e.max(top8.t[:], impf.t[:]), reads=[impf.b], writes=[top8.b])
                        fw.op("dve", lambda e, sl_=sl_: e.tensor_scalar(sl_.t[:, 0:64], impf.t[:], top8.t[:, 7:8], NEGB, ALU.is_lt, ALU.mult),
                              reads=[impf.b, top8.b], writes=[sl_.b])
                        fw.op("pe", lambda e, s_=s_, sl_=sl_: e.transpose(pT.t[:, s_ * 128:(s_ + 1) * 128], sl_.t[:], self.identf.t[:]),
                              reads=[sl_.b, self.identf.b], writes=[pT.b])
                    for hh in range(4 * g, 4 * g + 4):
                        fw.op("dve", lambda e, hh=hh: e.scalar_tensor_tensor(Qs[hh].t[64:128, :], self.smc("arow", 0, 512)[0:64, :], -SLOPES[hh], pT.t[0:64, :],
                                                                           ALU.mult, ALU.add),
                              reads=[self.sm.b, pT.b], writes=[Qs[hh].b])
                    fw.mute = STAGE < 5
                    for hh in range(4 * g, 4 * g + 4):
                        if hh not in HEADS:
                            continue
                        chunks = []
                        for c in range(0, 4 * i + 4):
                            dl = c - 4 * i
                            mask = None
                            if dl >= 0:
                                st0 = BC["caus"] + 384 - 128 * dl
                                mask = bc.t[:, st0:st0 + 512]
                            chunks.append((KS[g].t[:, c * 128:(c + 1) * 128], KS[g].b, VS.t[:, g, c, :], VS.b,
                                           SM["bias_sw"] + hh * 32 + dl + 28, mask))
                        run_branch(hh, chunks, lambda po, used, hh=hh: combine(hh, 1, po))
                        chunks = []
                        for c in range(max(0, 4 * i - 4), 4 * i + 4):
                            dl = c - 4 * i
                            if dl >= 0:
                                st0 = BC["caus"] + 384 - 128 * dl
                            else:
                                st0 = BC["far"] + 384 - 128 * (dl + 4)
                            mask = bc.t[:, st0:st0 + 512]
                            sl8 = c % 8
                            chunks.append((KW[g].t[:, sl8 * 128:(sl8 + 1) * 128], KW[g].b, VW.t[:, g, sl8, :], VW.b,
                                           SM["bias_sw"] + hh * 32 + dl + 28, mask))
                        run_branch(hh, chunks, lambda po, used, hh=hh: combine(hh, 2, po))
                pump(0)
                fw.mute = False
                for mo in range(8):
                    pp = gbank()
                    for k in range(8):
                        rhs = oT.t[:, k, :] if k < 4 else yp.t[:, k - 4, :]
                        rb = oT.b if k < 4 else yp.b
                        fw.op("pe", lambda e, k=k, pp=pp, rhs=rhs, mo=mo: e.matmul(pp.t[:], wo.t[:, k, mo * 128:(mo + 1) * 128], rhs,
                                                                                  start=(k == 0), stop=(k == 7)),
                              reads=[wo.b, rb], writes=[pp.b])
                    xq = pa[mo % 2]
                    src_c = sdram[mo * 128:(mo + 1) * 128, i * T:(i + 1) * T]
                    dst_c = self.xs[mo * 128:(mo + 1) * 128, i * T:(i + 1) * T]
                    fw.dma("sp", xq.t[:, 0:T], src_c, reads=[sbufs[i]], writes=[xq.b], sem_buf=xq.b)
                    fw.op("dve", lambda e, pp=pp, mo=mo: e.tensor_tensor(xq.t[:, 0:T], pp.t[:], xq.t[:, 0:T], ALU.add),
                          reads=[pp.b, xq.b], writes=[xq.b])
                    fw.dma("sp", dst_c, xq.t[:, 0:T], reads=[xq.b], writes=[self.xs_b[i]], sem_buf=xq.b)


def _prep_weights(inp):
    f = lambda a: np.ascontiguousarray(a, dtype=np.float32)
    out = {}
    sm, bc, ex, e0 = make_consts()

    def pc(v):
        return np.asarray(v, np.float32).reshape(-1, 128).T

    for l in range(4):
        sm[:, SM["normg"] + l * 8: SM["normg"] + (l + 1) * 8] = pc(inp["mix_norm"][l])
        sm[:, SM["normg"] + (4 + l) * 8: SM["normg"] + (5 + l) * 8] = pc(inp["ffn_norm"][l])
    sm[:, SM["normg"] + 64: SM["normg"] + 72] = pc(inp["final_norm"])
    for i in range(2):
        sm[:, SM["cb1_%d" % i]: SM["cb1_%d" % i] + 16] = pc(inp["conv_b_pw1"][i])
        sm[:, SM["cbdw_%d" % i]: SM["cbdw_%d" % i] + 8] = pc(inp["conv_b_dw"][i])
        sm[:, SM["clng_%d" % i]: SM["clng_%d" % i] + 8] = pc(inp["conv_ln_g"][i])
        sm[:, SM["clnb_%d" % i]: SM["clnb_%d" % i] + 8] = pc(inp["conv_ln_b"][i])
        sm[:, SM["cb2_%d" % i]: SM["cb2_%d" % i] + 8] = pc(inp["conv_b_pw2"][i])
        wdw = np.asarray(inp["conv_w_dw"][i], np.float32)
        sm[:, SM["cdw_%d" % i]: SM["cdw_%d" % i] + 248] = wdw.reshape(31, 8, 128).transpose(2, 0, 1).reshape(128, 248)
        sm[:, SM["pscale_%d" % i]: SM["pscale_%d" % i] + 4] = pc(inp["pool_scale"][i])
    out["smalls"] = f(sm)
    out["bconst"] = f(bc)
    out["exrows"] = f(ex)
    out["e0rows"] = f(e0)
    wg = np.asarray(inp["ffn_w_gate"], np.float32).reshape(4, 8, 128, HC, 128)
    wu = np.asarray(inp["ffn_w_up"], np.float32).reshape(4, 8, 128, HC, 128)
    wgu = np.stack([wg, wu], axis=0)
    out["wgu"] = f(wgu.transpose(1, 4, 3, 0, 2, 5).reshape(4, HC, 128, 2 * 8 * 128))
    wd = np.asarray(inp["ffn_w_down"], np.float32).reshape(4, HC, 128, 8, 128)
    out["wd"] = f(wd.transpose(0, 3, 2, 1, 4).reshape(4, 8, 128, HC * 128))
    perm = np.concatenate([np.arange(0, 512), np.arange(512, 640), np.arange(640, 768), np.arange(768, 896),
                           np.arange(1024, 1152), np.arange(1280, 1304), np.arange(1304, 1816),
                           np.arange(896, 1024), np.arange(1152, 1280)])
    win = np.asarray(inp["nsa_w_in"], np.float32)[:, :, perm].reshape(2, 8, 128, 1816)
    out["win"] = f(win.transpose(0, 2, 1, 3).reshape(2, 128, 8 * 1816))
    wo = np.asarray(inp["mix_w_out"], np.float32).reshape(2, 8, 128, 1024)
    out["wo"] = f(wo.transpose(0, 2, 1, 3).reshape(2, 128, 8 * 1024))
    w1c = np.stack([np.asarray(inp["cmp_k_w1"], np.float32), np.asarray(inp["cmp_v_w1"], np.float32)], axis=1)
    w1c = w1c.reshape(2, 2, 32, 64, 128).transpose(0, 3, 1, 2, 4)
    w1z = np.zeros((2, 128, 2, 2, 32, 128), np.float32)
    w1z[:, 0:64, 0] = w1c
    w1z[:, 64:128, 1] = w1c
    out["w1c"] = f(w1z.reshape(2, 128, 2 * 2 * 32 * 128))
    pos = np.stack([np.asarray(inp["cmp_pos_k"], np.float32), np.asarray(inp["cmp_pos_v"], np.float32)], axis=1)
    pos = pos.transpose(0, 3, 1, 2)
    out["post"] = f(np.repeat(np.concatenate([pos, pos], axis=1).reshape(2, 128, 64), 2, axis=2))
    w2c = np.stack([np.asarray(inp["cmp_k_w2"], np.float32), np.asarray(inp["cmp_v_w2"], np.float32)], axis=1)
    out["w2c"] = f(w2c.transpose(0, 2, 1, 3).reshape(2, 128, 128))
    pw = np.asarray(inp["pool_w"], np.float32)
    out["poolw"] = f(pw.transpose(0, 2, 1, 3).reshape(2, 128, 512))
    cw1 = np.asarray(inp["conv_w_pw1"], np.float32).reshape(2, 8, 128, 2048)
    out["cw1"] = f(cw1.transpose(0, 2, 1, 3).reshape(2, 128, 8 * 2048))
    cw2 = np.asarray(inp["conv_w_pw2"], np.float32).reshape(2, 8, 128, 1024)
    out["cw2"] = f(cw2.transpose(0, 2, 1, 3).reshape(2, 128, 8 * 1024))
    return out


_CACHE = {}


def run(inputs, layers=(0, 1, 2, 3), final=True, n_cores=8):
    key = (tuple(layers), final)
    if key not in _CACHE:
        pr = Prog(layers, final)
        _CACHE[key] = (pr.build(), set(pr.dins.keys()))
    nc, names = _CACHE[key]
    w = _prep_weights(inputs)
    x = np.asarray(inputs["x"], np.float32)
    in_maps = []
    for b in range(n_cores):
        m = {k: v for k, v in w.items() if k in names}
        m["xT"] = np.ascontiguousarray(x[b].T)
        in_maps.append(m)
    res = run_bass_kernel_spmd(nc, in_maps, core_ids=list(range(n_cores)))
    out = np.stack([np.ascontiguousarray(res.results[b]["yT"].T) for b in range(n_cores)], axis=0)
    return out.astype(np.float32)


def kernel(**inputs):
    return run(inputs)
```
